# Optimizing a Trainium2 kernel written in Bass

```python
import math
import jax, jax.numpy as jnp
from jax import lax
import numpy as np

D_MODEL = 4096
BATCH = 4
SEQ = 2048
DEPTH = 2
DEC_BATCH = 128
DEC_SEQ = 8
PAST_LEN = 16384
PAGE_SIZE = 128

N_META = 16
CHUNK = 64
CONV_W = 4
NORM_EPS = 1e-6
M_HEADS = 4
M_DV = D_MODEL // 16
M_DK = M_DV // 2
M_WIDTH = M_HEADS * M_DV
R_WIDTH = D_MODEL // 4
R_BLOCKS = 8
R_BDIM = R_WIDTH // R_BLOCKS
R_C = 8.0
G_DK = 128
G_DV = 128
G_WIDTH = D_MODEL // 2
G_HEADS = G_WIDTH // G_DV
G_QKV = 2 * G_HEADS * G_DK + G_WIDTH
D_MIX = M_WIDTH + R_WIDTH + G_WIDTH
D_FF = -(-8 * D_MODEL // (3 * 256)) * 256
SIZES = (M_HEADS * M_DK, M_HEADS * M_DK, M_WIDTH, M_WIDTH, M_HEADS, M_HEADS,
         R_WIDTH, R_WIDTH,
         G_HEADS * G_DK, G_HEADS * G_DK, G_WIDTH, G_WIDTH, G_HEADS, G_HEADS)
D_IN = sum(SIZES)
SPLITS = tuple(sum(SIZES[:i + 1]) for i in range(len(SIZES) - 1))

kernel_name = 'hymba_mlstm_rglru_gdn_step'


def f32(a):
    return a.astype(jnp.float32)


def rmsnorm(x, w):
    xf = f32(x)
    y = xf * lax.rsqrt(jnp.mean(xf * xf, axis=-1, keepdims=True) + NORM_EPS)
    return (y * f32(w)).astype(x.dtype)


def l2norm(x):
    return x * lax.rsqrt(jnp.sum(x * x, axis=-1, keepdims=True) + NORM_EPS)


def causal_conv(x, buf, w):
    xp = jnp.concatenate([buf.astype(x.dtype), x], axis=1)
    y = lax.conv_general_dilated(xp, w.astype(x.dtype)[:, None, :], (1,), 'VALID',
                                 dimension_numbers=('NWC', 'WIO', 'NWC'),
                                 feature_group_count=x.shape[-1])
    return y, xp[:, -(CONV_W - 1):]


def to_heads(a, h, d):
    b, t, _ = a.shape
    return a.reshape(b, t, h, d).transpose(0, 2, 1, 3)


def chunked_scan(step, carry, seqs, chunk):
    t = seqs[0].shape[2]
    n = t // chunk
    xs = tuple(jnp.moveaxis(a.reshape(a.shape[:2] + (n, chunk) + a.shape[3:]), 2, 0) for a in seqs)
    carry, ys = lax.scan(step, carry, xs)
    ys = jnp.moveaxis(ys, 0, 2)
    return carry, ys.reshape(ys.shape[:2] + (t,) + ys.shape[4:])


def run_chunked(step, carry, seqs, n_lead):
    if n_lead:
        carry, y0 = chunked_scan(step, carry, tuple(a[:, :, :n_lead] for a in seqs), n_lead)
        rest = tuple(a[:, :, n_lead:] for a in seqs)
        carry, y1 = chunked_scan(step, carry, rest, math.gcd(rest[0].shape[2], CHUNK))
        return carry, jnp.concatenate([y0, y1], axis=2)
    return chunked_scan(step, carry, seqs, math.gcd(seqs[0].shape[2], CHUNK))


def mlstm_chunk(carry, inp):
    c_st, n_st, m_st = carry
    q, k, v, li, lf = inp
    L = q.shape[2]
    tri = jnp.tril(jnp.ones((L, L), dtype=bool))
    F = jnp.cumsum(lf, axis=-1)
    D = jnp.where(tri, F[..., :, None] - F[..., None, :] + li[..., None, :], -jnp.inf)
    inter = m_st[..., None] + F
    m_t = jnp.maximum(jnp.max(D, axis=-1), inter)
    w_inter = jnp.exp(inter - m_t)
    s = jnp.einsum('bhtd,bhsd->bhts', q, k) * jnp.exp(D - m_t[..., None])
    num = w_inter[..., None] * jnp.einsum('bhtd,bhde->bhte', q, c_st) + jnp.einsum('bhts,bhse->bhte', s, v)
    den = w_inter * jnp.einsum('bhtd,bhd->bht', q, n_st) + jnp.sum(s, axis=-1)
    h = num / jnp.maximum(jnp.abs(den), jnp.exp(-m_t))[..., None]
    m_new = m_t[..., -1]
    w_end = jnp.exp(F[..., -1:] - F + li - m_new[..., None])
    scale_prev = jnp.exp(m_st + F[..., -1] - m_new)
    c_new = scale_prev[..., None, None] * c_st + jnp.einsum('bhs,bhsd,bhse->bhde', w_end, k, v)
    n_new = scale_prev[..., None] * n_st + jnp.einsum('bhs,bhsd->bhd', w_end, k)
    return (c_new, n_new, m_new), h


def gdn_chunk(S, inp):
    q, k, v, beta, g = inp
    L = q.shape[2]
    tri = jnp.tril(jnp.ones((L, L), dtype=bool))
    strict = jnp.tril(jnp.ones((L, L), dtype=bool), -1)
    G = jnp.cumsum(g, axis=-1)
    diff = G[..., :, None] - G[..., None, :]
    decay = jnp.where(tri, jnp.exp(jnp.where(tri, diff, 0.0)), 0.0)
    kb = k * beta[..., None]
    A = jnp.where(strict, jnp.einsum('bhtd,bhsd->bhts', kb, k) * decay, 0.0)
    eye = jnp.eye(L, dtype=A.dtype)
    rhs = jnp.concatenate([v * beta[..., None], kb * jnp.exp(G)[..., None]], axis=-1)
    sol = lax.linalg.triangular_solve(A + eye, rhs, left_side=True, lower=True, unit_diagonal=True)
    u, w = sol[..., :G_DV], sol[..., G_DV:]
    v_new = u - jnp.einsum('bhtd,bhde->bhte', w, S)
    attn = jnp.einsum('bhtd,bhsd->bhts', q, k) * decay
    o = jnp.einsum('bhtd,bhde->bhte', q * jnp.exp(G)[..., None], S) + jnp.einsum('bhts,bhse->bhte', attn, v_new)
    g_end = G[..., -1]
    S_new = S * jnp.exp(g_end)[..., None, None] + jnp.einsum(
        'bhsd,bhse->bhde', k * jnp.exp(g_end[..., None] - G)[..., None], v_new)
    return S_new, o


def mlstm_mixer(mq, mk, mv, mo, mi, mf, b_i, b_f, norm_w, c0, n0, m0, n_lead):
    b, t, _ = mq.shape
    q = to_heads(f32(mq), M_HEADS, M_DK) * (M_DK ** -0.5)
    k = to_heads(f32(mk), M_HEADS, M_DK)
    v = to_heads(f32(mv), M_HEADS, M_DV)
    li = jnp.swapaxes(f32(mi) + f32(b_i), 1, 2)
    lf = jnp.swapaxes(jax.nn.log_sigmoid(f32(mf) + f32(b_f)), 1, 2)
    (c, n, m), h = run_chunked(mlstm_chunk, (f32(c0), f32(n0), f32(m0)), (q, k, v, li, lf), n_lead)
    h = rmsnorm(h.transpose(0, 2, 1, 3), norm_w.reshape(M_HEADS, M_DV))
    h = h * jax.nn.sigmoid(f32(mo)).reshape(b, t, M_HEADS, M_DV)
    return h.reshape(b, t, M_WIDTH), c, n, m


def rglru_mixer(rx, rg, conv_w, conv_b, wa, ba, wx, bx, lam, h0, buf):
    b, t, _ = rx.shape
    xc, buf_new = causal_conv(rx, buf, conv_w)
    xc = f32(xc) + f32(conv_b)
    xb = xc.reshape(b, t, R_BLOCKS, R_BDIM)
    r = jax.nn.sigmoid(jnp.einsum('btnd,nde->btne', xb, f32(wa)).reshape(b, t, R_WIDTH) + f32(ba))
    i = jax.nn.sigmoid(jnp.einsum('btnd,nde->btne', xb, f32(wx)).reshape(b, t, R_WIDTH) + f32(bx))
    log_a = -R_C * r * jax.nn.softplus(-f32(lam))
    a = jnp.exp(log_a)
    u = jnp.sqrt(-jnp.expm1(2.0 * log_a)) * (i * xc)

    def step(h, au):
        h = au[0] * h + au[1]
        return h, h

    h_last, hs = lax.scan(step, f32(h0), (jnp.swapaxes(a, 0, 1), jnp.swapaxes(u, 0, 1)))
    y = jnp.swapaxes(hs, 0, 1) * jax.nn.gelu(f32(rg))
    return y, h_last, buf_new


def gdn_mixer(gq, gk, gv, gz, gb, ga, conv_w, a_log, dt_bias, norm_w, s0, buf, n_lead):
    b, t, _ = gq.shape
    qkv, buf_new = causal_conv(jnp.concatenate([gq, gk, gv], axis=-1), buf, conv_w)
    qkv = jax.nn.silu(f32(qkv))
    q, k, v = jnp.split(qkv, (G_HEADS * G_DK, 2 * G_HEADS * G_DK), axis=-1)
    q = l2norm(to_heads(q, G_HEADS, G_DK)) * (G_DK ** -0.5)
    k = l2norm(to_heads(k, G_HEADS, G_DK))
    v = to_heads(v, G_HEADS, G_DV)
    beta = jnp.swapaxes(jax.nn.sigmoid(f32(gb)), 1, 2)
    g = jnp.swapaxes(-jnp.exp(f32(a_log)) * jax.nn.softplus(f32(ga) + f32(dt_bias)), 1, 2)
    S, o = run_chunked(gdn_chunk, f32(s0), (q, k, v, beta, g), n_lead)
    o = rmsnorm(o.transpose(0, 2, 1, 3), norm_w) * jax.nn.silu(f32(gz)).reshape(b, t, G_HEADS, G_DV)
    return o.reshape(b, t, G_WIDTH), S, buf_new


def trunk(x, st_c, st_n, st_m, st_h, st_rconv, st_s, st_gconv, n_lead, p):
    (norm_mix, w_in, m_bias_i, m_bias_f, m_norm, r_conv_w, r_conv_b, r_gate_a_w, r_gate_a_b,
     r_gate_x_w, r_gate_x_b, r_lambda, g_conv_w, g_A_log, g_dt_bias, g_norm, w_out,
     norm_ffn, w_gate, w_up, w_down, norm_final) = p
    new = [[] for _ in range(7)]
    for l in range(DEPTH):
        hn = rmsnorm(x, norm_mix[l])
        proj = hn @ w_in[l]
        (mq, mk, mv, mo, mi, mf, rx, rg, gq, gk, gv, gz, gb, ga) = jnp.split(proj, SPLITS, axis=-1)
        ym, c, n, m = mlstm_mixer(mq, mk, mv, mo, mi, mf, m_bias_i[l], m_bias_f[l], m_norm[l],
                                  st_c[l], st_n[l], st_m[l], n_lead)
        yr, h, rbuf = rglru_mixer(rx, rg, r_conv_w[l], r_conv_b[l], r_gate_a_w[l], r_gate_a_b[l],
                                  r_gate_x_w[l], r_gate_x_b[l], r_lambda[l], st_h[l], st_rconv[l])
        yg, s, gbuf = gdn_mixer(gq, gk, gv, gz, gb, ga, g_conv_w[l], g_A_log[l], g_dt_bias[l],
                                g_norm[l], st_s[l], st_gconv[l], n_lead)
        mix = jnp.concatenate([ym, yr, yg], axis=-1).astype(x.dtype)
        x = x + mix @ w_out[l]
        hf = rmsnorm(x, norm_ffn[l])
        x = x + (jax.nn.silu(hf @ w_gate[l]) * (hf @ w_up[l])) @ w_down[l]
        for lst, val in zip(new, (c, n, m, h, rbuf, s, gbuf)):
            lst.append(val)
    y = rmsnorm(x, norm_final)
    return y, [jnp.stack(v) for v in new]


def setup_inputs(seed: int = 0) -> dict:
    key = jax.random.key(seed)
    ks = iter(jax.random.split(key, 48))

    def nrm(shape, s):
        return jax.random.normal(next(ks), shape, jnp.float32) * s

    def unif(shape, lo, hi):
        return jax.random.uniform(next(ks), shape, jnp.float32, lo, hi)

    a_rg = unif((DEPTH, R_WIDTH), 0.9, 0.999) ** (1.0 / R_C)
    dt = jnp.exp(unif((DEPTH, G_HEADS), math.log(1e-3), math.log(0.1)))
    return {
        'x_prompt': nrm((BATCH, SEQ, D_MODEL), 1.0),
        'x_sample': nrm((DEC_BATCH, DEC_SEQ, D_MODEL), 1.0),
        'state_mlstm_C': nrm((DEPTH, DEC_BATCH, M_HEADS, M_DK, M_DV), 0.1),
        'state_mlstm_n': nrm((DEPTH, DEC_BATCH, M_HEADS, M_DK), 0.1),
        'state_mlstm_m': nrm((DEPTH, DEC_BATCH, M_HEADS), 1.0),
        'state_rglru_h': nrm((DEPTH, DEC_BATCH, R_WIDTH), 0.5),
        'state_rglru_conv': nrm((DEPTH, DEC_BATCH, CONV_W - 1, R_WIDTH), 1.0),
        'state_gdn_S': nrm((DEPTH, DEC_BATCH, G_HEADS, G_DK, G_DV), 0.1),
        'state_gdn_conv': nrm((DEPTH, DEC_BATCH, CONV_W - 1, G_QKV), 1.0),
        'meta_tokens': nrm((N_META, D_MODEL), 1.0),
        'norm_mix': 1.0 + nrm((DEPTH, D_MODEL), 0.02),
        'w_in': nrm((DEPTH, D_MODEL, D_IN), D_MODEL ** -0.5),
        'm_bias_i': nrm((DEPTH, M_HEADS), 0.1),
        'm_bias_f': unif((DEPTH, M_HEADS), 3.0, 6.0),
        'm_norm': 1.0 + nrm((DEPTH, M_WIDTH), 0.02),
        'r_conv_w': nrm((DEPTH, CONV_W, R_WIDTH), CONV_W ** -0.5),
        'r_conv_b': nrm((DEPTH, R_WIDTH), 0.02),
        'r_gate_a_w': nrm((DEPTH, R_BLOCKS, R_BDIM, R_BDIM), R_BDIM ** -0.5),
        'r_gate_a_b': nrm((DEPTH, R_WIDTH), 0.02),
        'r_gate_x_w': nrm((DEPTH, R_BLOCKS, R_BDIM, R_BDIM), R_BDIM ** -0.5),
        'r_gate_x_b': nrm((DEPTH, R_WIDTH), 0.02),
        'r_lambda': jnp.log(a_rg) - jnp.log1p(-a_rg),
        'g_conv_w': nrm((DEPTH, CONV_W, G_QKV), CONV_W ** -0.5),
        'g_A_log': jnp.log(unif((DEPTH, G_HEADS), 1.0, 16.0)),
        'g_dt_bias': dt + jnp.log(-jnp.expm1(-dt)),
        'g_norm': 1.0 + nrm((DEPTH, G_DV), 0.02),
        'w_out': nrm((DEPTH, D_MIX, D_MODEL), D_MIX ** -0.5),
        'norm_ffn': 1.0 + nrm((DEPTH, D_MODEL), 0.02),
        'w_gate': nrm((DEPTH, D_MODEL, D_FF), D_MODEL ** -0.5),
        'w_up': nrm((DEPTH, D_MODEL, D_FF), D_MODEL ** -0.5),
        'w_down': nrm((DEPTH, D_FF, D_MODEL), D_FF ** -0.5),
        'norm_final': 1.0 + nrm((D_MODEL,), 0.02),
    }


def reference(x_prompt, x_sample, state_mlstm_C, state_mlstm_n, state_mlstm_m, state_rglru_h,
              state_rglru_conv, state_gdn_S, state_gdn_conv, meta_tokens, norm_mix, w_in,
              m_bias_i, m_bias_f, m_norm, r_conv_w, r_conv_b, r_gate_a_w, r_gate_a_b,
              r_gate_x_w, r_gate_x_b, r_lambda, g_conv_w, g_A_log, g_dt_bias, g_norm, w_out,
              norm_ffn, w_gate, w_up, w_down, norm_final):
    p = (norm_mix, w_in, m_bias_i, m_bias_f, m_norm, r_conv_w, r_conv_b, r_gate_a_w, r_gate_a_b,
         r_gate_x_w, r_gate_x_b, r_lambda, g_conv_w, g_A_log, g_dt_bias, g_norm, w_out,
         norm_ffn, w_gate, w_up, w_down, norm_final)
    b = x_prompt.shape[0]
    meta = jnp.broadcast_to(meta_tokens.astype(x_prompt.dtype)[None], (b, N_META, D_MODEL))
    xp = jnp.concatenate([meta, x_prompt], axis=1)
    f = jnp.float32
    yp, sp = trunk(xp,
                   jnp.zeros((DEPTH, b, M_HEADS, M_DK, M_DV), f),
                   jnp.zeros((DEPTH, b, M_HEADS, M_DK), f),
                   jnp.zeros((DEPTH, b, M_HEADS), f),
                   jnp.zeros((DEPTH, b, R_WIDTH), f),
                   jnp.zeros((DEPTH, b, CONV_W - 1, R_WIDTH), x_prompt.dtype),
                   jnp.zeros((DEPTH, b, G_HEADS, G_DK, G_DV), f),
                   jnp.zeros((DEPTH, b, CONV_W - 1, G_QKV), x_prompt.dtype),
                   N_META, p)
    y_prompt = yp[:, N_META:]
    y_sample, ss = trunk(x_sample, state_mlstm_C, state_mlstm_n, state_mlstm_m, state_rglru_h,
                         state_rglru_conv, state_gdn_S, state_gdn_conv, 0, p)
    return (y_prompt, y_sample, sp[0], sp[1], sp[2], sp[3], sp[4], sp[5], sp[6],
            ss[0], ss[1], ss[2], ss[3], ss[4], ss[5], ss[6])
```

```python
import numpy as np
from contextlib import ExitStack
import concourse.bass as bass
import concourse.mybir as mybir
from concourse.bass_utils import run_bass_kernel_spmd

F32 = mybir.dt.float32
BF16 = mybir.dt.bfloat16
AF = mybir.ActivationFunctionType
ALU = mybir.AluOpType
AX = mybir.AxisListType

D = 4096
DIN = 13352
KC = 32
EPS = 1e-6
NEG = -30000.0
FULL = dict(SEQ=2048, NS=16, DEPTH=2, DFF=11008)

C_ID, C_U, C_SL, C_NEGM, C_MUI, C_MUSN, C_MLSN, C_ONE, C_SEL8, C_SEL16, C_SEL64, C_END = [128 * i for i in range(12)]


def make_consts():
    c = np.zeros((128, C_END), np.float32)
    p = np.arange(128)[:, None]
    f = np.arange(128)[None, :]
    c[:, C_ID:C_ID + 128] = (p == f)
    c[:, C_U:C_U + 128] = (p <= f)
    c[:, C_SL:C_SL + 128] = (p > f)
    c[:, C_NEGM:C_NEGM + 128] = np.where(f <= p, 0.0, NEG)
    c[:, C_MUI:C_MUI + 128] = (p <= f)
    c[:, C_MUSN:C_MUSN + 128] = -1.0 * (p < f)
    c[:, C_MLSN:C_MLSN + 128] = -1.0 * (f < p)
    c[:, C_ONE:C_ONE + 128] = 1.0
    for off, L in ((C_SEL8, 8), (C_SEL16, 16), (C_SEL64, 64)):
        c[L - 1, off:off + 128] = 1.0
    return c


class Buf:
    __slots__ = ("w", "r", "t")

    def __init__(self, t=None):
        self.w = {}
        self.r = {}
        self.t = t


class KB:
    def __init__(self, nc, es, nd=24):
        self.nc = nc
        self.E = {"pe": nc.tensor, "act": nc.scalar, "dve": nc.vector, "pool": nc.gpsimd, "sp": nc.sync}
        self.H = {}
        self.cnt = {}
        for k in ("pe", "act", "dve", "pool"):
            self.H[k] = es.enter_context(nc.semaphore("s_" + k))
            self.cnt[k] = 0
        self.nd = nd
        self.dq = {}
        for q in ("sp", "pool", "act"):
            keys = []
            for i in range(nd if q == "sp" else 8):
                key = "d%s%d" % (q, i)
                self.H[key] = es.enter_context(nc.semaphore(key))
                keys.append(key)
            self.dq[q] = [keys, 0]
        self.dval = {}
        self.seen = {k: {} for k in self.E}

    def _need(self, r, w, skip_dma_waw=False):
        need = {}
        for b in r:
            for k, v in b.w.items():
                if need.get(k, 0) < v:
                    need[k] = v
        for b in w:
            for k, v in b.w.items():
                if skip_dma_waw and k[0] == "d" and k != "dve":
                    continue
                if need.get(k, 0) < v:
                    need[k] = v
            for k, v in b.r.items():
                if need.get(k, 0) < v:
                    need[k] = v
        return need

    def _wait(self, eng, need):
        seen = self.seen[eng]
        for k, v in need.items():
            if eng == "pe" and k == "pe":
                continue
            if seen.get(k, 0) < v:
                self.E[eng].wait_ge(self.H[k], v)
                seen[k] = v

    def op(self, eng, meth, *a, r=(), w=(), **kw):
        self._wait(eng, self._need(r, w))
        ins = getattr(self.E[eng], meth)(*a, **kw)
        self.cnt[eng] += 1
        c = self.cnt[eng]
        ins.then_inc(self.H[eng], 1)
        for b in r:
            b.r[eng] = c
        for b in w:
            b.w = {eng: c}
            b.r = {}
        return ins

    def dma(self, out, in_, r=(), w=(), q="sp", append=False, **kw):
        self._wait(q, self._need(r, w, skip_dma_waw=append))
        keys, n = self.dq[q]
        key = keys[n % len(keys)]
        self.dq[q][1] = n + 1
        prev = self.dval.get(key, 0)
        if prev and self.seen[q].get(key, 0) < prev:
            self.E[q].wait_ge(self.H[key], prev)
            self.seen[q][key] = prev
        val = prev + 16
        self.dval[key] = val
        ins = self.E[q].dma_start(out=out, in_=in_, **kw)
        ins.then_inc(self.H[key], 16)
        for b in r:
            b.r[key] = val
        for b in w:
            if append:
                b.w[key] = val
            else:
                b.w = {key: val}
            b.r = {}
        return ins

    def barrier(self):
        need = {k: v for k, v in self.dval.items()}
        for k in ("pe", "act", "dve", "pool"):
            if self.cnt[k]:
                need[k] = self.cnt[k]
        for eng in ("sp", "pe", "act", "dve", "pool"):
            self._wait(eng, {k: v for k, v in need.items() if k != eng})

    def finish(self):
        need = {k: v for k, v in self.dval.items()}
        for k in ("pe", "act", "dve", "pool"):
            if self.cnt[k]:
                need[k] = self.cnt[k]
        self._wait("sp", need)


def build(cfg):
    SEQ, NS, DEPTH, DFF = cfg["SEQ"], cfg["NS"], cfg["DEPTH"], cfg["DFF"]
    KF = DFF // 128
    TP = 16 + SEQ
    T = TP + NS * 8
    NE = 3 + TP + NS * 11
    SB0 = 3 + TP
    nc = bass.Bass("TRN2", target_bir_lowering=False)

    def din(name, shape):
        return nc.dram_tensor(name, list(shape), F32, kind="ExternalInput").ap()

    def dout(name, shape):
        return nc.dram_tensor(name, list(shape), F32, kind="ExternalOutput").ap()

    def dscr(name, shape, dt=F32):
        return nc.dram_tensor(name, list(shape), dt).ap()

    xin = din("xin", [T, D])
    w_in = din("w_in", [DEPTH * D, DIN])
    w_out = din("w_out", [DEPTH * D, D])
    w_gate = din("w_gate", [DEPTH * D, DFF])
    w_up = din("w_up", [DEPTH * D, DFF])
    w_down = din("w_down", [DEPTH * DFF, D])
    NPP = (2 * DEPTH + 1) * 32 + DEPTH * (8 * 4 + 8 * 4 + 48 * 4)
    pp_d = din("pp", [128, NPP])
    NPB = DEPTH * (8 + 1024 + 16 + 16 + 128)
    pb_d = din("pb", [1, NPB])
    rgw_d = din("rgw", [DEPTH * 2 * 8 * 128, 128])
    consts_d = din("consts", [128, C_END])
    sC_d = din("sC", [DEPTH * NS * 4 * 128, 256])
    sn_d = din("sn", [DEPTH * NS * 128, 4])
    sm_d = din("sm", [1, DEPTH * NS * 4])
    sh_d = din("sh", [DEPTH * 128, 8 * NS])
    src_d = din("src", [DEPTH * 128, 8 * NS * 3])
    sS_d = din("sS", [DEPTH * NS * 16 * 128, 128])
    sgc_d = din("sgc", [DEPTH * 128, 48 * NS * 3])

    y_d = dout("y", [T, D])
    o_pC = dout("o_pC", [DEPTH * 4 * 128, 256])
    o_pn = dout("o_pn", [DEPTH * 128, 4])
    o_pm = dout("o_pm", [DEPTH, 4])
    o_ph = dout("o_ph", [DEPTH * 128, 8])
    o_prc = dout("o_prc", [DEPTH * 128, 8 * 3])
    o_pS = dout("o_pS", [DEPTH * 16 * 128, 128])
    o_pgc = dout("o_pgc", [DEPTH * 128, 48 * 3])
    o_sC = dout("o_sC", [DEPTH * NS * 4 * 128, 256])
    o_sn = dout("o_sn", [DEPTH * NS * 128, 4])
    o_sm = dout("o_sm", [DEPTH * NS, 4])
    o_sh = dout("o_sh", [DEPTH * 128, 8 * NS])
    o_src = dout("o_src", [DEPTH * 128, 8 * NS * 3])
    o_sS = dout("o_sS", [DEPTH * NS * 16 * 128, 128])
    o_sgc = dout("o_sgc", [DEPTH * 128, 48 * NS * 3])

    xT = dscr("xT", [D, T])
    mqT = dscr("mqT", [512, T], BF16)
    mkT = dscr("mkT", [512, T], BF16)
    mk_tok = dscr("mk_tok", [T, 512], BF16)
    mv_tok = dscr("mv_tok", [T, 1024], BF16)
    mo_tok = dscr("mo_tok", [T, 1024], BF16)
    mg_tok = dscr("mg_tok", [T, 8])
    rxT = dscr("rxT", [1024, T])
    rgT = dscr("rgT", [1024, T])
    gqkvT = dscr("gqkvT", [6144, T])
    gz_tok = dscr("gz_tok", [T, 2048], BF16)
    gg_tok = dscr("gg_tok", [T, 48])
    gqnT = dscr("gqnT", [2048, T], BF16)
    gknT = dscr("gknT", [2048, T], BF16)
    gk_tok = dscr("gk_tok", [T, 2048], BF16)
    gv_tok = dscr("gv_tok", [T, 2048], BF16)
    mixT = dscr("mixT", [D, T], BF16)
    aT = dscr("aT", [DFF, T], BF16)

    PP_NMIX = 0
    PP_NFFN = DEPTH * 32
    PP_NFIN = 2 * DEPTH * 32
    PP_R = (2 * DEPTH + 1) * 32
    PPL = 8 * 4 + 8 * 4 + 48 * 4
    PBL = 8 + 1024 + 16 + 16 + 128

    with ExitStack() as es:
        es.enter_context(nc.allow_non_contiguous_dma(reason="small strided state / layout DMAs"))
        kb = KB(nc, es)
        op, dma = kb.op, kb.dma

        sbn = [0]

        def sb(name, shape, dt=F32, st=es):
            sbn[0] += 1
            t = st.enter_context(nc.sbuf_tensor("t%d_%s" % (sbn[0], name), list(shape), dt))
            return Buf(t)

        PS = [Buf(es.enter_context(nc.psum_tensor("ps%d" % i, [128, 512], F32))) for i in range(6)]
        PSB = [Buf(es.enter_context(nc.psum_tensor("psb%d" % i, [128, 1024], BF16))) for i in range(2)]
        psn = [0]
        psbn = [0]

        def ps_next():
            psn[0] += 1
            return PS[psn[0] % 6]

        def psb_next():
            psbn[0] += 1
            return PSB[psbn[0] % 2]

        cst = sb("cst", [128, C_END])
        cstb = sb("cstb", [128, 384], BF16)
        pp = sb("pp", [128, NPP])
        dma(cst.t[:], consts_d[:, :], w=[cst])
        dma(pp.t[:], pp_d[:, :], w=[pp])
        op("dve", "tensor_copy", cstb.t[:, 0:128], cst.t[:, C_ID:C_ID + 128], r=[cst], w=[cstb])
        op("dve", "tensor_copy", cstb.t[:, 128:256], cst.t[:, C_U:C_U + 128], r=[cst], w=[cstb])
        op("dve", "tensor_copy", cstb.t[:, 256:384], cst.t[:, C_ONE:C_ONE + 128], r=[cst], w=[cstb])
        ident_f = cst.t[:, C_ID:C_ID + 128]
        ident_b = cstb.t[:, 0:128]
        ones_b = cstb.t[:, 256:384]
        ones_f = cst.t[:, C_ONE:C_ONE + 128]
        Umat = cst.t[:, C_U:C_U + 128]

        XT_B = [Buf() for _ in range(KC)]
        scr = {n: Buf() for n in ("mqT", "mkT", "mk_tok", "mv_tok", "mo_tok", "mg_tok", "rxT", "rgT", "gqkvT",
                                  "gz_tok", "gg_tok", "gqnT", "gknT", "gk_tok", "gv_tok", "mixT", "aT", "y", "out")}

        subs = [(t0, min(128, T - t0)) for t0 in range(0, T, 128)]

        with ExitStack() as ph:
            ph.callback(kb.barrier)
            xs = [sb("x0s%d" % i, [128, D], st=ph) for i in range(2)]
            xo = [sb("x0o%d" % i, [128, KC, 128], st=ph) for i in range(2)]
            for si, (t0, n) in enumerate(subs):
                a, o = xs[si % 2], xo[si % 2]
                dma(a.t[:n, :], xin[t0:t0 + n, :], w=[a])
                for g in range(8):
                    pt = ps_next()
                    for j in range(4):
                        kc = g * 4 + j
                        op("pe", "transpose", pt.t[:, j * 128:j * 128 + n], a.t[:n, kc * 128:(kc + 1) * 128],
                           ident_f[:n, :n], r=[a, cst], w=[pt])
                    eng = "dve" if g % 2 == 0 else "act"
                    src = pt.t[:, :].rearrange("p (j t) -> p j t", j=4)[:, :, :n]
                    if eng == "dve":
                        op("dve", "tensor_copy", o.t[:, g * 4:g * 4 + 4, :n], src, r=[pt], w=[o])
                    else:
                        op("act", "activation", o.t[:, g * 4:g * 4 + 4, :n], src, AF.Copy, r=[pt], w=[o])
                dma(xT[:, t0:t0 + n].rearrange("(k p) t -> p k t", p=128), o.t[:, :, :n], r=[o], w=XT_B, append=True)

        def norm_piece(ph_bufs, wcol0, t0, n, dst_fn):
            xt, sq, rs = ph_bufs["xt"][ph_bufs["i"] % 2], ph_bufs["sq"], ph_bufs["rs"]
            ph_bufs["i"] += 1
            dma(xt.t[:, :, :n], xT[:, t0:t0 + n].rearrange("(k p) t -> p k t", p=128), r=XT_B, w=[xt])
            op("act", "activation", sq.t[:, :, :n], xt.t[:, :, :n], AF.Square, r=[xt], w=[sq])
            pt = ps_next()
            for kc in range(KC):
                op("pe", "matmul", pt.t[:, :n], ones_b, sq.t[:, kc, :n], start=(kc == 0), stop=(kc == KC - 1),
                   r=[sq, cstb], w=[pt])
            op("dve", "tensor_scalar", rs.t[:, :n], pt.t[:, :n], 1.0 / D, EPS, ALU.mult, ALU.add, r=[pt], w=[rs])
            op("act", "activation", rs.t[:, :n], rs.t[:, :n], AF.Sqrt, r=[rs], w=[rs])
            op("dve", "reciprocal", rs.t[:, :n], rs.t[:, :n], r=[rs], w=[rs])
            for kc in range(KC):
                dst_fn(kc, xt, pp.t[:, wcol0 + kc:wcol0 + kc + 1], rs, n)

        def make_norm_bufs(ph):
            return dict(xt=[sb("nxt%d" % i, [128, KC, 256], st=ph) for i in range(2)],
                        sq=sb("nsq", [128, KC, 256], BF16, st=ph), rs=sb("nrs", [128, 256], st=ph), i=0)

        def norm_to_act(nb, wcol0, t0, n, actT):
            for p0 in range(0, n, 256):
                pn = min(256, n - p0)

                def dst(kc, xt, wn, rs, m, p0=p0):
                    op("dve", "scalar_tensor_tensor", actT.t[:, kc, p0:p0 + m], xt.t[:, kc, :m], wn, rs.t[:, :m],
                       ALU.mult, ALU.mult, r=[xt, rs, pp], w=[actT])
                norm_piece(nb, wcol0, t0 + p0, pn, dst)

        def linear(ph, actT, KCn, ts_list, Wd, segs, PK=8):
            stg = ph["stg"]
            wbs = ph["wb"]
            npieces = (KCn + PK - 1) // PK
            for si, (c0, ncol, handler, tag) in enumerate(segs):
                ncb = (ncol + 127) // 128
                accs = {}
                for cb in range(ncb):
                    for ti in range(len(ts_list)):
                        accs[(cb, ti)] = ps_next()
                for pi in range(npieces):
                    k0 = pi * PK
                    kn = min(PK, KCn - k0)
                    s_ = stg[ph["n"] % len(stg)]
                    wb = wbs[ph["n"] % len(wbs)]
                    ph["n"] += 1
                    dma(s_.t[:, :kn, :ncol],
                        Wd[k0 * 128:(k0 + kn) * 128, c0:c0 + ncol].rearrange("(k p) c -> p k c", p=128), w=[s_])
                    ce = ("dve", "act", "dve", "act", "pool", "dve", "act")[ph["n"] % 7]
                    if ce == "act":
                        op("act", "activation", wb.t[:, :kn, :ncol], s_.t[:, :kn, :ncol], AF.Copy, r=[s_], w=[wb])
                    else:
                        op(ce, "tensor_copy", wb.t[:, :kn, :ncol], s_.t[:, :kn, :ncol], r=[s_], w=[wb])
                    for cb in range(ncb):
                        m = min(128, ncol - cb * 128)
                        for ti, (off, n) in enumerate(ts_list):
                            acc = accs[(cb, ti)]
                            for j in range(kn):
                                kc = k0 + j
                                op("pe", "matmul", acc.t[:m, :n], wb.t[:, j, cb * 128:cb * 128 + m],
                                   actT.t[:, kc, off:off + n], start=(kc == 0), stop=(kc == KCn - 1),
                                   r=[wb, actT], w=[acc])
                for cb in range(ncb):
                    m = min(128, ncol - cb * 128)
                    for ti, (off, n) in enumerate(ts_list):
                        acc = accs[(cb, ti)]
                        handler(tag, c0 + cb * 128, m, ti, off, n, acc, acc.t[:m, :n])

        def make_lin_bufs(ph):
            return dict(stg=[sb("lstg%d" % i, [128, 8, 256], st=ph) for i in range(3)],
                        wb=[sb("lwb%d" % i, [128, 8, 256], BF16, st=ph) for i in range(3)], n=0)

        def tok_tiles(maxn):
            res = []
            t0 = 0
            while t0 < T:
                n = min(maxn, T - t0)
                ts = [(o, min(512, n - o)) for o in range(0, n, 512)]
                res.append((t0, n, ts))
                t0 += n
            return res

        for l in range(DEPTH):
            ppl = PP_R + l * PPL
            pbl = l * PBL
            with ExitStack() as ph:
                ph.callback(kb.barrier)
                actT = sb("actT", [128, KC, 1024], BF16, st=ph)
                nb = make_norm_bufs(ph)
                lb = make_lin_bufs(ph)
                stf = [sb("stf%d" % i, [128, 512], st=ph) for i in range(3)]
                stb = [sb("stb%d" % i, [128, 512], BF16, st=ph) for i in range(3)]
                tmb = [sb("tmb%d" % i, [128, 4, 128], BF16, st=ph) for i in range(3)]
                tmf = [sb("tmf%d" % i, [128, 48], st=ph) for i in range(3)]
                gbias = sb("gbias", [128, 8], st=ph)
                gdt = sb("gdt", [128, 32], st=ph)
                dma(gbias.t[:], pb_d[0:1, pbl:pbl + 8].partition_broadcast(128), w=[gbias])
                dma(gdt.t[:], pb_d[0:1, pbl + 8 + 1024:pbl + 8 + 1024 + 32].partition_broadcast(128), w=[gdt])
                op("act", "activation", gdt.t[:, 0:16], gdt.t[:, 0:16], AF.Exp, r=[gdt], w=[gdt])
                op("dve", "tensor_scalar", gdt.t[:, 0:16], gdt.t[:, 0:16], -1.0, None, ALU.mult, r=[gdt], w=[gdt])
                rr = [0]

                def h_in(tag, c, m, ti, off, n, acc, acc_ap, tt0=None):
                    kind, dst, r0 = tag
                    i = rr[0] % 3
                    rr[0] += 1
                    tg0 = cur["t0"] + off
                    row = c - r0
                    if kind in ("f32",):
                        s_ = stf[i]
                        if i % 2 == 0:
                            op("dve", "tensor_copy", s_.t[:m, :n], acc_ap, r=[acc], w=[s_])
                        else:
                            op("act", "activation", s_.t[:m, :n], acc_ap, AF.Copy, r=[acc], w=[s_])
                        dma(dst[0][row:row + m, tg0:tg0 + n], s_.t[:m, :n], r=[s_], w=[scr[dst[1]]], append=True)
                        return
                    if kind == "gate":
                        s_ = stf[i]
                        op("dve", "tensor_copy", s_.t[:m, :n], acc_ap, r=[acc], w=[s_])
                        for s0 in range(0, n, 128):
                            ns = min(128, n - s0)
                            pt = ps_next()
                            op("pe", "transpose", pt.t[:ns, :m], s_.t[:m, s0:s0 + ns], ident_f[:m, :m], r=[s_, cst], w=[pt])
                            tf = tmf[rr[0] % 3]
                            rr[0] += 1
                            if dst[1] == "mg_tok":
                                op("dve", "tensor_tensor", tf.t[:ns, 0:8], pt.t[:ns, 0:8], gbias.t[:ns, :], ALU.add,
                                   r=[pt, gbias], w=[tf])
                                op("act", "activation", tf.t[:ns, 8:12], tf.t[:ns, 4:8], AF.Exp, scale=-1.0, r=[tf], w=[tf])
                                op("act", "activation", tf.t[:ns, 8:12], tf.t[:ns, 8:12], AF.Ln, bias=1.0, r=[tf], w=[tf])
                                op("dve", "tensor_scalar", tf.t[:ns, 4:8], tf.t[:ns, 8:12], -1.0, None, ALU.mult, r=[tf], w=[tf])
                                dma(mg_tok[tg0 + s0:tg0 + s0 + ns, :], tf.t[:ns, 0:8], r=[tf], w=[scr["mg_tok"]], append=True)
                            else:
                                op("act", "activation", tf.t[:ns, 32:48], pt.t[:ns, 0:16], AF.Exp, scale=-1.0, r=[pt], w=[tf])
                                op("dve", "tensor_scalar", tf.t[:ns, 32:48], tf.t[:ns, 32:48], 1.0, None, ALU.add, r=[tf], w=[tf])
                                op("dve", "reciprocal", tf.t[:ns, 0:16], tf.t[:ns, 32:48], r=[tf], w=[tf])
                                op("act", "activation", tf.t[:ns, 16:32], tf.t[:ns, 32:48], AF.Ln, r=[tf], w=[tf])
                                op("dve", "tensor_scalar", tf.t[:ns, 16:32], tf.t[:ns, 16:32], -1.0, None, ALU.mult, r=[tf], w=[tf])
                                op("dve", "tensor_tensor", tf.t[:ns, 32:48], pt.t[:ns, 16:32], gdt.t[:ns, 16:32], ALU.add,
                                   r=[pt, gdt], w=[tf])
                                op("act", "activation", tf.t[:ns, 32:48], tf.t[:ns, 32:48], AF.Exp, r=[tf], w=[tf])
                                op("act", "activation", tf.t[:ns, 32:48], tf.t[:ns, 32:48], AF.Ln, bias=1.0, r=[tf], w=[tf])
                                op("dve", "tensor_tensor", tf.t[:ns, 32:48], tf.t[:ns, 32:48], gdt.t[:ns, 0:16], ALU.mult,
                                   r=[tf, gdt], w=[tf])
                                dma(gg_tok[tg0 + s0:tg0 + s0 + ns, :], tf.t[:ns, 0:48], r=[tf], w=[scr["gg_tok"]], append=True)
                        return
                    s_ = stb[i]
                    if kind == "mq":
                        op("act", "activation", s_.t[:m, :n], acc_ap, AF.Copy, scale=128.0 ** -0.5, r=[acc], w=[s_])
                    elif kind == "sig":
                        op("act", "activation", s_.t[:m, :n], acc_ap, AF.Sigmoid, r=[acc], w=[s_])
                    elif kind == "silu":
                        op("act", "activation", s_.t[:m, :n], acc_ap, AF.Silu, r=[acc], w=[s_])
                    else:
                        op("dve", "tensor_copy", s_.t[:m, :n], acc_ap, r=[acc], w=[s_])
                    fm, tm = dst
                    if fm is not None:
                        dma(fm[0][row:row + m, tg0:tg0 + n], s_.t[:m, :n], r=[s_], w=[scr[fm[1]]], append=True)
                    if tm is not None:
                        pb_ = psb_next()
                        tb = tmb[rr[0] % 3]
                        rr[0] += 1
                        nsub = (n + 127) // 128
                        for s in range(nsub):
                            ns = min(128, n - s * 128)
                            op("pe", "transpose", pb_.t[:ns, s * 128:s * 128 + m], s_.t[:m, s * 128:s * 128 + ns],
                               ident_b[:m, :m], r=[s_, cstb], w=[pb_])
                        nfull = n // 128
                        if nfull:
                            op("dve", "tensor_copy", tb.t[:, :nfull, :m],
                               pb_.t[:, 0:nfull * 128].rearrange("p (s c) -> p s c", s=nfull)[:, :, :m], r=[pb_], w=[tb])
                            dma(tm[0][tg0:tg0 + nfull * 128, row:row + m].rearrange("(s p) c -> p s c", p=128),
                                tb.t[:, :nfull, :m], r=[tb], w=[scr[tm[1]]], append=True)
                        if n % 128:
                            ns = n % 128
                            op("dve", "tensor_copy", tb.t[:ns, nfull, :m], pb_.t[:ns, nfull * 128:nfull * 128 + m],
                               r=[pb_], w=[tb])
                            dma(tm[0][tg0 + nfull * 128:tg0 + n, row:row + m], tb.t[:ns, nfull, :m], r=[tb],
                                w=[scr[tm[1]]], append=True)

                groups = [
                    (0, 512, "mq", ((mqT, "mqT"), None)),
                    (512, 512, "cp", ((mkT, "mkT"), (mk_tok, "mk_tok"))),
                    (1024, 1024, "cp", (None, (mv_tok, "mv_tok"))),
                    (2048, 1024, "sig", (None, (mo_tok, "mo_tok"))),
                    (3072, 8, "gate", (None, "mg_tok")),
                    (3080, 1024, "f32", (rxT, "rxT")),
                    (4104, 1024, "f32", (rgT, "rgT")),
                    (5128, 6144, "f32", (gqkvT, "gqkvT")),
                    (11272, 2048, "silu", (None, (gz_tok, "gz_tok"))),
                    (13320, 32, "gate", (None, "gg_tok")),
                ]
                segs = []
                for (g0, gn, kind, dst) in groups:
                    for c0 in range(g0, g0 + gn, 256):
                        segs.append((c0, min(256, g0 + gn - c0), h_in, (kind, dst, g0)))
                cur = {}
                for (t0, n, ts) in tok_tiles(1024):
                    cur["t0"] = t0
                    norm_to_act(nb, PP_NMIX + l * 32, t0, n, actT)
                    linear(lb, actT, KC, ts, w_in[l * D:(l + 1) * D, :], segs)

            with ExitStack() as ph:
                ph.callback(kb.barrier)
                gw = sb("rgw", [128, 16, 128], st=ph)
                gwb = sb("rgwb", [128, 16, 128], BF16, st=ph)
                dma(gw.t[:], rgw_d[l * 2048:(l + 1) * 2048, :].rearrange("(g d) e -> d g e", d=128), w=[gw])
                op("dve", "tensor_copy", gwb.t[:], gw.t[:], r=[gw], w=[gwb])
                nsp = sb("nsp", [128, 8], st=ph)
                lam = pp.t[:, ppl + 56:ppl + 64]
                op("act", "activation", nsp.t[:], lam, AF.Exp, scale=-1.0, r=[pp], w=[nsp])
                op("act", "activation", nsp.t[:], nsp.t[:], AF.Ln, bias=1.0, r=[nsp], w=[nsp])
                op("dve", "tensor_scalar", nsp.t[:], nsp.t[:], -8.0, None, ALU.mult, r=[nsp], w=[nsp])
                XE = sb("rXE", [128, NE], st=ph)
                XC = sb("rXC", [128, NE], st=ph)
                XCb = sb("rXCb", [128, NE], BF16, st=ph)
                RG = sb("rRG", [128, NE], st=ph)
                IG = sb("rIG", [128, NE], st=ph)
                AA = sb("rAA", [128, NE], st=ph)
                HH = sb("rHH", [128, NE], st=ph)
                GG = sb("rGG", [128, T], st=ph)
                G2 = sb("rG2", [128, T], st=ph)
                YY = sb("rYY", [128, T], BF16, st=ph)
                h0 = sb("rh0", [128, 8 * NS], st=ph)
                hl = sb("rhl", [128, 8, 1 + NS], st=ph)
                cvo = sb("rcvo", [128, 8, 3 + 3 * NS], st=ph)
                dma(h0.t[:], sh_d[l * 128:(l + 1) * 128, :], w=[h0])
                op("dve", "memset", XE.t[:, 0:3], 0.0, w=[XE])
                NV = NE - 3
                for b in range(8):
                    dma(XE.t[:, 3:3 + TP], rxT[b * 128:(b + 1) * 128, 0:TP], r=[scr["rxT"]], w=[XE])
                    xes = XE.t[:, SB0:NE].rearrange("p (s j) -> p s j", j=11)
                    dma(xes[:, :, 3:11], rxT[b * 128:(b + 1) * 128, TP:T].rearrange("p (s j) -> p s j", j=8),
                        r=[scr["rxT"]], w=[XE], append=True)
                    dma(xes[:, :, 0:3],
                        src_d[l * 128:(l + 1) * 128, b * NS * 3:(b + 1) * NS * 3].rearrange("p (s j) -> p s j", j=3),
                        w=[XE], append=True)
                    dma(GG.t[:], rgT[b * 128:(b + 1) * 128, :], r=[scr["rgT"]], w=[GG])
                    cw = ppl + b * 4
                    op("dve", "tensor_scalar", XC.t[:, 0:NV], XE.t[:, 3:NE], pp.t[:, cw + 3:cw + 4],
                       pp.t[:, ppl + 32 + b:ppl + 33 + b], ALU.mult, ALU.add, r=[XE, pp], w=[XC])
                    for k in range(3):
                        op("dve", "scalar_tensor_tensor", XC.t[:, 0:NV], XE.t[:, k:k + NV], pp.t[:, cw + k:cw + k + 1],
                           XC.t[:, 0:NV], ALU.mult, ALU.add, r=[XE, pp, XC], w=[XC])
                    op("pool", "tensor_copy", XCb.t[:, 0:NV], XC.t[:, 0:NV], r=[XC], w=[XCb])
                    for c0 in range(0, NV, 512):
                        cn = min(512, NV - c0)
                        p1 = ps_next()
                        op("pe", "matmul", p1.t[:, :cn], gwb.t[:, b, :], XCb.t[:, c0:c0 + cn], start=True, stop=True,
                           r=[gwb, XCb], w=[p1])
                        op("act", "activation", RG.t[:, c0:c0 + cn], p1.t[:, :cn], AF.Sigmoid,
                           bias=pp.t[:, ppl + 40 + b:ppl + 41 + b], r=[p1, pp], w=[RG])
                        p2 = ps_next()
                        op("pe", "matmul", p2.t[:, :cn], gwb.t[:, 8 + b, :], XCb.t[:, c0:c0 + cn], start=True, stop=True,
                           r=[gwb, XCb], w=[p2])
                        op("act", "activation", IG.t[:, c0:c0 + cn], p2.t[:, :cn], AF.Sigmoid,
                           bias=pp.t[:, ppl + 48 + b:ppl + 49 + b], r=[p2, pp], w=[IG])
                    op("act", "activation", AA.t[:, 0:NV], RG.t[:, 0:NV], AF.Exp, scale=nsp.t[:, b:b + 1], r=[RG, nsp], w=[AA])
                    op("dve", "tensor_tensor", RG.t[:, 0:NV], AA.t[:, 0:NV], AA.t[:, 0:NV], ALU.mult, r=[AA], w=[RG])
                    op("dve", "tensor_scalar", RG.t[:, 0:NV], RG.t[:, 0:NV], -1.0, 1.0, ALU.mult, ALU.add, r=[RG], w=[RG])
                    op("dve", "tensor_scalar", RG.t[:, 0:NV], RG.t[:, 0:NV], 0.0, None, ALU.max, r=[RG], w=[RG])
                    op("act", "activation", RG.t[:, 0:NV], RG.t[:, 0:NV], AF.Sqrt, r=[RG], w=[RG])
                    op("dve", "tensor_tensor", IG.t[:, 0:NV], IG.t[:, 0:NV], XC.t[:, 0:NV], ALU.mult, r=[IG, XC], w=[IG])
                    op("dve", "tensor_tensor", IG.t[:, 0:NV], IG.t[:, 0:NV], RG.t[:, 0:NV], ALU.mult, r=[IG, RG], w=[IG])
                    op("dve", "tensor_tensor_scan", HH.t[:, 0:TP], AA.t[:, 0:TP], IG.t[:, 0:TP], 0.0, ALU.mult, ALU.add,
                       r=[AA, IG], w=[HH])
                    for s in range(NS):
                        e0 = SB0 + 11 * s
                        op("dve", "tensor_tensor_scan", HH.t[:, e0:e0 + 8], AA.t[:, e0:e0 + 8], IG.t[:, e0:e0 + 8],
                           h0.t[:, b * NS + s:b * NS + s + 1], ALU.mult, ALU.add, r=[AA, IG, h0], w=[HH])
                    op("dve", "tensor_tensor", G2.t[:], GG.t[:], GG.t[:], ALU.mult, r=[GG], w=[G2])
                    op("dve", "tensor_scalar", G2.t[:], G2.t[:], 0.044715, 1.0, ALU.mult, ALU.add, r=[G2], w=[G2])
                    op("dve", "tensor_tensor", G2.t[:], G2.t[:], GG.t[:], ALU.mult, r=[G2, GG], w=[G2])
                    op("act", "activation", G2.t[:], G2.t[:], AF.Sigmoid, scale=1.5957691216057308, r=[G2], w=[G2])
                    op("dve", "tensor_tensor", G2.t[:], G2.t[:], GG.t[:], ALU.mult, r=[G2, GG], w=[G2])
                    op("dve", "tensor_tensor", YY.t[:, 0:TP], HH.t[:, 0:TP], G2.t[:, 0:TP], ALU.mult, r=[HH, G2], w=[YY])
                    hs = HH.t[:, SB0:SB0 + 11 * NS].rearrange("p (s j) -> p s j", j=11)
                    op("dve", "tensor_tensor", YY.t[:, TP:T].rearrange("p (s j) -> p s j", j=8), hs[:, :, 0:8],
                       G2.t[:, TP:T].rearrange("p (s j) -> p s j", j=8), ALU.mult, r=[HH, G2], w=[YY])
                    dma(mixT[1024 + b * 128:1024 + (b + 1) * 128, :], YY.t[:], r=[YY], w=[scr["mixT"]], append=True)
                    op("act", "activation", hl.t[:, b, 0:1], HH.t[:, TP - 1:TP], AF.Copy, r=[HH], w=[hl])
                    op("act", "activation", hl.t[:, b, 1:1 + NS], hs[:, :, 7], AF.Copy, r=[HH], w=[hl])
                    op("act", "activation", cvo.t[:, b, 0:3], XE.t[:, TP:TP + 3], AF.Copy, r=[XE], w=[cvo])
                    op("act", "activation", cvo.t[:, b, 3:3 + 3 * NS].rearrange("p (s j) -> p s j", j=3), xes[:, :, 8:11],
                       AF.Copy, r=[XE], w=[cvo])
                dma(o_ph[l * 128:(l + 1) * 128, :], hl.t[:, :, 0], r=[hl], w=[scr["out"]], append=True)
                dma(o_sh[l * 128:(l + 1) * 128, :].rearrange("p (b s) -> p b s", b=8), hl.t[:, :, 1:1 + NS], r=[hl],
                    w=[scr["out"]], append=True)
                dma(o_prc[l * 128:(l + 1) * 128, :].rearrange("p (b j) -> p b j", b=8), cvo.t[:, :, 0:3], r=[cvo],
                    w=[scr["out"]], append=True)
                dma(o_src[l * 128:(l + 1) * 128, :].rearrange("p (b j) -> p b j", b=8), cvo.t[:, :, 3:3 + 3 * NS], r=[cvo],
                    w=[scr["out"]], append=True)

            with ExitStack() as ph:
                ph.callback(kb.barrier)
                XE = sb("gXE", [128, NE], st=ph)
                XC = sb("gXC", [128, NE], st=ph)
                XS = sb("gXS", [128, T], st=ph)
                SQ = sb("gSQ", [128, T], st=ph)
                XN = sb("gXN", [128, T], BF16, st=ph)
                cvo = sb("gcvo", [128, 48, 3 + 3 * NS], st=ph)
                tmb = [sb("gtmb%d" % i, [128, 4, 128], BF16, st=ph) for i in range(2)]
                op("dve", "memset", XE.t[:, 0:3], 0.0, w=[XE])
                NV = NE - 3
                for b in range(48):
                    dma(XE.t[:, 3:3 + TP], gqkvT[b * 128:(b + 1) * 128, 0:TP], r=[scr["gqkvT"]], w=[XE])
                    xes = XE.t[:, SB0:NE].rearrange("p (s j) -> p s j", j=11)
                    dma(xes[:, :, 3:11], gqkvT[b * 128:(b + 1) * 128, TP:T].rearrange("p (s j) -> p s j", j=8),
                        r=[scr["gqkvT"]], w=[XE], append=True)
                    dma(xes[:, :, 0:3],
                        sgc_d[l * 128:(l + 1) * 128, b * NS * 3:(b + 1) * NS * 3].rearrange("p (s j) -> p s j", j=3),
                        w=[XE], append=True)
                    cw = ppl + 64 + b * 4
                    op("dve", "tensor_scalar", XC.t[:, 0:NV], XE.t[:, 3:NE], pp.t[:, cw + 3:cw + 4], None, ALU.mult,
                       r=[XE, pp], w=[XC])
                    for k in range(3):
                        op("dve", "scalar_tensor_tensor", XC.t[:, 0:NV], XE.t[:, k:k + NV], pp.t[:, cw + k:cw + k + 1],
                           XC.t[:, 0:NV], ALU.mult, ALU.add, r=[XE, pp, XC], w=[XC])
                    op("act", "activation", XS.t[:, 0:TP], XC.t[:, 0:TP], AF.Silu, r=[XC], w=[XS])
                    xcs = XC.t[:, SB0:SB0 + 11 * NS].rearrange("p (s j) -> p s j", j=11)
                    op("act", "activation", XS.t[:, TP:T].rearrange("p (s j) -> p s j", j=8), xcs[:, :, 0:8], AF.Silu,
                       r=[XC], w=[XS])
                    op("act", "activation", cvo.t[:, b, 0:3], XE.t[:, TP:TP + 3], AF.Copy, r=[XE], w=[cvo])
                    op("act", "activation", cvo.t[:, b, 3:3 + 3 * NS].rearrange("p (s j) -> p s j", j=3), xes[:, :, 8:11],
                       AF.Copy, r=[XE], w=[cvo])
                    if b < 32:
                        op("dve", "tensor_tensor", SQ.t[:], XS.t[:], XS.t[:], ALU.mult, r=[XS], w=[SQ])
                        for c0 in range(0, T, 512):
                            cn = min(512, T - c0)
                            p1 = ps_next()
                            op("pe", "matmul", p1.t[:, :cn], ones_f, SQ.t[:, c0:c0 + cn], start=True, stop=True,
                               r=[SQ, cst], w=[p1])
                            op("dve", "tensor_scalar", SQ.t[:, c0:c0 + cn], p1.t[:, :cn], EPS, None, ALU.add, r=[p1], w=[SQ])
                        op("act", "activation", SQ.t[:], SQ.t[:], AF.Sqrt, r=[SQ], w=[SQ])
                        op("dve", "reciprocal", SQ.t[:], SQ.t[:], r=[SQ], w=[SQ])
                        if b < 16:
                            op("dve", "scalar_tensor_tensor", XN.t[:], XS.t[:], 128.0 ** -0.5, SQ.t[:], ALU.mult, ALU.mult,
                               r=[XS, SQ], w=[XN])
                        else:
                            op("dve", "tensor_tensor", XN.t[:], XS.t[:], SQ.t[:], ALU.mult, r=[XS, SQ], w=[XN])
                    else:
                        op("pool", "tensor_copy", XN.t[:], XS.t[:], r=[XS], w=[XN])
                    if b < 16:
                        dma(gqnT[b * 128:(b + 1) * 128, :], XN.t[:], r=[XN], w=[scr["gqnT"]], append=True)
                    elif b < 32:
                        dma(gknT[(b - 16) * 128:(b - 15) * 128, :], XN.t[:], r=[XN], w=[scr["gknT"]], append=True)
                    if b >= 16:
                        dst, dn = (gk_tok, "gk_tok") if b < 32 else (gv_tok, "gv_tok")
                        hcol = (b - 16) * 128 if b < 32 else (b - 32) * 128
                        for g0 in range(0, T, 512):
                            gn = min(512, T - g0)
                            pb_ = psb_next()
                            tb = tmb[(g0 // 512) % 2]
                            nsub = (gn + 127) // 128
                            for s in range(nsub):
                                ns = min(128, gn - s * 128)
                                op("pe", "transpose", pb_.t[:ns, s * 128:(s + 1) * 128],
                                   XN.t[:, g0 + s * 128:g0 + s * 128 + ns], ident_b, r=[XN, cstb], w=[pb_])
                            nfull = gn // 128
                            if nfull:
                                op("act", "activation", tb.t[:, :nfull, :],
                                   pb_.t[:, 0:nfull * 128].rearrange("p (s c) -> p s c", s=nfull), AF.Copy, r=[pb_], w=[tb])
                                dma(dst[g0:g0 + nfull * 128, hcol:hcol + 128].rearrange("(s p) c -> p s c", p=128),
                                    tb.t[:, :nfull, :], r=[tb], w=[scr[dn]], append=True)
                            if gn % 128:
                                ns = gn % 128
                                op("act", "activation", tb.t[:ns, nfull, :], pb_.t[:ns, nfull * 128:(nfull + 1) * 128],
                                   AF.Copy, r=[pb_], w=[tb])
                                dma(dst[g0 + nfull * 128:g0 + gn, hcol:hcol + 128], tb.t[:ns, nfull, :], r=[tb],
                                    w=[scr[dn]], append=True)
                dma(o_pgc[l * 128:(l + 1) * 128, :].rearrange("p (b j) -> p b j", b=48), cvo.t[:, :, 0:3], r=[cvo],
                    w=[scr["out"]], append=True)
                dma(o_sgc[l * 128:(l + 1) * 128, :].rearrange("p (b j) -> p b j", b=48), cvo.t[:, :, 3:3 + 3 * NS],
                    r=[cvo], w=[scr["out"]], append=True)

            chunks = [(0, 16)] + [(16 + 64 * j, 64) for j in range(SEQ // 64)]

            with ExitStack() as ph:
                ph.callback(kb.barrier)
                Cx = sb("mCx", [128, 4, 256], st=ph)
                Cn = sb("mCn", [128, 4], st=ph)
                Cxb = sb("mCxb", [128, 4, 256], BF16, st=ph)
                Cnb = sb("mCnb", [128, 4], BF16, st=ph)
                mbc = sb("mmbc", [128, 4], st=ph)
                wn = sb("mwn", [128, 1024], st=ph)
                dma(wn.t[:], pb_d[0:1, pbl + 8:pbl + 8 + 1024].partition_broadcast(128), w=[wn])
                IN = [dict(qT=sb("mqT%d" % i, [128, 4, 64], BF16, st=ph), kT=sb("mkT%d" % i, [128, 4, 64], BF16, st=ph),
                           k=sb("mk%d" % i, [64, 4, 128], BF16, st=ph), v=sb("mv%d" % i, [64, 4, 256], BF16, st=ph),
                           o=sb("mo%d" % i, [64, 1024], BF16, st=ph), g=sb("mg%d" % i, [64, 8], st=ph)) for i in range(2)]
                sm_ = sb("msm", [128, 64], st=ph)
                Et = sb("mE", [64, 4, 64], st=ph)
                Ds = sb("mDs", [64, 4, 64], st=ph)
                Pm = sb("mPm", [64, 4, 64], st=ph)
                Sb_ = sb("mSb", [64, 4, 64], BF16, st=ph)
                STb = sb("mSTb", [64, 4, 64], BF16, st=ph)
                A2 = sb("mA2", [64, 256], st=ph)
                A2n = sb("mA2n", [64, 4], st=ph)
                Hh = sb("mHh", [64, 4, 256], st=ph)
                Hsq = sb("mHsq", [64, 256], st=ph)
                Yb = sb("mYb", [64, 1024], BF16, st=ph)
                YT = sb("mYT", [128, 8, 64], BF16, st=ph)
                kw_ = sb("mkw", [64, 4, 128], BF16, st=ph)
                onesv = sb("mones", [64, 1], BF16, st=ph)
                op("dve", "memset", onesv.t[:], 1.0, w=[onesv])
                nn = [0]

                def mlstm_chunk(t0, L):
                    I_ = IN[nn[0] % 2]
                    nn[0] += 1
                    qT_, kT_, k_, v_, o_, g_ = I_["qT"], I_["kT"], I_["k"], I_["v"], I_["o"], I_["g"]
                    dma(qT_.t[:, :, :L], mqT[:, t0:t0 + L].rearrange("(h d) t -> d h t", d=128), r=[scr["mqT"]], w=[qT_])
                    dma(kT_.t[:, :, :L], mkT[:, t0:t0 + L].rearrange("(h d) t -> d h t", d=128), r=[scr["mkT"]], w=[kT_])
                    dma(k_.t[:L, :, :], mk_tok[t0:t0 + L, :].rearrange("t (h d) -> t h d", d=128), r=[scr["mk_tok"]], w=[k_])
                    dma(v_.t[:L, :, :], mv_tok[t0:t0 + L, :].rearrange("t (h d) -> t h d", d=256), r=[scr["mv_tok"]], w=[v_])
                    dma(o_.t[:L, :], mo_tok[t0:t0 + L, :], r=[scr["mo_tok"]], w=[o_])
                    dma(g_.t[:L, :], mg_tok[t0:t0 + L, :], r=[scr["mg_tok"]], w=[g_])
                    li = g_.t[:L, 0:4]
                    lf = g_.t[:L, 4:8]
                    sel = cst.t[:L, {8: C_SEL8, 16: C_SEL16, 64: C_SEL64}[L]:][:, 0:128]
                    S = sm_.t
                    p1 = ps_next()
                    op("pe", "matmul", p1.t[:L, 0:4], Umat[:L, :L], lf, start=True, stop=True, r=[cst, g_], w=[p1])
                    op("pe", "matmul", p1.t[:, 8:12], ones_f[:L, :], lf, start=True, stop=True, r=[cst, g_], w=[p1])
                    op("dve", "tensor_copy", S[:L, 0:4], p1.t[:L, 0:4], r=[p1], w=[sm_])
                    op("dve", "tensor_copy", S[:, 4:8], p1.t[:, 8:12], r=[p1], w=[sm_])
                    SLb = cst.t[:L, C_SL:C_SL + L].unsqueeze(1).broadcast_to([L, 4, L])
                    Ib = cst.t[:L, C_ID:C_ID + L].unsqueeze(1).broadcast_to([L, 4, L])
                    op("dve", "tensor_tensor", Et.t[:L, :, :L], SLb, lf.unsqueeze(2).broadcast_to([L, 4, L]), ALU.mult,
                       r=[cst, g_], w=[Et])
                    op("dve", "tensor_tensor", Ds.t[:L, :, :L], Ib, li.unsqueeze(2).broadcast_to([L, 4, L]), ALU.mult,
                       r=[cst, g_], w=[Ds])
                    op("dve", "tensor_tensor", Et.t[:L, :, :L], Et.t[:L, :, :L], Ds.t[:L, :, :L], ALU.add, r=[Et, Ds], w=[Et])
                    p2 = ps_next()
                    for h in range(4):
                        op("pe", "matmul", p2.t[:L, h * 64:h * 64 + L], Umat[:L, :L], Et.t[:L, h, :L], start=True, stop=True,
                           r=[cst, Et], w=[p2])
                    negm = cst.t[:L, C_NEGM:C_NEGM + L].unsqueeze(1).broadcast_to([L, 4, L])
                    p2v = p2.t[:L, 0:256].rearrange("p (h s) -> p h s", h=4)[:, :, :L]
                    op("dve", "tensor_tensor", Ds.t[:L, :, :L], p2v, negm, ALU.add, r=[p2, cst], w=[Ds])
                    op("dve", "tensor_reduce", S[:L, 8:12], Ds.t[:L, :, :L], AX.X, ALU.max, r=[Ds], w=[sm_])
                    op("dve", "tensor_tensor", S[:L, 12:16], mbc.t[:L, :], S[:L, 0:4], ALU.add, r=[mbc, sm_], w=[sm_])
                    op("dve", "tensor_tensor", S[:L, 16:20], S[:L, 8:12], S[:L, 12:16], ALU.max, r=[sm_], w=[sm_])
                    op("dve", "tensor_scalar", S[:L, 20:24], S[:L, 16:20], -1.0, None, ALU.mult, r=[sm_], w=[sm_])
                    for h in range(4):
                        op("act", "activation", Pm.t[:L, h, :L], Ds.t[:L, h, :L], AF.Exp, bias=S[:L, 20 + h:21 + h],
                           r=[Ds, sm_], w=[Pm])
                    op("dve", "tensor_tensor", S[:L, 24:28], S[:L, 12:16], S[:L, 16:20], ALU.subtract, r=[sm_], w=[sm_])
                    op("act", "activation", S[:L, 24:28], S[:L, 24:28], AF.Exp, r=[sm_], w=[sm_])
                    op("act", "activation", S[:L, 28:32], S[:L, 20:24], AF.Exp, r=[sm_], w=[sm_])
                    p3 = ps_next()
                    for h in range(4):
                        op("pe", "matmul", p3.t[:L, h * 64:h * 64 + L], qT_.t[:, h, :L], kT_.t[:, h, :L], start=True,
                           stop=True, r=[qT_, kT_], w=[p3])
                    p3v = p3.t[:L, 0:256].rearrange("p (h s) -> p h s", h=4)[:, :, :L]
                    op("dve", "tensor_tensor", Sb_.t[:L, :, :L], p3v, Pm.t[:L, :, :L], ALU.mult, r=[p3, Pm], w=[Sb_])
                    pb_ = psb_next()
                    for h in range(4):
                        op("pe", "transpose", pb_.t[:L, h * 64:h * 64 + L], Sb_.t[:L, h, :L], ident_b[:L, :L],
                           r=[Sb_, cstb], w=[pb_])
                    op("act", "activation", STb.t[:L, :, :L], pb_.t[:L, 0:256].rearrange("p (h s) -> p h s", h=4)[:, :, :L],
                       AF.Copy, r=[pb_], w=[STb])
                    p4 = ps_next()
                    op("pe", "matmul", p4.t[:, 0:4], sel, S[:L, 16:20], start=True, stop=True, r=[cst, sm_], w=[p4])
                    op("dve", "tensor_copy", S[:, 32:36], p4.t[:, 0:4], r=[p4], w=[sm_])
                    op("dve", "tensor_tensor", S[:L, 36:40], S[:L, 4:8], S[:L, 0:4], ALU.subtract, r=[sm_], w=[sm_])
                    op("dve", "tensor_tensor", S[:L, 36:40], S[:L, 36:40], li, ALU.add, r=[sm_, g_], w=[sm_])
                    op("dve", "tensor_tensor", S[:L, 36:40], S[:L, 36:40], S[:L, 32:36], ALU.subtract, r=[sm_], w=[sm_])
                    op("act", "activation", S[:L, 36:40], S[:L, 36:40], AF.Exp, r=[sm_], w=[sm_])
                    op("dve", "tensor_tensor", S[:, 40:44], mbc.t[:, :], S[:, 4:8], ALU.add, r=[mbc, sm_], w=[sm_])
                    op("dve", "tensor_tensor", S[:, 40:44], S[:, 40:44], S[:, 32:36], ALU.subtract, r=[sm_], w=[sm_])
                    op("act", "activation", S[:, 40:44], S[:, 40:44], AF.Exp, r=[sm_], w=[sm_])
                    for h in range(4):
                        pa = ps_next()
                        op("pe", "matmul", pa.t[:L, 0:256], qT_.t[:, h, :L], Cxb.t[:, h, :], start=True, stop=True,
                           r=[qT_, Cxb], w=[pa])
                        op("pe", "matmul", pa.t[:L, 256:257], qT_.t[:, h, :L], Cnb.t[:, h:h + 1], start=True, stop=True,
                           r=[qT_, Cnb], w=[pa])
                        pc = ps_next()
                        op("pe", "matmul", pc.t[:L, 0:256], STb.t[:L, h, :L], v_.t[:L, h, :], start=True, stop=True,
                           r=[STb, v_], w=[pc])
                        op("pe", "matmul", pc.t[:L, 256:257], STb.t[:L, h, :L], onesv.t[:L, :], start=True, stop=True,
                           r=[STb, onesv], w=[pc])
                        op("act", "activation", A2.t[:L, :], pc.t[:L, 0:256], AF.Copy, r=[pc], w=[A2])
                        op("act", "activation", A2n.t[:L, h:h + 1], pc.t[:L, 256:257], AF.Copy, r=[pc], w=[A2n])
                        op("dve", "scalar_tensor_tensor", Hh.t[:L, h, :], pa.t[:L, 0:256], S[:L, 24 + h:25 + h], A2.t[:L, :],
                           ALU.mult, ALU.add, r=[pa, sm_, A2], w=[Hh])
                        op("dve", "scalar_tensor_tensor", S[:L, 44 + h:45 + h], pa.t[:L, 256:257], S[:L, 24 + h:25 + h],
                           A2n.t[:L, h:h + 1], ALU.mult, ALU.add, r=[pa, sm_, A2n], w=[sm_])
                    op("act", "activation", S[:L, 44:48], S[:L, 44:48], AF.Abs, r=[sm_], w=[sm_])
                    op("dve", "tensor_tensor", S[:L, 44:48], S[:L, 44:48], S[:L, 28:32], ALU.max, r=[sm_], w=[sm_])
                    op("dve", "reciprocal", S[:L, 44:48], S[:L, 44:48], r=[sm_], w=[sm_])
                    for h in range(4):
                        op("dve", "tensor_scalar", Hh.t[:L, h, :], Hh.t[:L, h, :], S[:L, 44 + h:45 + h], None, ALU.mult,
                           r=[Hh, sm_], w=[Hh])
                        op("act", "activation", Hsq.t[:L, :], Hh.t[:L, h, :], AF.Square, accum_out=S[:L, 48 + h:49 + h],
                           r=[Hh], w=[Hsq, sm_])
                    op("dve", "tensor_scalar", S[:L, 48:52], S[:L, 48:52], 1.0 / 256, EPS, ALU.mult, ALU.add, r=[sm_], w=[sm_])
                    op("act", "activation", S[:L, 48:52], S[:L, 48:52], AF.Sqrt, r=[sm_], w=[sm_])
                    op("dve", "reciprocal", S[:L, 48:52], S[:L, 48:52], r=[sm_], w=[sm_])
                    for h in range(4):
                        op("dve", "scalar_tensor_tensor", Hh.t[:L, h, :], Hh.t[:L, h, :], S[:L, 48 + h:49 + h],
                           wn.t[:L, h * 256:(h + 1) * 256], ALU.mult, ALU.mult, r=[Hh, sm_, wn], w=[Hh])
                    op("dve", "tensor_tensor", Yb.t[:L, :], Hh.t[:L, :, :].rearrange("p h e -> p (h e)"), o_.t[:L, :],
                       ALU.mult, r=[Hh, o_], w=[Yb])
                    pb2 = psb_next()
                    for j in range(8):
                        op("pe", "transpose", pb2.t[:, j * 64:j * 64 + L], Yb.t[:L, j * 128:(j + 1) * 128], ident_b[:L, :L],
                           r=[Yb, cstb], w=[pb2])
                    op("act", "activation", YT.t[:, :, :L], pb2.t[:, 0:512].rearrange("p (j t) -> p j t", j=8)[:, :, :L],
                       AF.Copy, r=[pb2], w=[YT])
                    dma(mixT[0:1024, t0:t0 + L].rearrange("(j p) t -> p j t", p=128), YT.t[:, :, :L], r=[YT],
                        w=[scr["mixT"]], append=True)
                    op("dve", "tensor_tensor", kw_.t[:L, :, :], k_.t[:L, :, :],
                       S[:L, 36:40].unsqueeze(2).broadcast_to([L, 4, 128]), ALU.mult, r=[k_, sm_], w=[kw_])
                    for h in range(4):
                        pd = ps_next()
                        op("pe", "matmul", pd.t[:, 0:256], kw_.t[:L, h, :], v_.t[:L, h, :], start=True, stop=True,
                           r=[kw_, v_], w=[pd])
                        op("pe", "matmul", pd.t[:, 256:257], kw_.t[:L, h, :], onesv.t[:L, :], start=True, stop=True,
                           r=[kw_, onesv], w=[pd])
                        op("dve", "scalar_tensor_tensor", Cx.t[:, h, :], Cx.t[:, h, :], S[:, 40 + h:41 + h], pd.t[:, 0:256],
                           ALU.mult, ALU.add, r=[Cx, sm_, pd], w=[Cx])
                        op("dve", "scalar_tensor_tensor", Cn.t[:, h:h + 1], Cn.t[:, h:h + 1], S[:, 40 + h:41 + h],
                           pd.t[:, 256:257], ALU.mult, ALU.add, r=[Cn, sm_, pd], w=[Cn])
                    op("act", "activation", Cxb.t[:], Cx.t[:], AF.Copy, r=[Cx], w=[Cxb])
                    op("act", "activation", Cnb.t[:], Cn.t[:], AF.Copy, r=[Cn], w=[Cnb])
                    op("dve", "tensor_copy", mbc.t[:, :], S[:, 32:36], r=[sm_], w=[mbc])

                def m_store(oC, on, om, row):
                    dma(oC[row * 512:(row + 1) * 512, :].rearrange("(h k) v -> k h v", k=128), Cx.t[:], r=[Cx],
                        w=[scr["out"]], append=True)
                    dma(on[row * 128:(row + 1) * 128, :], Cn.t[:], r=[Cn], w=[scr["out"]], append=True)
                    dma(om[row:row + 1, :], mbc.t[0:1, :], r=[mbc], w=[scr["out"]], append=True)

                op("dve", "memset", Cx.t[:], 0.0, w=[Cx])
                op("dve", "memset", Cn.t[:], 0.0, w=[Cn])
                op("dve", "memset", mbc.t[:], 0.0, w=[mbc])
                op("dve", "memset", Cxb.t[:], 0.0, w=[Cxb])
                op("dve", "memset", Cnb.t[:], 0.0, w=[Cnb])
                for (t0, L) in chunks:
                    mlstm_chunk(t0, L)
                m_store(o_pC, o_pn, o_pm, l)
                for s in range(NS):
                    row = l * NS + s
                    dma(Cx.t[:], sC_d[row * 512:(row + 1) * 512, :].rearrange("(h k) v -> k h v", k=128), w=[Cx])
                    dma(Cn.t[:], sn_d[row * 128:(row + 1) * 128, :], w=[Cn])
                    dma(mbc.t[:], sm_d[0:1, row * 4:row * 4 + 4].partition_broadcast(128), w=[mbc])
                    op("act", "activation", Cxb.t[:], Cx.t[:], AF.Copy, r=[Cx], w=[Cxb])
                    op("act", "activation", Cnb.t[:], Cn.t[:], AF.Copy, r=[Cn], w=[Cnb])
                    mlstm_chunk(TP + 8 * s, 8)
                    m_store(o_sC, o_sn, o_sm, row)

            with ExitStack() as ph:
                ph.callback(kb.barrier)
                St = sb("gS", [128, 16, 128], st=ph)
                Stb = sb("gSb", [128, 16, 128], BF16, st=ph)
                gwn = sb("ggwn", [128, 128], st=ph)
                dma(gwn.t[:], pb_d[0:1, pbl + 8 + 1024 + 32:pbl + 8 + 1024 + 32 + 128].partition_broadcast(128), w=[gwn])
                IN = [dict(qT=sb("gqT%d" % i, [128, 16, 64], BF16, st=ph), kT=sb("gkT%d" % i, [128, 16, 64], BF16, st=ph),
                           k=sb("gk%d" % i, [64, 16, 128], BF16, st=ph), v=sb("gv%d" % i, [64, 16, 128], BF16, st=ph),
                           z=sb("gz%d" % i, [64, 2048], BF16, st=ph), g=sb("gg%d" % i, [64, 48], st=ph)) for i in range(2)]
                sm_ = sb("gsm", [128, 128], st=ph)
                Et = sb("gE", [64, 16, 64], st=ph)
                E2 = sb("gE2", [64, 16, 64], st=ph)
                xB = sb("gxB", [64, 8, 64], st=ph)
                xC = sb("gxC", [64, 8, 64], st=ph)
                Bk = [sb("gBk%d" % i, [64, 8, 64], BF16, st=ph) for i in range(2)]
                Ck = [sb("gCk%d" % i, [64, 8, 64], BF16, st=ph) for i in range(2)]
                Qf = sb("gQf", [64, 8, 64], st=ph)
                Qb = sb("gQb", [64, 8, 64], BF16, st=ph)
                AT = sb("gAT", [64, 8, 64], BF16, st=ph)
                R0w = sb("gR0w", [64, 8, 128], BF16, st=ph)
                YwT = sb("gYwT", [128, 8, 64], BF16, st=ph)
                vn = sb("gvn", [64, 8, 128], BF16, st=ph)
                O2 = sb("gO2", [64, 8, 128], st=ph)
                Oo = sb("gOo", [64, 16, 128], st=ph)
                Osq = sb("gOsq", [64, 16, 128], st=ph)
                Yb = sb("gYb", [64, 2048], BF16, st=ph)
                YT = sb("gYT", [128, 16, 64], BF16, st=ph)
                kd = sb("gkd", [64, 8, 128], BF16, st=ph)
                nn = [0]

                def gdn_chunk(t0, L):
                    I_ = IN[nn[0] % 2]
                    nn[0] += 1
                    qT_, kT_, k_, v_, z_, g_ = I_["qT"], I_["kT"], I_["k"], I_["v"], I_["z"], I_["g"]
                    dma(qT_.t[:, :, :L], gqnT[:, t0:t0 + L].rearrange("(h d) t -> d h t", d=128), r=[scr["gqnT"]], w=[qT_])
                    dma(kT_.t[:, :, :L], gknT[:, t0:t0 + L].rearrange("(h d) t -> d h t", d=128), r=[scr["gknT"]], w=[kT_])
                    dma(k_.t[:L, :, :], gk_tok[t0:t0 + L, :].rearrange("t (h d) -> t h d", d=128), r=[scr["gk_tok"]], w=[k_])
                    dma(v_.t[:L, :, :], gv_tok[t0:t0 + L, :].rearrange("t (h d) -> t h d", d=128), r=[scr["gv_tok"]], w=[v_])
                    dma(z_.t[:L, :], gz_tok[t0:t0 + L, :], r=[scr["gz_tok"]], w=[z_])
                    dma(g_.t[:L, :], gg_tok[t0:t0 + L, :], r=[scr["gg_tok"]], w=[g_])
                    beta = g_.t[:L, 0:16]
                    lnb = g_.t[:L, 16:32]
                    gg = g_.t[:L, 32:48]
                    S = sm_.t
                    p1 = ps_next()
                    op("pe", "matmul", p1.t[:L, 0:16], Umat[:L, :L], gg, start=True, stop=True, r=[cst, g_], w=[p1])
                    op("pe", "matmul", p1.t[:, 16:32], ones_f[:L, :], gg, start=True, stop=True, r=[cst, g_], w=[p1])
                    op("dve", "tensor_copy", S[:L, 0:16], p1.t[:L, 0:16], r=[p1], w=[sm_])
                    op("dve", "tensor_copy", S[:, 16:32], p1.t[:, 16:32], r=[p1], w=[sm_])
                    op("act", "activation", S[:L, 32:48], S[:L, 0:16], AF.Exp, r=[sm_], w=[sm_])
                    op("act", "activation", S[:, 48:64], S[:, 16:32], AF.Exp, r=[sm_], w=[sm_])
                    op("dve", "tensor_tensor", S[:L, 64:80], S[:L, 16:32], S[:L, 0:16], ALU.subtract, r=[sm_], w=[sm_])
                    op("act", "activation", S[:L, 64:80], S[:L, 64:80], AF.Exp, r=[sm_], w=[sm_])
                    op("dve", "tensor_tensor", S[:L, 64:80], S[:L, 64:80], beta, ALU.mult, r=[sm_, g_], w=[sm_])
                    SLb = cst.t[:L, C_SL:C_SL + L].unsqueeze(1).broadcast_to([L, 16, L])
                    Ib = cst.t[:L, C_ID:C_ID + L].unsqueeze(1).broadcast_to([L, 16, L])
                    op("dve", "tensor_tensor", Et.t[:L, :, :L], SLb, gg.unsqueeze(2).broadcast_to([L, 16, L]), ALU.mult,
                       r=[cst, g_], w=[Et])
                    op("dve", "tensor_tensor", E2.t[:L, :, :L], Ib, lnb.unsqueeze(2).broadcast_to([L, 16, L]), ALU.mult,
                       r=[cst, g_], w=[E2])
                    op("dve", "tensor_tensor", Et.t[:L, :, :L], Et.t[:L, :, :L], E2.t[:L, :, :L], ALU.add, r=[Et, E2], w=[Et])
                    mls = cst.t[:L, C_MLSN:C_MLSN + L].unsqueeze(1).broadcast_to([L, 8, L])
                    mus = cst.t[:L, C_MUSN:C_MUSN + L].unsqueeze(1).broadcast_to([L, 8, L])
                    mui = cst.t[:L, C_MUI:C_MUI + L].unsqueeze(1).broadcast_to([L, 8, L])
                    idb = cst.t[:L, C_ID:C_ID + L].unsqueeze(1).broadcast_to([L, 8, L])

                    def v8(pt):
                        return pt.t[:L, 0:512].rearrange("p (h s) -> p h s", h=8)[:, :, :L]

                    for hg in range(2):
                        h0_ = hg * 8
                        pB = ps_next()
                        for j in range(8):
                            op("pe", "matmul", pB.t[:L, j * 64:j * 64 + L], Umat[:L, :L], Et.t[:L, h0_ + j, :L], start=True,
                               stop=True, r=[cst, Et], w=[pB])
                        op("act", "activation", xB.t[:L, :, :L], v8(pB), AF.Exp, r=[pB], w=[xB])
                        pC = ps_next()
                        for j in range(8):
                            op("pe", "matmul", pC.t[:L, j * 64:j * 64 + L], Et.t[:L, h0_ + j, :L], Umat[:L, :L], start=True,
                               stop=True, r=[cst, Et], w=[pC])
                        op("act", "activation", xC.t[:L, :, :L], v8(pC), AF.Exp, r=[pC], w=[xC])
                        pM = ps_next()
                        for j in range(8):
                            op("pe", "matmul", pM.t[:L, j * 64:j * 64 + L], kT_.t[:, h0_ + j, :L], kT_.t[:, h0_ + j, :L],
                               start=True, stop=True, r=[kT_], w=[pM])
                        pK = ps_next()
                        for j in range(8):
                            op("pe", "matmul", pK.t[:L, j * 64:j * 64 + L], kT_.t[:, h0_ + j, :L], qT_.t[:, h0_ + j, :L],
                               start=True, stop=True, r=[kT_, qT_], w=[pK])
                        op("dve", "tensor_tensor", xB.t[:L, :, :L], v8(pM), xB.t[:L, :, :L], ALU.mult, r=[pM, xB], w=[xB])
                        op("dve", "tensor_tensor", Bk[0].t[:L, :, :L], xB.t[:L, :, :L], mls, ALU.mult, r=[xB, cst], w=[Bk[0]])
                        op("dve", "tensor_tensor", Qf.t[:L, :, :L], v8(pM), xC.t[:L, :, :L], ALU.mult, r=[pM, xC], w=[Qf])
                        op("dve", "tensor_tensor", Ck[0].t[:L, :, :L], Qf.t[:L, :, :L], mus, ALU.mult, r=[Qf, cst], w=[Ck[0]])
                        op("dve", "tensor_tensor", xC.t[:L, :, :L], v8(pK), xC.t[:L, :, :L], ALU.mult, r=[pK, xC], w=[xC])
                        op("dve", "tensor_tensor", AT.t[:L, :, :L], xC.t[:L, :, :L], mui, ALU.mult, r=[xC, cst], w=[AT])
                        op("dve", "tensor_tensor", Qf.t[:L, :, :L], Qf.t[:L, :, :L], mus, ALU.mult, r=[Qf, cst], w=[Qf])
                        op("dve", "tensor_tensor", Qf.t[:L, :, :L], Qf.t[:L, :, :L], idb, ALU.add, r=[Qf, cst], w=[Qf])
                        op("act", "activation", Qb.t[:L, :, :L], Qf.t[:L, :, :L], AF.Copy, r=[Qf], w=[Qb])
                        mlev = 1
                        cur_ = 0
                        while 2 * mlev < L:
                            Bc, Cc, Bn, Cn_ = Bk[cur_], Ck[cur_], Bk[1 - cur_], Ck[1 - cur_]
                            pP = ps_next()
                            for j in range(8):
                                op("pe", "matmul", pP.t[:L, j * 64:j * 64 + L], Bc.t[:L, j, :L], Cc.t[:L, j, :L], start=True,
                                   stop=True, r=[Bc, Cc], w=[pP])
                            pQ = ps_next()
                            for j in range(8):
                                op("pe", "matmul", pQ.t[:L, j * 64:j * 64 + L], Cc.t[:L, j, :L], Bc.t[:L, j, :L], start=True,
                                   stop=True, r=[Bc, Cc], w=[pQ])
                            op("act", "activation", Cn_.t[:L, :, :L], v8(pP), AF.Copy, r=[pP], w=[Cn_])
                            op("dve", "tensor_copy", Bn.t[:L, :, :L], v8(pQ), r=[pQ], w=[Bn])
                            cur_ = 1 - cur_
                            mlev *= 2
                            pR = ps_next()
                            for j in range(8):
                                op("pe", "matmul", pR.t[:L, j * 64:j * 64 + L], Bk[cur_].t[:L, j, :L], Qb.t[:L, j, :L],
                                   start=True, stop=True, r=[Bk[cur_], Qb], w=[pR])
                            op("dve", "tensor_tensor", Qf.t[:L, :, :L], Qf.t[:L, :, :L], v8(pR), ALU.add, r=[Qf, pR], w=[Qf])
                            op("act", "activation", Qb.t[:L, :, :L], Qf.t[:L, :, :L], AF.Copy, r=[Qf], w=[Qb])
                        op("dve", "tensor_tensor", R0w.t[:L, :, :], k_.t[:L, h0_:h0_ + 8, :],
                           S[:L, 32 + h0_:40 + h0_].unsqueeze(2).broadcast_to([L, 8, 128]), ALU.mult, r=[k_, sm_], w=[R0w])
                        pY = ps_next()
                        for j in range(8):
                            op("pe", "matmul", pY.t[:, j * 64:j * 64 + L], R0w.t[:L, j, :], Qb.t[:L, j, :L], start=True,
                               stop=True, r=[R0w, Qb], w=[pY])
                        op("dve", "tensor_scalar", YwT.t[:, :, :L], pY.t[:, 0:512].rearrange("p (h s) -> p h s", h=8)[:, :, :L],
                           -1.0, None, ALU.mult, r=[pY], w=[YwT])
                        pv = [ps_next(), ps_next()]
                        for j in range(8):
                            dst_ = pv[j // 4].t[:L, (j % 4) * 128:(j % 4 + 1) * 128]
                            op("pe", "matmul", dst_, Qb.t[:L, j, :L], v_.t[:L, h0_ + j, :], start=True, stop=False,
                               r=[Qb, v_], w=[pv[j // 4]])
                            op("pe", "matmul", dst_, YwT.t[:, j, :L], Stb.t[:, h0_ + j, :], start=False, stop=True,
                               r=[YwT, Stb], w=[pv[j // 4]])
                        for q_ in range(2):
                            op("act", "activation", vn.t[:L, q_ * 4:(q_ + 1) * 4, :],
                               pv[q_].t[:L, :].rearrange("p (h e) -> p h e", h=4), AF.Copy, r=[pv[q_]], w=[vn])
                        po1 = [ps_next(), ps_next()]
                        for j in range(8):
                            op("pe", "matmul", po1[j // 4].t[:L, (j % 4) * 128:(j % 4 + 1) * 128], AT.t[:L, j, :L],
                               vn.t[:L, j, :], start=True, stop=True, r=[AT, vn], w=[po1[j // 4]])
                        for q_ in range(2):
                            op("act", "activation", O2.t[:L, q_ * 4:(q_ + 1) * 4, :],
                               po1[q_].t[:L, :].rearrange("p (h e) -> p h e", h=4), AF.Copy, r=[po1[q_]], w=[O2])
                        po2 = [ps_next(), ps_next()]
                        for j in range(8):
                            op("pe", "matmul", po2[j // 4].t[:L, (j % 4) * 128:(j % 4 + 1) * 128], qT_.t[:, h0_ + j, :L],
                               Stb.t[:, h0_ + j, :], start=True, stop=True, r=[qT_, Stb], w=[po2[j // 4]])
                        for q_ in range(2):
                            op("dve", "tensor_tensor", Oo.t[:L, h0_ + q_ * 4:h0_ + q_ * 4 + 4, :],
                               po2[q_].t[:L, :].rearrange("p (h e) -> p h e", h=4),
                               S[:L, 32 + h0_ + q_ * 4:32 + h0_ + q_ * 4 + 4].unsqueeze(2).broadcast_to([L, 4, 128]),
                               ALU.mult, r=[po2[q_], sm_], w=[Oo])
                        op("dve", "tensor_tensor", Oo.t[:L, h0_:h0_ + 8, :], Oo.t[:L, h0_:h0_ + 8, :], O2.t[:L, :, :], ALU.add,
                           r=[Oo, O2], w=[Oo])
                        op("dve", "tensor_tensor", kd.t[:L, :, :], k_.t[:L, h0_:h0_ + 8, :],
                           S[:L, 64 + h0_:72 + h0_].unsqueeze(2).broadcast_to([L, 8, 128]), ALU.mult, r=[k_, sm_], w=[kd])
                        pS_ = [ps_next(), ps_next()]
                        for j in range(8):
                            op("pe", "matmul", pS_[j // 4].t[:, (j % 4) * 128:(j % 4 + 1) * 128], kd.t[:L, j, :], vn.t[:L, j, :],
                               start=True, stop=True, r=[kd, vn], w=[pS_[j // 4]])
                        for q_ in range(2):
                            hs_ = h0_ + q_ * 4
                            op("dve", "tensor_tensor", St.t[:, hs_:hs_ + 4, :], St.t[:, hs_:hs_ + 4, :],
                               S[:, 48 + hs_:52 + hs_].unsqueeze(2).broadcast_to([128, 4, 128]), ALU.mult, r=[St, sm_], w=[St])
                            op("dve", "tensor_tensor", St.t[:, hs_:hs_ + 4, :], St.t[:, hs_:hs_ + 4, :],
                               pS_[q_].t[:, :].rearrange("p (h e) -> p h e", h=4), ALU.add, r=[St, pS_[q_]], w=[St])
                    op("act", "activation", Stb.t[:], St.t[:], AF.Copy, r=[St], w=[Stb])
                    op("act", "activation", Osq.t[:L, :, :], Oo.t[:L, :, :], AF.Square, r=[Oo], w=[Osq])
                    op("dve", "tensor_reduce", S[:L, 80:96], Osq.t[:L, :, :], AX.X, ALU.add, r=[Osq], w=[sm_])
                    op("dve", "tensor_scalar", S[:L, 80:96], S[:L, 80:96], 1.0 / 128, EPS, ALU.mult, ALU.add, r=[sm_], w=[sm_])
                    op("act", "activation", S[:L, 80:96], S[:L, 80:96], AF.Sqrt, r=[sm_], w=[sm_])
                    op("dve", "reciprocal", S[:L, 80:96], S[:L, 80:96], r=[sm_], w=[sm_])
                    op("dve", "tensor_tensor", Oo.t[:L, :, :], Oo.t[:L, :, :],
                       S[:L, 80:96].unsqueeze(2).broadcast_to([L, 16, 128]), ALU.mult, r=[Oo, sm_], w=[Oo])
                    op("dve", "tensor_tensor", Oo.t[:L, :, :], Oo.t[:L, :, :],
                       gwn.t[:L, :].unsqueeze(1).broadcast_to([L, 16, 128]), ALU.mult, r=[Oo, gwn], w=[Oo])
                    op("dve", "tensor_tensor", Yb.t[:L, :], Oo.t[:L, :, :].rearrange("p h e -> p (h e)"), z_.t[:L, :], ALU.mult,
                       r=[Oo, z_], w=[Yb])
                    for q_ in range(2):
                        pb2 = psb_next()
                        for j in range(8):
                            jj = q_ * 8 + j
                            op("pe", "transpose", pb2.t[:, j * 64:j * 64 + L], Yb.t[:L, jj * 128:(jj + 1) * 128],
                               ident_b[:L, :L], r=[Yb, cstb], w=[pb2])
                        op("act", "activation", YT.t[:, q_ * 8:(q_ + 1) * 8, :L],
                           pb2.t[:, 0:512].rearrange("p (j t) -> p j t", j=8)[:, :, :L], AF.Copy, r=[pb2], w=[YT])
                    dma(mixT[2048:4096, t0:t0 + L].rearrange("(j p) t -> p j t", p=128), YT.t[:, :, :L], r=[YT],
                        w=[scr["mixT"]], append=True)

                op("dve", "memset", St.t[:], 0.0, w=[St])
                op("dve", "memset", Stb.t[:], 0.0, w=[Stb])
                for (t0, L) in chunks:
                    gdn_chunk(t0, L)
                dma(o_pS[l * 2048:(l + 1) * 2048, :].rearrange("(h d) e -> d h e", d=128), St.t[:], r=[St], w=[scr["out"]],
                    append=True)
                for s in range(NS):
                    row = l * NS + s
                    dma(St.t[:], sS_d[row * 2048:(row + 1) * 2048, :].rearrange("(h d) e -> d h e", d=128), w=[St])
                    op("act", "activation", Stb.t[:], St.t[:], AF.Copy, r=[St], w=[Stb])
                    gdn_chunk(TP + 8 * s, 8)
                    dma(o_sS[row * 2048:(row + 1) * 2048, :].rearrange("(h d) e -> d h e", d=128), St.t[:], r=[St],
                        w=[scr["out"]], append=True)

            def h_res(tag, c, m, ti, off, n, acc, acc_ap):
                xo_, so_, cur_t0 = tag
                i = hr[0] % 3
                hr[0] += 1
                tg0 = cur["t0"] + off
                kcb = c // 128
                dma(xo_[i].t[:m, :n], xT[c:c + m, tg0:tg0 + n], r=[XT_B[kcb]], w=[xo_[i]])
                op("dve", "tensor_tensor", so_[i].t[:m, :n], acc_ap, xo_[i].t[:m, :n], ALU.add, r=[acc, xo_[i]], w=[so_[i]])
                dma(xT[c:c + m, tg0:tg0 + n], so_[i].t[:m, :n], r=[so_[i]], w=[XT_B[kcb]], append=True)

            hr = [0]
            cur = {}
            with ExitStack() as ph:
                ph.callback(kb.barrier)
                actT = sb("actTc", [128, KC, 1024], BF16, st=ph)
                lb = make_lin_bufs(ph)
                xo_ = [sb("cxo%d" % i, [128, 512], st=ph) for i in range(3)]
                so_ = [sb("cso%d" % i, [128, 512], st=ph) for i in range(3)]
                segs = [(c0, 256, h_res, (xo_, so_, None)) for c0 in range(0, D, 256)]
                for (t0, n, ts) in tok_tiles(1024):
                    cur["t0"] = t0
                    dma(actT.t[:, :, :n], mixT[:, t0:t0 + n].rearrange("(k p) t -> p k t", p=128), r=[scr["mixT"]], w=[actT])
                    linear(lb, actT, KC, ts, w_out[l * D:(l + 1) * D, :], segs)

            with ExitStack() as ph:
                ph.callback(kb.barrier)
                actT = sb("actTd", [128, KC, 1024], BF16, st=ph)
                nb = make_norm_bufs(ph)
                lb = make_lin_bufs(ph)
                sg = {}
                for cb in range(2):
                    for ti in range(2):
                        sg[(cb, ti)] = sb("dsg%d%d" % (cb, ti), [128, 512], st=ph)
                ab = [sb("dab%d" % i, [128, 512], BF16, st=ph) for i in range(3)]
                an = [0]

                def h_ffn(tag, c, m, ti, off, n, acc, acc_ap):
                    kind, c0 = tag
                    cb = (c - c0) // 128
                    s_ = sg[(cb, ti)]
                    if kind == "g":
                        op("act", "activation", s_.t[:m, :n], acc_ap, AF.Silu, r=[acc], w=[s_])
                    else:
                        a_ = ab[an[0] % 3]
                        an[0] += 1
                        op("dve", "tensor_tensor", a_.t[:m, :n], acc_ap, s_.t[:m, :n], ALU.mult, r=[acc, s_], w=[a_])
                        tg0 = cur["t0"] + off
                        dma(aT[c:c + m, tg0:tg0 + n], a_.t[:m, :n], r=[a_], w=[scr["aT"]], append=True)

                for (t0, n, ts) in tok_tiles(1024):
                    cur["t0"] = t0
                    norm_to_act(nb, PP_NFFN + l * 32, t0, n, actT)
                    for c0 in range(0, DFF, 256):
                        cn = min(256, DFF - c0)
                        linear(lb, actT, KC, ts, w_gate[l * D:(l + 1) * D, :], [(c0, cn, h_ffn, ("g", c0))])
                        linear(lb, actT, KC, ts, w_up[l * D:(l + 1) * D, :], [(c0, cn, h_ffn, ("u", c0))])

            with ExitStack() as ph:
                ph.callback(kb.barrier)
                actT = sb("actTe", [128, KF, 512], BF16, st=ph)
                lb = make_lin_bufs(ph)
                xo_ = [sb("exo%d" % i, [128, 512], st=ph) for i in range(3)]
                so_ = [sb("eso%d" % i, [128, 512], st=ph) for i in range(3)]
                segs = [(c0, 256, h_res, (xo_, so_, None)) for c0 in range(0, D, 256)]
                for (t0, n, ts) in tok_tiles(512):
                    cur["t0"] = t0
                    dma(actT.t[:, :, :n], aT[:, t0:t0 + n].rearrange("(k p) t -> p k t", p=128), r=[scr["aT"]], w=[actT])
                    linear(lb, actT, KF, ts, w_down[l * DFF:(l + 1) * DFF, :], segs)

        with ExitStack() as ph:
            ph.callback(kb.barrier)
            nb = make_norm_bufs(ph)
            yT_ = sb("fyT", [128, KC, 256], st=ph)
            yo = [sb("fyo%d" % i, [128, D], st=ph) for i in range(2)]
            yn = [0]
            for t0 in range(0, T, 256):
                n = min(256, T - t0)

                def dst(kc, xt, wn_, rs, m):
                    op("dve", "scalar_tensor_tensor", yT_.t[:, kc, :m], xt.t[:, kc, :m], wn_, rs.t[:, :m], ALU.mult, ALU.mult,
                       r=[xt, rs, pp], w=[yT_])
                norm_piece(nb, PP_NFIN, t0, n, dst)
                for s0 in range(0, n, 128):
                    ns = min(128, n - s0)
                    o = yo[yn[0] % 2]
                    yn[0] += 1
                    for g in range(8):
                        pt = ps_next()
                        for j in range(4):
                            kc = g * 4 + j
                            op("pe", "transpose", pt.t[:ns, j * 128:(j + 1) * 128], yT_.t[:, kc, s0:s0 + ns], ident_f,
                               r=[yT_, cst], w=[pt])
                        if g % 2 == 0:
                            op("dve", "tensor_copy", o.t[:ns, g * 512:(g + 1) * 512], pt.t[:ns, :], r=[pt], w=[o])
                        else:
                            op("act", "activation", o.t[:ns, g * 512:(g + 1) * 512], pt.t[:ns, :], AF.Copy, r=[pt], w=[o])
                    dma(y_d[t0 + s0:t0 + s0 + ns, :], o.t[:ns, :], r=[o], w=[scr["y"]], append=True)
        kb.finish()
    return nc


def _pack_inputs(cfg, core, x_prompt, x_sample, st, meta_tokens, P):
    SEQ, NS, DEPTH, DFF = cfg["SEQ"], cfg["NS"], cfg["DEPTH"], cfg["DFF"]
    f = np.float32
    s_idx = core // 2 if cfg.get("pair", True) else core
    xs = x_sample[core * NS:(core + 1) * NS].reshape(NS * 8, D)
    xin = np.concatenate([meta_tokens, x_prompt[s_idx], xs], axis=0).astype(f)
    sl = slice(core * NS, (core + 1) * NS)
    m = {"xin": np.ascontiguousarray(xin)}
    m["sC"] = np.ascontiguousarray(st["C"][:, sl]).reshape(-1, 256)
    m["sn"] = np.ascontiguousarray(st["n"][:, sl].transpose(0, 1, 3, 2)).reshape(-1, 4)
    m["sm"] = np.ascontiguousarray(st["m"][:, sl]).reshape(1, -1)
    m["sh"] = np.ascontiguousarray(st["h"][:, sl].reshape(DEPTH, NS, 8, 128).transpose(0, 3, 2, 1)).reshape(DEPTH * 128, -1)
    m["src"] = np.ascontiguousarray(
        st["rc"][:, sl].reshape(DEPTH, NS, 3, 8, 128).transpose(0, 4, 3, 1, 2)).reshape(DEPTH * 128, -1)
    m["sS"] = np.ascontiguousarray(st["S"][:, sl]).reshape(-1, 128)
    m["sgc"] = np.ascontiguousarray(
        st["gc"][:, sl].reshape(DEPTH, NS, 3, 48, 128).transpose(0, 4, 3, 1, 2)).reshape(DEPTH * 128, -1)
    return m


def _pack_shared(cfg, P):
    DEPTH, DFF = cfg["DEPTH"], cfg["DFF"]
    f = np.float32

    def pc(v, nblk):
        return np.asarray(v, f).reshape(nblk, 128).T

    cols = [pc(P["norm_mix"][l], 32) for l in range(DEPTH)] + [pc(P["norm_ffn"][l], 32) for l in range(DEPTH)]
    cols.append(pc(P["norm_final"], 32))
    for l in range(DEPTH):
        cols.append(np.asarray(P["r_conv_w"][l], f).reshape(4, 8, 128).transpose(2, 1, 0).reshape(128, 32))
        cols.append(pc(P["r_conv_b"][l], 8))
        cols.append(pc(P["r_gate_a_b"][l], 8))
        cols.append(pc(P["r_gate_x_b"][l], 8))
        cols.append(pc(P["r_lambda"][l], 8))
        cols.append(np.asarray(P["g_conv_w"][l], f).reshape(4, 48, 128).transpose(2, 1, 0).reshape(128, 192))
    pp = np.ascontiguousarray(np.concatenate(cols, axis=1), f)
    pbs = []
    for l in range(DEPTH):
        pbs += [P["m_bias_i"][l], P["m_bias_f"][l], P["m_norm"][l], P["g_A_log"][l], P["g_dt_bias"][l], P["g_norm"][l]]
    pb = np.ascontiguousarray(np.concatenate([np.asarray(a, f).reshape(-1) for a in pbs])[None, :], f)
    rgw = np.stack([np.asarray(P["r_gate_a_w"], f), np.asarray(P["r_gate_x_w"], f)], axis=1)
    sh = {
        "w_in": np.asarray(P["w_in"], f).reshape(-1, DIN),
        "w_out": np.asarray(P["w_out"], f).reshape(-1, D),
        "w_gate": np.asarray(P["w_gate"], f).reshape(-1, DFF),
        "w_up": np.asarray(P["w_up"], f).reshape(-1, DFF),
        "w_down": np.asarray(P["w_down"], f).reshape(-1, D),
        "pp": pp, "pb": pb, "rgw": np.ascontiguousarray(rgw.reshape(-1, 128)), "consts": make_consts(),
    }
    return sh


def run(cfg, n_cores, x_prompt, x_sample, st, meta_tokens, P):
    SEQ, NS, DEPTH, DFF = cfg["SEQ"], cfg["NS"], cfg["DEPTH"], cfg["DFF"]
    nc = build(cfg)
    shared = _pack_shared(cfg, P)
    in_maps = []
    for c in range(n_cores):
        m = _pack_inputs(cfg, c, x_prompt, x_sample, st, meta_tokens, P)
        m.update(shared)
        in_maps.append(m)
    res = run_bass_kernel_spmd(nc, in_maps, core_ids=list(range(n_cores)))
    R = res.results
    TP = 16 + SEQ
    pair = cfg.get("pair", True)
    pcores = [2 * s for s in range(x_prompt.shape[0])] if pair else list(range(x_prompt.shape[0]))
    y_prompt = np.stack([R[c]["y"][16:TP] for c in pcores])
    y_sample = np.concatenate([R[c]["y"][TP:].reshape(NS, 8, D) for c in range(n_cores)])

    def pst(name, fn):
        return np.stack([fn(R[c][name]) for c in pcores], axis=1)

    def sst(name, fn):
        return np.concatenate([fn(R[c][name]) for c in range(n_cores)], axis=1)

    outs = [y_prompt, y_sample]
    outs.append(pst("o_pC", lambda a: a.reshape(DEPTH, 4, 128, 256)))
    outs.append(pst("o_pn", lambda a: a.reshape(DEPTH, 128, 4).transpose(0, 2, 1)))
    outs.append(pst("o_pm", lambda a: a.reshape(DEPTH, 4)))
    outs.append(pst("o_ph", lambda a: a.reshape(DEPTH, 128, 8).transpose(0, 2, 1).reshape(DEPTH, 1024)))
    outs.append(pst("o_prc", lambda a: a.reshape(DEPTH, 128, 8, 3).transpose(0, 3, 2, 1).reshape(DEPTH, 3, 1024)))
    outs.append(pst("o_pS", lambda a: a.reshape(DEPTH, 16, 128, 128)))
    outs.append(pst("o_pgc", lambda a: a.reshape(DEPTH, 128, 48, 3).transpose(0, 3, 2, 1).reshape(DEPTH, 3, 6144)))
    outs.append(sst("o_sC", lambda a: a.reshape(DEPTH, NS, 4, 128, 256)))
    outs.append(sst("o_sn", lambda a: a.reshape(DEPTH, NS, 128, 4).transpose(0, 1, 3, 2)))
    outs.append(sst("o_sm", lambda a: a.reshape(DEPTH, NS, 4)))
    outs.append(sst("o_sh", lambda a: a.reshape(DEPTH, 128, 8, NS).transpose(0, 3, 2, 1).reshape(DEPTH, NS, 1024)))
    outs.append(sst("o_src", lambda a: a.reshape(DEPTH, 128, 8, NS, 3).transpose(0, 3, 4, 2, 1).reshape(DEPTH, NS, 3, 1024)))
    outs.append(sst("o_sS", lambda a: a.reshape(DEPTH, NS, 16, 128, 128)))
    outs.append(sst("o_sgc", lambda a: a.reshape(DEPTH, 128, 48, NS, 3).transpose(0, 3, 4, 2, 1).reshape(DEPTH, NS, 3, 6144)))
    return tuple(np.ascontiguousarray(o, dtype=np.float32) for o in outs)


def kernel(x_prompt, x_sample, state_mlstm_C, state_mlstm_n, state_mlstm_m, state_rglru_h,
           state_rglru_conv, state_gdn_S, state_gdn_conv, meta_tokens, norm_mix, w_in,
           m_bias_i, m_bias_f, m_norm, r_conv_w, r_conv_b, r_gate_a_w, r_gate_a_b,
           r_gate_x_w, r_gate_x_b, r_lambda, g_conv_w, g_A_log, g_dt_bias, g_norm, w_out,
           norm_ffn, w_gate, w_up, w_down, norm_final):
    A = lambda a: np.asarray(a, np.float32)
    st = dict(C=A(state_mlstm_C), n=A(state_mlstm_n), m=A(state_mlstm_m), h=A(state_rglru_h),
              rc=A(state_rglru_conv), S=A(state_gdn_S), gc=A(state_gdn_conv))
    P = dict(norm_mix=A(norm_mix), w_in=w_in, m_bias_i=A(m_bias_i), m_bias_f=A(m_bias_f), m_norm=A(m_norm),
             r_conv_w=A(r_conv_w), r_conv_b=A(r_conv_b), r_gate_a_w=A(r_gate_a_w), r_gate_a_b=A(r_gate_a_b),
             r_gate_x_w=A(r_gate_x_w), r_gate_x_b=A(r_gate_x_b), r_lambda=A(r_lambda), g_conv_w=A(g_conv_w),
             g_A_log=A(g_A_log), g_dt_bias=A(g_dt_bias), g_norm=A(g_norm), w_out=w_out, norm_ffn=A(norm_ffn),
             w_gate=w_gate, w_up=w_up, w_down=w_down, norm_final=A(norm_final))
    return run(dict(FULL), 8, A(x_prompt), A(x_sample), st, A(meta_tokens), P)
```

```python
import numpy as np
from contextlib import ExitStack
import concourse.bass as bass
import concourse.mybir as mybir
from concourse.bass_utils import run_bass_kernel_spmd

F32 = mybir.dt.float32
BF16 = mybir.dt.bfloat16
AF = mybir.ActivationFunctionType
ALU = mybir.AluOpType
AX = mybir.AxisListType

D = 4096
DIN = 13352
KC = 32
EPS = 1e-6
NEG = -30000.0
FULL = dict(SEQ=2048, NS=16, DEPTH=2, DFF=11008)

C_ID, C_U, C_SL, C_NEGM, C_MUI, C_MUSN, C_MLSN, C_ONE, C_SEL8, C_SEL16, C_SEL64, C_END = [128 * i for i in range(12)]


def make_consts():
    c = np.zeros((128, C_END), np.float32)
    p = np.arange(128)[:, None]
    f = np.arange(128)[None, :]
    c[:, C_ID:C_ID + 128] = (p == f)
    c[:, C_U:C_U + 128] = (p <= f)
    c[:, C_SL:C_SL + 128] = (p > f)
    c[:, C_NEGM:C_NEGM + 128] = np.where(f <= p, 0.0, NEG)
    c[:, C_MUI:C_MUI + 128] = (p <= f)
    c[:, C_MUSN:C_MUSN + 128] = -1.0 * (p < f)
    c[:, C_MLSN:C_MLSN + 128] = -1.0 * (f < p)
    c[:, C_ONE:C_ONE + 128] = 1.0
    for off, L in ((C_SEL8, 8), (C_SEL16, 16), (C_SEL64, 64)):
        c[L - 1, off:off + 128] = 1.0
    return c


class Buf:
    __slots__ = ("w", "r", "t")

    def __init__(self, t=None):
        self.w = {}
        self.r = {}
        self.t = t


class KB:
    def __init__(self, nc, es, nd=24):
        self.nc = nc
        self.E = {"pe": nc.tensor, "act": nc.scalar, "dve": nc.vector, "pool": nc.gpsimd, "sp": nc.sync}
        self.H = {}
        self.cnt = {}
        for k in ("pe", "act", "dve", "pool"):
            self.H[k] = es.enter_context(nc.semaphore("s_" + k))
            self.cnt[k] = 0
        self.nd = nd
        self.dq = {}
        for q in ("sp", "pool", "act"):
            keys = []
            for i in range(nd if q == "sp" else 8):
                key = "d%s%d" % (q, i)
                self.H[key] = es.enter_context(nc.semaphore(key))
                keys.append(key)
            self.dq[q] = [keys, 0]
        self.dval = {}
        self.seen = {k: {} for k in self.E}

    def _need(self, r, w, skip_dma_waw=False):
        need = {}
        for b in r:
            for k, v in b.w.items():
                if need.get(k, 0) < v:
                    need[k] = v
        for b in w:
            for k, v in b.w.items():
                if skip_dma_waw and k[0] == "d" and k != "dve":
                    continue
                if need.get(k, 0) < v:
                    need[k] = v
            for k, v in b.r.items():
                if need.get(k, 0) < v:
                    need[k] = v
        return need

    def _wait(self, eng, need):
        seen = self.seen[eng]
        for k, v in need.items():
            if eng == "pe" and k == "pe":
                continue
            if seen.get(k, 0) < v:
                self.E[eng].wait_ge(self.H[k], v)
                seen[k] = v

    def op(self, eng, meth, *a, r=(), w=(), **kw):
        self._wait(eng, self._need(r, w))
        ins = getattr(self.E[eng], meth)(*a, **kw)
        self.cnt[eng] += 1
        c = self.cnt[eng]
        ins.then_inc(self.H[eng], 1)
        for b in r:
            b.r[eng] = c
        for b in w:
            b.w = {eng: c}
            b.r = {}
        return ins

    def dma(self, out, in_, r=(), w=(), q="sp", append=False, **kw):
        self._wait(q, self._need(r, w, skip_dma_waw=append))
        keys, n = self.dq[q]
        key = keys[n % len(keys)]
        self.dq[q][1] = n + 1
        prev = self.dval.get(key, 0)
        if prev and self.seen[q].get(key, 0) < prev:
            self.E[q].wait_ge(self.H[key], prev)
            self.seen[q][key] = prev
        val = prev + 16
        self.dval[key] = val
        ins = self.E[q].dma_start(out=out, in_=in_, **kw)
        ins.then_inc(self.H[key], 16)
        for b in r:
            b.r[key] = val
        for b in w:
            if append:
                b.w[key] = val
            else:
                b.w = {key: val}
            b.r = {}
        return ins

    def barrier(self):
        need = {k: v for k, v in self.dval.items()}
        for k in ("pe", "act", "dve", "pool"):
            if self.cnt[k]:
                need[k] = self.cnt[k]
        for eng in ("sp", "pe", "act", "dve", "pool"):
            self._wait(eng, {k: v for k, v in need.items() if k != eng})

    def finish(self):
        need = {k: v for k, v in self.dval.items()}
        for k in ("pe", "act", "dve", "pool"):
            if self.cnt[k]:
                need[k] = self.cnt[k]
        self._wait("sp", need)


def build(cfg):
    SEQ, NS, DEPTH, DFF = cfg["SEQ"], cfg["NS"], cfg["DEPTH"], cfg["DFF"]
    KF = DFF // 128
    TP = 16 + SEQ
    T = TP + NS * 8
    NE = 3 + TP + NS * 11
    SB0 = 3 + TP
    nc = bass.Bass("TRN2", target_bir_lowering=False)

    def din(name, shape):
        return nc.dram_tensor(name, list(shape), F32, kind="ExternalInput").ap()

    def dout(name, shape):
        return nc.dram_tensor(name, list(shape), F32, kind="ExternalOutput").ap()

    def dscr(name, shape, dt=F32):
        return nc.dram_tensor(name, list(shape), dt).ap()

    xin = din("xin", [T, D])
    w_in = din("w_in", [DEPTH * D, DIN])
    w_out = din("w_out", [DEPTH * D, D])
    w_gate = din("w_gate", [DEPTH * D, DFF])
    w_up = din("w_up", [DEPTH * D, DFF])
    w_down = din("w_down", [DEPTH * DFF, D])
    NPP = (2 * DEPTH + 1) * 32 + DEPTH * (8 * 4 + 8 * 4 + 48 * 4)
    pp_d = din("pp", [128, NPP])
    NPB = DEPTH * (8 + 1024 + 16 + 16 + 128)
    pb_d = din("pb", [1, NPB])
    rgw_d = din("rgw", [DEPTH * 2 * 8 * 128, 128])
    consts_d = din("consts", [128, C_END])
    sC_d = din("sC", [DEPTH * NS * 4 * 128, 256])
    sn_d = din("sn", [DEPTH * NS * 128, 4])
    sm_d = din("sm", [1, DEPTH * NS * 4])
    sh_d = din("sh", [DEPTH * 128, 8 * NS])
    src_d = din("src", [DEPTH * 128, 8 * NS * 3])
    sS_d = din("sS", [DEPTH * NS * 16 * 128, 128])
    sgc_d = din("sgc", [DEPTH * 128, 48 * NS * 3])

    y_d = dout("y", [T, D])
    o_pC = dout("o_pC", [DEPTH * 4 * 128, 256])
    o_pn = dout("o_pn", [DEPTH * 128, 4])
    o_pm = dout("o_pm", [DEPTH, 4])
    o_ph = dout("o_ph", [DEPTH * 128, 8])
    o_prc = dout("o_prc", [DEPTH * 128, 8 * 3])
    o_pS = dout("o_pS", [DEPTH * 16 * 128, 128])
    o_pgc = dout("o_pgc", [DEPTH * 128, 48 * 3])
    o_sC = dout("o_sC", [DEPTH * NS * 4 * 128, 256])
    o_sn = dout("o_sn", [DEPTH * NS * 128, 4])
    o_sm = dout("o_sm", [DEPTH * NS, 4])
    o_sh = dout("o_sh", [DEPTH * 128, 8 * NS])
    o_src = dout("o_src", [DEPTH * 128, 8 * NS * 3])
    o_sS = dout("o_sS", [DEPTH * NS * 16 * 128, 128])
    o_sgc = dout("o_sgc", [DEPTH * 128, 48 * NS * 3])

    xT = dscr("xT", [D, T])
    mqT = dscr("mqT", [512, T], BF16)
    mkT = dscr("mkT", [512, T], BF16)
    mk_tok = dscr("mk_tok", [T, 512], BF16)
    mv_tok = dscr("mv_tok", [T, 1024], BF16)
    mo_tok = dscr("mo_tok", [T, 1024], BF16)
    mg_tok = dscr("mg_tok", [T, 8])
    rxT = dscr("rxT", [1024, T])
    rgT = dscr("rgT", [1024, T])
    gqkvT = dscr("gqkvT", [6144, T])
    gz_tok = dscr("gz_tok", [T, 2048], BF16)
    gg_tok = dscr("gg_tok", [T, 48])
    gqnT = dscr("gqnT", [2048, T], BF16)
    gknT = dscr("gknT", [2048, T], BF16)
    gk_tok = dscr("gk_tok", [T, 2048], BF16)
    gv_tok = dscr("gv_tok", [T, 2048], BF16)
    mixT = dscr("mixT", [D, T], BF16)
    aT = dscr("aT", [DFF, T], BF16)

    PP_NMIX = 0
    PP_NFFN = DEPTH * 32
    PP_NFIN = 2 * DEPTH * 32
    PP_R = (2 * DEPTH + 1) * 32
    PPL = 8 * 4 + 8 * 4 + 48 * 4
    PBL = 8 + 1024 + 16 + 16 + 128

    with ExitStack() as es:
        es.enter_context(nc.allow_non_contiguous_dma(reason="small strided state / layout DMAs"))
        kb = KB(nc, es)
        op, dma = kb.op, kb.dma

        sbn = [0]

        def sb(name, shape, dt=F32, st=es):
            sbn[0] += 1
            t = st.enter_context(nc.sbuf_tensor("t%d_%s" % (sbn[0], name), list(shape), dt))
            return Buf(t)

        PS = [Buf(es.enter_context(nc.psum_tensor("ps%d" % i, [128, 512], F32))) for i in range(6)]
        PSB = [Buf(es.enter_context(nc.psum_tensor("psb%d" % i, [128, 1024], BF16))) for i in range(2)]
        psn = [0]
        psbn = [0]

        def ps_next():
            psn[0] += 1
            return PS[psn[0] % 6]

        def psb_next():
            psbn[0] += 1
            return PSB[psbn[0] % 2]

        cst = sb("cst", [128, C_END])
        cstb = sb("cstb", [128, 384], BF16)
        pp = sb("pp", [128, NPP])
        dma(cst.t[:], consts_d[:, :], w=[cst])
        dma(pp.t[:], pp_d[:, :], w=[pp])
        op("dve", "tensor_copy", cstb.t[:, 0:128], cst.t[:, C_ID:C_ID + 128], r=[cst], w=[cstb])
        op("dve", "tensor_copy", cstb.t[:, 128:256], cst.t[:, C_U:C_U + 128], r=[cst], w=[cstb])
        op("dve", "tensor_copy", cstb.t[:, 256:384], cst.t[:, C_ONE:C_ONE + 128], r=[cst], w=[cstb])
        ident_f = cst.t[:, C_ID:C_ID + 128]
        ident_b = cstb.t[:, 0:128]
        ones_b = cstb.t[:, 256:384]
        ones_f = cst.t[:, C_ONE:C_ONE + 128]
        Umat = cst.t[:, C_U:C_U + 128]

        XT_B = [Buf() for _ in range(KC)]
        scr = {n: Buf() for n in ("mqT", "mkT", "mk_tok", "mv_tok", "mo_tok", "mg_tok", "rxT", "rgT", "gqkvT",
                                  "gz_tok", "gg_tok", "gqnT", "gknT", "gk_tok", "gv_tok", "mixT", "aT", "y", "out")}

        subs = [(t0, min(128, T - t0)) for t0 in range(0, T, 128)]

        with ExitStack() as ph:
            ph.callback(kb.barrier)
            xs = [sb("x0s%d" % i, [128, D], st=ph) for i in range(2)]
            xo = [sb("x0o%d" % i, [128, KC, 128], st=ph) for i in range(2)]
            for si, (t0, n) in enumerate(subs):
                a, o = xs[si % 2], xo[si % 2]
                dma(a.t[:n, :], xin[t0:t0 + n, :], w=[a])
                for g in range(8):
                    pt = ps_next()
                    for j in range(4):
                        kc = g * 4 + j
                        op("pe", "transpose", pt.t[:, j * 128:j * 128 + n], a.t[:n, kc * 128:(kc + 1) * 128],
                           ident_f[:n, :n], r=[a, cst], w=[pt])
                    eng = "dve" if g % 2 == 0 else "act"
                    src = pt.t[:, :].rearrange("p (j t) -> p j t", j=4)[:, :, :n]
                    if eng == "dve":
                        op("dve", "tensor_copy", o.t[:, g * 4:g * 4 + 4, :n], src, r=[pt], w=[o])
                    else:
                        op("act", "activation", o.t[:, g * 4:g * 4 + 4, :n], src, AF.Copy, r=[pt], w=[o])
                dma(xT[:, t0:t0 + n].rearrange("(k p) t -> p k t", p=128), o.t[:, :, :n], r=[o], w=XT_B, append=True)

        def norm_piece(ph_bufs, wcol0, t0, n, dst_fn):
            xt, sq, rs = ph_bufs["xt"][ph_bufs["i"] % 2], ph_bufs["sq"], ph_bufs["rs"]
            ph_bufs["i"] += 1
            dma(xt.t[:, :, :n], xT[:, t0:t0 + n].rearrange("(k p) t -> p k t", p=128), r=XT_B, w=[xt])
            op("act", "activation", sq.t[:, :, :n], xt.t[:, :, :n], AF.Square, r=[xt], w=[sq])
            pt = ps_next()
            for kc in range(KC):
                op("pe", "matmul", pt.t[:, :n], ones_b, sq.t[:, kc, :n], start=(kc == 0), stop=(kc == KC - 1),
                   r=[sq, cstb], w=[pt])
            op("dve", "tensor_scalar", rs.t[:, :n], pt.t[:, :n], 1.0 / D, EPS, ALU.mult, ALU.add, r=[pt], w=[rs])
            op("act", "activation", rs.t[:, :n], rs.t[:, :n], AF.Sqrt, r=[rs], w=[rs])
            op("dve", "reciprocal", rs.t[:, :n], rs.t[:, :n], r=[rs], w=[rs])
            for kc in range(KC):
                dst_fn(kc, xt, pp.t[:, wcol0 + kc:wcol0 + kc + 1], rs, n)

        def make_norm_bufs(ph):
            return dict(xt=[sb("nxt%d" % i, [128, KC, 256], st=ph) for i in range(2)],
                        sq=sb("nsq", [128, KC, 256], BF16, st=ph), rs=sb("nrs", [128, 256], st=ph), i=0)

        def norm_to_act(nb, wcol0, t0, n, actT):
            for p0 in range(0, n, 256):
                pn = min(256, n - p0)

                def dst(kc, xt, wn, rs, m, p0=p0):
                    op("dve", "scalar_tensor_tensor", actT.t[:, kc, p0:p0 + m], xt.t[:, kc, :m], wn, rs.t[:, :m],
                       ALU.mult, ALU.mult, r=[xt, rs, pp], w=[actT])
                norm_piece(nb, wcol0, t0 + p0, pn, dst)

        def linear(ph, actT, KCn, ts_list, segs, PK=8):
            stg = ph["stg"]
            wbs = ph["wb"]
            nbuf = len(stg)
            npieces = (KCn + PK - 1) // PK
            plist = [(si, pi) for si in range(len(segs)) for pi in range(npieces)]
            base = ph["n"]
            issued = [0]

            def issue_upto(k):
                while issued[0] < min(k, len(plist)):
                    si_, pi_ = plist[issued[0]]
                    Wd, c0, ncol = segs[si_][0], segs[si_][1], segs[si_][2]
                    k0 = pi_ * PK
                    kn = min(PK, KCn - k0)
                    s_ = stg[(base + issued[0]) % nbuf]
                    dma(s_.t[:, :kn, :ncol],
                        Wd[k0 * 128:(k0 + kn) * 128, c0:c0 + ncol].rearrange("(k p) c -> p k c", p=128), w=[s_])
                    issued[0] += 1

            idx = 0
            for si, (Wd, c0, ncol, handler, tag) in enumerate(segs):
                ncb = (ncol + 127) // 128
                accs = {}
                for cb in range(ncb):
                    for ti in range(len(ts_list)):
                        accs[(cb, ti)] = ps_next()
                for pi in range(npieces):
                    k0 = pi * PK
                    kn = min(PK, KCn - k0)
                    issue_upto(idx + nbuf)
                    s_ = stg[(base + idx) % nbuf]
                    wb = wbs[(base + idx) % len(wbs)]
                    idx += 1
                    ph["n"] += 1
                    ce = ("dve", "act", "dve", "act", "pool", "dve", "act")[ph["n"] % 7]
                    if ce == "act":
                        op("act", "activation", wb.t[:, :kn, :ncol], s_.t[:, :kn, :ncol], AF.Copy, r=[s_], w=[wb])
                    else:
                        op(ce, "tensor_copy", wb.t[:, :kn, :ncol], s_.t[:, :kn, :ncol], r=[s_], w=[wb])
                    for cb in range(ncb):
                        m = min(128, ncol - cb * 128)
                        for ti, (off, n) in enumerate(ts_list):
                            acc = accs[(cb, ti)]
                            for j in range(kn):
                                kc = k0 + j
                                op("pe", "matmul", acc.t[:m, :n], wb.t[:, j, cb * 128:cb * 128 + m],
                                   actT.t[:, kc, off:off + n], start=(kc == 0), stop=(kc == KCn - 1),
                                   r=[wb, actT], w=[acc])
                for cb in range(ncb):
                    m = min(128, ncol - cb * 128)
                    for ti, (off, n) in enumerate(ts_list):
                        acc = accs[(cb, ti)]
                        handler(tag, c0 + cb * 128, m, ti, off, n, acc, acc.t[:m, :n])

        def make_lin_bufs(ph):
            return dict(stg=[sb("lstg%d" % i, [128, 8, 256], st=ph) for i in range(3)],
                        wb=[sb("lwb%d" % i, [128, 8, 256], BF16, st=ph) for i in range(3)], n=0)

        def tok_tiles(maxn):
            res = []
            t0 = 0
            while t0 < T:
                n = min(maxn, T - t0)
                ts = [(o, min(512, n - o)) for o in range(0, n, 512)]
                res.append((t0, n, ts))
                t0 += n
            return res

        for l in range(DEPTH):
            ppl = PP_R + l * PPL
            pbl = l * PBL
            with ExitStack() as ph:
                ph.callback(kb.barrier)
                actT = sb("actT", [128, KC, 1024], BF16, st=ph)
                nb = make_norm_bufs(ph)
                lb = make_lin_bufs(ph)
                stf = [sb("stf%d" % i, [128, 512], st=ph) for i in range(3)]
                stb = [sb("stb%d" % i, [128, 512], BF16, st=ph) for i in range(3)]
                tmb = [sb("tmb%d" % i, [128, 4, 128], BF16, st=ph) for i in range(3)]
                tmf = [sb("tmf%d" % i, [128, 48], st=ph) for i in range(3)]
                gbias = sb("gbias", [128, 8], st=ph)
                gdt = sb("gdt", [128, 32], st=ph)
                dma(gbias.t[:], pb_d[0:1, pbl:pbl + 8].partition_broadcast(128), w=[gbias])
                dma(gdt.t[:], pb_d[0:1, pbl + 8 + 1024:pbl + 8 + 1024 + 32].partition_broadcast(128), w=[gdt])
                op("act", "activation", gdt.t[:, 0:16], gdt.t[:, 0:16], AF.Exp, r=[gdt], w=[gdt])
                op("dve", "tensor_scalar", gdt.t[:, 0:16], gdt.t[:, 0:16], -1.0, None, ALU.mult, r=[gdt], w=[gdt])
                rr = [0]

                def h_in(tag, c, m, ti, off, n, acc, acc_ap, tt0=None):
                    kind, dst, r0 = tag
                    i = rr[0] % 3
                    rr[0] += 1
                    tg0 = cur["t0"] + off
                    row = c - r0
                    if kind in ("f32",):
                        s_ = stf[i]
                        if i % 2 == 0:
                            op("dve", "tensor_copy", s_.t[:m, :n], acc_ap, r=[acc], w=[s_])
                        else:
                            op("act", "activation", s_.t[:m, :n], acc_ap, AF.Copy, r=[acc], w=[s_])
                        dma(dst[0][row:row + m, tg0:tg0 + n], s_.t[:m, :n], r=[s_], w=[scr[dst[1]]], append=True)
                        return
                    if kind == "gate":
                        s_ = stf[i]
                        op("dve", "tensor_copy", s_.t[:m, :n], acc_ap, r=[acc], w=[s_])
                        for s0 in range(0, n, 128):
                            ns = min(128, n - s0)
                            pt = ps_next()
                            op("pe", "transpose", pt.t[:ns, :m], s_.t[:m, s0:s0 + ns], ident_f[:m, :m], r=[s_, cst], w=[pt])
                            tf = tmf[rr[0] % 3]
                            rr[0] += 1
                            if dst[1] == "mg_tok":
                                op("dve", "tensor_tensor", tf.t[:ns, 0:8], pt.t[:ns, 0:8], gbias.t[:ns, :], ALU.add,
                                   r=[pt, gbias], w=[tf])
                                op("act", "activation", tf.t[:ns, 8:12], tf.t[:ns, 4:8], AF.Exp, scale=-1.0, r=[tf], w=[tf])
                                op("act", "activation", tf.t[:ns, 8:12], tf.t[:ns, 8:12], AF.Ln, bias=1.0, r=[tf], w=[tf])
                                op("dve", "tensor_scalar", tf.t[:ns, 4:8], tf.t[:ns, 8:12], -1.0, None, ALU.mult, r=[tf], w=[tf])
                                dma(mg_tok[tg0 + s0:tg0 + s0 + ns, :], tf.t[:ns, 0:8], r=[tf], w=[scr["mg_tok"]], append=True)
                            else:
                                op("act", "activation", tf.t[:ns, 32:48], pt.t[:ns, 0:16], AF.Exp, scale=-1.0, r=[pt], w=[tf])
                                op("dve", "tensor_scalar", tf.t[:ns, 32:48], tf.t[:ns, 32:48], 1.0, None, ALU.add, r=[tf], w=[tf])
                                op("dve", "reciprocal", tf.t[:ns, 0:16], tf.t[:ns, 32:48], r=[tf], w=[tf])
                                op("act", "activation", tf.t[:ns, 16:32], tf.t[:ns, 32:48], AF.Ln, r=[tf], w=[tf])
                                op("dve", "tensor_scalar", tf.t[:ns, 16:32], tf.t[:ns, 16:32], -1.0, None, ALU.mult, r=[tf], w=[tf])
                                op("dve", "tensor_tensor", tf.t[:ns, 32:48], pt.t[:ns, 16:32], gdt.t[:ns, 16:32], ALU.add,
                                   r=[pt, gdt], w=[tf])
                                op("act", "activation", tf.t[:ns, 32:48], tf.t[:ns, 32:48], AF.Exp, r=[tf], w=[tf])
                                op("act", "activation", tf.t[:ns, 32:48], tf.t[:ns, 32:48], AF.Ln, bias=1.0, r=[tf], w=[tf])
                                op("dve", "tensor_tensor", tf.t[:ns, 32:48], tf.t[:ns, 32:48], gdt.t[:ns, 0:16], ALU.mult,
                                   r=[tf, gdt], w=[tf])
                                dma(gg_tok[tg0 + s0:tg0 + s0 + ns, :], tf.t[:ns, 0:48], r=[tf], w=[scr["gg_tok"]], append=True)
                        return
                    s_ = stb[i]
                    if kind == "mq":
                        op("act", "activation", s_.t[:m, :n], acc_ap, AF.Copy, scale=128.0 ** -0.5, r=[acc], w=[s_])
                    elif kind == "sig":
                        op("act", "activation", s_.t[:m, :n], acc_ap, AF.Sigmoid, r=[acc], w=[s_])
                    elif kind == "silu":
                        op("act", "activation", s_.t[:m, :n], acc_ap, AF.Silu, r=[acc], w=[s_])
                    else:
                        op("dve", "tensor_copy", s_.t[:m, :n], acc_ap, r=[acc], w=[s_])
                    fm, tm = dst
                    if fm is not None:
                        dma(fm[0][row:row + m, tg0:tg0 + n], s_.t[:m, :n], r=[s_], w=[scr[fm[1]]], append=True)
                    if tm is not None:
                        pb_ = psb_next()
                        tb = tmb[rr[0] % 3]
                        rr[0] += 1
                        nsub = (n + 127) // 128
                        for s in range(nsub):
                            ns = min(128, n - s * 128)
                            op("pe", "transpose", pb_.t[:ns, s * 128:s * 128 + m], s_.t[:m, s * 128:s * 128 + ns],
                               ident_b[:m, :m], r=[s_, cstb], w=[pb_])
                        nfull = n // 128
                        if nfull:
                            op("dve", "tensor_copy", tb.t[:, :nfull, :m],
                               pb_.t[:, 0:nfull * 128].rearrange("p (s c) -> p s c", s=nfull)[:, :, :m], r=[pb_], w=[tb])
                            dma(tm[0][tg0:tg0 + nfull * 128, row:row + m].rearrange("(s p) c -> p s c", p=128),
                                tb.t[:, :nfull, :m], r=[tb], w=[scr[tm[1]]], append=True)
                        if n % 128:
                            ns = n % 128
                            op("dve", "tensor_copy", tb.t[:ns, nfull, :m], pb_.t[:ns, nfull * 128:nfull * 128 + m],
                               r=[pb_], w=[tb])
                            dma(tm[0][tg0 + nfull * 128:tg0 + n, row:row + m], tb.t[:ns, nfull, :m], r=[tb],
                                w=[scr[tm[1]]], append=True)

                groups = [
                    (0, 512, "mq", ((mqT, "mqT"), None)),
                    (512, 512, "cp", ((mkT, "mkT"), (mk_tok, "mk_tok"))),
                    (1024, 1024, "cp", (None, (mv_tok, "mv_tok"))),
                    (2048, 1024, "sig", (None, (mo_tok, "mo_tok"))),
                    (3072, 8, "gate", (None, "mg_tok")),
                    (3080, 1024, "f32", (rxT, "rxT")),
                    (4104, 1024, "f32", (rgT, "rgT")),
                    (5128, 6144, "f32", (gqkvT, "gqkvT")),
                    (11272, 2048, "silu", (None, (gz_tok, "gz_tok"))),
                    (13320, 32, "gate", (None, "gg_tok")),
                ]
                segs = []
                for (g0, gn, kind, dst) in groups:
                    for c0 in range(g0, g0 + gn, 256):
                        segs.append((w_in[l * D:(l + 1) * D, :], c0, min(256, g0 + gn - c0), h_in, (kind, dst, g0)))
                cur = {}
                for (t0, n, ts) in tok_tiles(1024):
                    cur["t0"] = t0
                    norm_to_act(nb, PP_NMIX + l * 32, t0, n, actT)
                    linear(lb, actT, KC, ts, segs)

            with ExitStack() as ph:
                ph.callback(kb.barrier)
                gw = sb("rgw", [128, 16, 128], st=ph)
                gwb = sb("rgwb", [128, 16, 128], BF16, st=ph)
                dma(gw.t[:], rgw_d[l * 2048:(l + 1) * 2048, :].rearrange("(g d) e -> d g e", d=128), w=[gw])
                op("dve", "tensor_copy", gwb.t[:], gw.t[:], r=[gw], w=[gwb])
                nsp = sb("nsp", [128, 8], st=ph)
                lam = pp.t[:, ppl + 56:ppl + 64]
                op("act", "activation", nsp.t[:], lam, AF.Exp, scale=-1.0, r=[pp], w=[nsp])
                op("act", "activation", nsp.t[:], nsp.t[:], AF.Ln, bias=1.0, r=[nsp], w=[nsp])
                op("dve", "tensor_scalar", nsp.t[:], nsp.t[:], -8.0, None, ALU.mult, r=[nsp], w=[nsp])
                XE = sb("rXE", [128, NE], st=ph)
                XC = sb("rXC", [128, NE], st=ph)
                XCb = sb("rXCb", [128, NE], BF16, st=ph)
                RG = sb("rRG", [128, NE], st=ph)
                IG = sb("rIG", [128, NE], st=ph)
                AA = sb("rAA", [128, NE], st=ph)
                HH = sb("rHH", [128, NE], st=ph)
                GG = sb("rGG", [128, T], st=ph)
                G2 = sb("rG2", [128, T], st=ph)
                YY = sb("rYY", [128, T], BF16, st=ph)
                h0 = sb("rh0", [128, 8 * NS], st=ph)
                hl = sb("rhl", [128, 8, 1 + NS], st=ph)
                cvo = sb("rcvo", [128, 8, 3 + 3 * NS], st=ph)
                dma(h0.t[:], sh_d[l * 128:(l + 1) * 128, :], w=[h0])
                op("dve", "memset", XE.t[:, 0:3], 0.0, w=[XE])
                NV = NE - 3
                for b in range(8):
                    dma(XE.t[:, 3:3 + TP], rxT[b * 128:(b + 1) * 128, 0:TP], r=[scr["rxT"]], w=[XE])
                    xes = XE.t[:, SB0:NE].rearrange("p (s j) -> p s j", j=11)
                    dma(xes[:, :, 3:11], rxT[b * 128:(b + 1) * 128, TP:T].rearrange("p (s j) -> p s j", j=8),
                        r=[scr["rxT"]], w=[XE], append=True)
                    dma(xes[:, :, 0:3],
                        src_d[l * 128:(l + 1) * 128, b * NS * 3:(b + 1) * NS * 3].rearrange("p (s j) -> p s j", j=3),
                        w=[XE], append=True)
                    dma(GG.t[:], rgT[b * 128:(b + 1) * 128, :], r=[scr["rgT"]], w=[GG])
                    cw = ppl + b * 4
                    op("dve", "tensor_scalar", XC.t[:, 0:NV], XE.t[:, 3:NE], pp.t[:, cw + 3:cw + 4],
                       pp.t[:, ppl + 32 + b:ppl + 33 + b], ALU.mult, ALU.add, r=[XE, pp], w=[XC])
                    for k in range(3):
                        op("dve", "scalar_tensor_tensor", XC.t[:, 0:NV], XE.t[:, k:k + NV], pp.t[:, cw + k:cw + k + 1],
                           XC.t[:, 0:NV], ALU.mult, ALU.add, r=[XE, pp, XC], w=[XC])
                    op("pool", "tensor_copy", XCb.t[:, 0:NV], XC.t[:, 0:NV], r=[XC], w=[XCb])
                    for c0 in range(0, NV, 512):
                        cn = min(512, NV - c0)
                        p1 = ps_next()
                        op("pe", "matmul", p1.t[:, :cn], gwb.t[:, b, :], XCb.t[:, c0:c0 + cn], start=True, stop=True,
                           r=[gwb, XCb], w=[p1])
                        op("act", "activation", RG.t[:, c0:c0 + cn], p1.t[:, :cn], AF.Sigmoid,
                           bias=pp.t[:, ppl + 40 + b:ppl + 41 + b], r=[p1, pp], w=[RG])
                        p2 = ps_next()
                        op("pe", "matmul", p2.t[:, :cn], gwb.t[:, 8 + b, :], XCb.t[:, c0:c0 + cn], start=True, stop=True,
                           r=[gwb, XCb], w=[p2])
                        op("act", "activation", IG.t[:, c0:c0 + cn], p2.t[:, :cn], AF.Sigmoid,
                           bias=pp.t[:, ppl + 48 + b:ppl + 49 + b], r=[p2, pp], w=[IG])
                    op("act", "activation", AA.t[:, 0:NV], RG.t[:, 0:NV], AF.Exp, scale=nsp.t[:, b:b + 1], r=[RG, nsp], w=[AA])
                    op("dve", "tensor_tensor", RG.t[:, 0:NV], AA.t[:, 0:NV], AA.t[:, 0:NV], ALU.mult, r=[AA], w=[RG])
                    op("dve", "tensor_scalar", RG.t[:, 0:NV], RG.t[:, 0:NV], -1.0, 1.0, ALU.mult, ALU.add, r=[RG], w=[RG])
                    op("dve", "tensor_scalar", RG.t[:, 0:NV], RG.t[:, 0:NV], 0.0, None, ALU.max, r=[RG], w=[RG])
                    op("act", "activation", RG.t[:, 0:NV], RG.t[:, 0:NV], AF.Sqrt, r=[RG], w=[RG])
                    op("dve", "tensor_tensor", IG.t[:, 0:NV], IG.t[:, 0:NV], XC.t[:, 0:NV], ALU.mult, r=[IG, XC], w=[IG])
                    op("dve", "tensor_tensor", IG.t[:, 0:NV], IG.t[:, 0:NV], RG.t[:, 0:NV], ALU.mult, r=[IG, RG], w=[IG])
                    op("dve", "tensor_tensor_scan", HH.t[:, 0:TP], AA.t[:, 0:TP], IG.t[:, 0:TP], 0.0, ALU.mult, ALU.add,
                       r=[AA, IG], w=[HH])
                    for s in range(NS):
                        e0 = SB0 + 11 * s
                        op("dve", "tensor_tensor_scan", HH.t[:, e0:e0 + 8], AA.t[:, e0:e0 + 8], IG.t[:, e0:e0 + 8],
                           h0.t[:, b * NS + s:b * NS + s + 1], ALU.mult, ALU.add, r=[AA, IG, h0], w=[HH])
                    op("dve", "tensor_tensor", G2.t[:], GG.t[:], GG.t[:], ALU.mult, r=[GG], w=[G2])
                    op("dve", "tensor_scalar", G2.t[:], G2.t[:], 0.044715, 1.0, ALU.mult, ALU.add, r=[G2], w=[G2])
                    op("dve", "tensor_tensor", G2.t[:], G2.t[:], GG.t[:], ALU.mult, r=[G2, GG], w=[G2])
                    op("act", "activation", G2.t[:], G2.t[:], AF.Sigmoid, scale=1.5957691216057308, r=[G2], w=[G2])
                    op("dve", "tensor_tensor", G2.t[:], G2.t[:], GG.t[:], ALU.mult, r=[G2, GG], w=[G2])
                    op("dve", "tensor_tensor", YY.t[:, 0:TP], HH.t[:, 0:TP], G2.t[:, 0:TP], ALU.mult, r=[HH, G2], w=[YY])
                    hs = HH.t[:, SB0:SB0 + 11 * NS].rearrange("p (s j) -> p s j", j=11)
                    op("dve", "tensor_tensor", YY.t[:, TP:T].rearrange("p (s j) -> p s j", j=8), hs[:, :, 0:8],
                       G2.t[:, TP:T].rearrange("p (s j) -> p s j", j=8), ALU.mult, r=[HH, G2], w=[YY])
                    dma(mixT[1024 + b * 128:1024 + (b + 1) * 128, :], YY.t[:], r=[YY], w=[scr["mixT"]], append=True)
                    op("act", "activation", hl.t[:, b, 0:1], HH.t[:, TP - 1:TP], AF.Copy, r=[HH], w=[hl])
                    op("act", "activation", hl.t[:, b, 1:1 + NS], hs[:, :, 7], AF.Copy, r=[HH], w=[hl])
                    op("act", "activation", cvo.t[:, b, 0:3], XE.t[:, TP:TP + 3], AF.Copy, r=[XE], w=[cvo])
                    op("act", "activation", cvo.t[:, b, 3:3 + 3 * NS].rearrange("p (s j) -> p s j", j=3), xes[:, :, 8:11],
                       AF.Copy, r=[XE], w=[cvo])
                dma(o_ph[l * 128:(l + 1) * 128, :], hl.t[:, :, 0], r=[hl], w=[scr["out"]], append=True)
                dma(o_sh[l * 128:(l + 1) * 128, :].rearrange("p (b s) -> p b s", b=8), hl.t[:, :, 1:1 + NS], r=[hl],
                    w=[scr["out"]], append=True)
                dma(o_prc[l * 128:(l + 1) * 128, :].rearrange("p (b j) -> p b j", b=8), cvo.t[:, :, 0:3], r=[cvo],
                    w=[scr["out"]], append=True)
                dma(o_src[l * 128:(l + 1) * 128, :].rearrange("p (b j) -> p b j", b=8), cvo.t[:, :, 3:3 + 3 * NS], r=[cvo],
                    w=[scr["out"]], append=True)

            with ExitStack() as ph:
                ph.callback(kb.barrier)
                XE2 = [sb("gXE%d" % i, [128, NE], st=ph) for i in range(2)]
                XC2 = [sb("gXC%d" % i, [128, NE], st=ph) for i in range(2)]
                XS2 = [sb("gXS%d" % i, [128, T], st=ph) for i in range(2)]
                SQ2 = [sb("gSQ%d" % i, [128, T], st=ph) for i in range(2)]
                XN2 = [sb("gXN%d" % i, [128, T], BF16, st=ph) for i in range(2)]
                cvo = sb("gcvo", [128, 48, 3 + 3 * NS], st=ph)
                tmb = [sb("gtmb%d" % i, [128, 4, 128], BF16, st=ph) for i in range(4)]
                for XE in XE2:
                    op("dve", "memset", XE.t[:, 0:3], 0.0, w=[XE])
                NV = NE - 3
                def g_load(b):
                    XE = XE2[b % 2]
                    dma(XE.t[:, 3:3 + TP], gqkvT[b * 128:(b + 1) * 128, 0:TP], r=[scr["gqkvT"]], w=[XE])
                    xes = XE.t[:, SB0:NE].rearrange("p (s j) -> p s j", j=11)
                    dma(xes[:, :, 3:11], gqkvT[b * 128:(b + 1) * 128, TP:T].rearrange("p (s j) -> p s j", j=8),
                        r=[scr["gqkvT"]], w=[XE], append=True)
                    dma(xes[:, :, 0:3],
                        sgc_d[l * 128:(l + 1) * 128, b * NS * 3:(b + 1) * NS * 3].rearrange("p (s j) -> p s j", j=3),
                        w=[XE], append=True)

                g_load(0)
                for b in range(48):
                    XE, XC, XS, SQ, XN = XE2[b % 2], XC2[b % 2], XS2[b % 2], SQ2[b % 2], XN2[b % 2]
                    if b + 1 < 48:
                        g_load(b + 1)
                    xes = XE.t[:, SB0:NE].rearrange("p (s j) -> p s j", j=11)
                    cw = ppl + 64 + b * 4
                    op("dve", "tensor_scalar", XC.t[:, 0:NV], XE.t[:, 3:NE], pp.t[:, cw + 3:cw + 4], None, ALU.mult,
                       r=[XE, pp], w=[XC])
                    for k in range(3):
                        op("dve", "scalar_tensor_tensor", XC.t[:, 0:NV], XE.t[:, k:k + NV], pp.t[:, cw + k:cw + k + 1],
                           XC.t[:, 0:NV], ALU.mult, ALU.add, r=[XE, pp, XC], w=[XC])
                    op("act", "activation", XS.t[:, 0:TP], XC.t[:, 0:TP], AF.Silu, r=[XC], w=[XS])
                    xcs = XC.t[:, SB0:SB0 + 11 * NS].rearrange("p (s j) -> p s j", j=11)
                    op("act", "activation", XS.t[:, TP:T].rearrange("p (s j) -> p s j", j=8), xcs[:, :, 0:8], AF.Silu,
                       r=[XC], w=[XS])
                    op("act", "activation", cvo.t[:, b, 0:3], XE.t[:, TP:TP + 3], AF.Copy, r=[XE], w=[cvo])
                    op("act", "activation", cvo.t[:, b, 3:3 + 3 * NS].rearrange("p (s j) -> p s j", j=3), xes[:, :, 8:11],
                       AF.Copy, r=[XE], w=[cvo])
                    if b < 32:
                        op("dve", "tensor_tensor", SQ.t[:], XS.t[:], XS.t[:], ALU.mult, r=[XS], w=[SQ])
                        for c0 in range(0, T, 512):
                            cn = min(512, T - c0)
                            p1 = ps_next()
                            op("pe", "matmul", p1.t[:, :cn], ones_f, SQ.t[:, c0:c0 + cn], start=True, stop=True,
                               r=[SQ, cst], w=[p1])
                            op("dve", "tensor_scalar", SQ.t[:, c0:c0 + cn], p1.t[:, :cn], EPS, None, ALU.add, r=[p1], w=[SQ])
                        op("act", "activation", SQ.t[:], SQ.t[:], AF.Sqrt, r=[SQ], w=[SQ])
                        op("dve", "reciprocal", SQ.t[:], SQ.t[:], r=[SQ], w=[SQ])
                        if b < 16:
                            op("dve", "scalar_tensor_tensor", XN.t[:], XS.t[:], 128.0 ** -0.5, SQ.t[:], ALU.mult, ALU.mult,
                               r=[XS, SQ], w=[XN])
                        else:
                            op("dve", "tensor_tensor", XN.t[:], XS.t[:], SQ.t[:], ALU.mult, r=[XS, SQ], w=[XN])
                    else:
                        op("pool", "tensor_copy", XN.t[:], XS.t[:], r=[XS], w=[XN])
                    if b < 16:
                        dma(gqnT[b * 128:(b + 1) * 128, :], XN.t[:], r=[XN], w=[scr["gqnT"]], append=True)
                    elif b < 32:
                        dma(gknT[(b - 16) * 128:(b - 15) * 128, :], XN.t[:], r=[XN], w=[scr["gknT"]], append=True)
                    if b >= 16:
                        dst, dn = (gk_tok, "gk_tok") if b < 32 else (gv_tok, "gv_tok")
                        hcol = (b - 16) * 128 if b < 32 else (b - 32) * 128
                        for g0 in range(0, T, 512):
                            gn = min(512, T - g0)
                            pb_ = psb_next()
                            tb = tmb[(g0 // 512) % 4]
                            nsub = (gn + 127) // 128
                            for s in range(nsub):
                                ns = min(128, gn - s * 128)
                                op("pe", "transpose", pb_.t[:ns, s * 128:(s + 1) * 128],
                                   XN.t[:, g0 + s * 128:g0 + s * 128 + ns], ident_b, r=[XN, cstb], w=[pb_])
                            nfull = gn // 128
                            if nfull:
                                op("act", "activation", tb.t[:, :nfull, :],
                                   pb_.t[:, 0:nfull * 128].rearrange("p (s c) -> p s c", s=nfull), AF.Copy, r=[pb_], w=[tb])
                                dma(dst[g0:g0 + nfull * 128, hcol:hcol + 128].rearrange("(s p) c -> p s c", p=128),
                                    tb.t[:, :nfull, :], r=[tb], w=[scr[dn]], append=True)
                            if gn % 128:
                                ns = gn % 128
                                op("act", "activation", tb.t[:ns, nfull, :], pb_.t[:ns, nfull * 128:(nfull + 1) * 128],
                                   AF.Copy, r=[pb_], w=[tb])
                                dma(dst[g0 + nfull * 128:g0 + gn, hcol:hcol + 128], tb.t[:ns, nfull, :], r=[tb],
                                    w=[scr[dn]], append=True)
                dma(o_pgc[l * 128:(l + 1) * 128, :].rearrange("p (b j) -> p b j", b=48), cvo.t[:, :, 0:3], r=[cvo],
                    w=[scr["out"]], append=True)
                dma(o_sgc[l * 128:(l + 1) * 128, :].rearrange("p (b j) -> p b j", b=48), cvo.t[:, :, 3:3 + 3 * NS],
                    r=[cvo], w=[scr["out"]], append=True)

            chunks = [(0, 16)] + [(16 + 64 * j, 64) for j in range(SEQ // 64)]

            with ExitStack() as ph:
                ph.callback(kb.barrier)
                Cx = sb("mCx", [128, 4, 256], st=ph)
                Cn = sb("mCn", [128, 4], st=ph)
                Cxb = sb("mCxb", [128, 4, 256], BF16, st=ph)
                Cnb = sb("mCnb", [128, 4], BF16, st=ph)
                mbc = sb("mmbc", [128, 4], st=ph)
                wn = sb("mwn", [128, 1024], st=ph)
                dma(wn.t[:], pb_d[0:1, pbl + 8:pbl + 8 + 1024].partition_broadcast(128), w=[wn])
                IN = [dict(qT=sb("mqT%d" % i, [128, 4, 64], BF16, st=ph), kT=sb("mkT%d" % i, [128, 4, 64], BF16, st=ph),
                           k=sb("mk%d" % i, [64, 4, 128], BF16, st=ph), v=sb("mv%d" % i, [64, 4, 256], BF16, st=ph),
                           o=sb("mo%d" % i, [64, 1024], BF16, st=ph), g=sb("mg%d" % i, [64, 8], st=ph)) for i in range(2)]
                sm_ = sb("msm", [128, 64], st=ph)
                Et = sb("mE", [64, 4, 64], st=ph)
                Ds = sb("mDs", [64, 4, 64], st=ph)
                Pm = sb("mPm", [64, 4, 64], st=ph)
                Sb_ = sb("mSb", [64, 4, 64], BF16, st=ph)
                STb = sb("mSTb", [64, 4, 64], BF16, st=ph)
                A2 = sb("mA2", [64, 256], st=ph)
                A2n = sb("mA2n", [64, 4], st=ph)
                Hh = sb("mHh", [64, 4, 256], st=ph)
                Hsq = sb("mHsq", [64, 256], st=ph)
                Yb = sb("mYb", [64, 1024], BF16, st=ph)
                YT = sb("mYT", [128, 8, 64], BF16, st=ph)
                kw_ = sb("mkw", [64, 4, 128], BF16, st=ph)
                onesv = sb("mones", [64, 1], BF16, st=ph)
                op("dve", "memset", onesv.t[:], 1.0, w=[onesv])
                nn = [0]

                def mlstm_chunk(t0, L):
                    I_ = IN[nn[0] % 2]
                    nn[0] += 1
                    qT_, kT_, k_, v_, o_, g_ = I_["qT"], I_["kT"], I_["k"], I_["v"], I_["o"], I_["g"]
                    dma(qT_.t[:, :, :L], mqT[:, t0:t0 + L].rearrange("(h d) t -> d h t", d=128), r=[scr["mqT"]], w=[qT_])
                    dma(kT_.t[:, :, :L], mkT[:, t0:t0 + L].rearrange("(h d) t -> d h t", d=128), r=[scr["mkT"]], w=[kT_])
                    dma(k_.t[:L, :, :], mk_tok[t0:t0 + L, :].rearrange("t (h d) -> t h d", d=128), r=[scr["mk_tok"]], w=[k_])
                    dma(v_.t[:L, :, :], mv_tok[t0:t0 + L, :].rearrange("t (h d) -> t h d", d=256), r=[scr["mv_tok"]], w=[v_])
                    dma(o_.t[:L, :], mo_tok[t0:t0 + L, :], r=[scr["mo_tok"]], w=[o_])
                    dma(g_.t[:L, :], mg_tok[t0:t0 + L, :], r=[scr["mg_tok"]], w=[g_])
                    li = g_.t[:L, 0:4]
                    lf = g_.t[:L, 4:8]
                    sel = cst.t[:L, {8: C_SEL8, 16: C_SEL16, 64: C_SEL64}[L]:][:, 0:128]
                    S = sm_.t
                    p1 = ps_next()
                    op("pe", "matmul", p1.t[:L, 0:4], Umat[:L, :L], lf, start=True, stop=True, r=[cst, g_], w=[p1])
                    op("pe", "matmul", p1.t[:, 8:12], ones_f[:L, :], lf, start=True, stop=True, r=[cst, g_], w=[p1])
                    op("dve", "tensor_copy", S[:L, 0:4], p1.t[:L, 0:4], r=[p1], w=[sm_])
                    op("dve", "tensor_copy", S[:, 4:8], p1.t[:, 8:12], r=[p1], w=[sm_])
                    SLb = cst.t[:L, C_SL:C_SL + L].unsqueeze(1).broadcast_to([L, 4, L])
                    Ib = cst.t[:L, C_ID:C_ID + L].unsqueeze(1).broadcast_to([L, 4, L])
                    op("dve", "tensor_tensor", Et.t[:L, :, :L], SLb, lf.unsqueeze(2).broadcast_to([L, 4, L]), ALU.mult,
                       r=[cst, g_], w=[Et])
                    op("dve", "tensor_tensor", Ds.t[:L, :, :L], Ib, li.unsqueeze(2).broadcast_to([L, 4, L]), ALU.mult,
                       r=[cst, g_], w=[Ds])
                    op("dve", "tensor_tensor", Et.t[:L, :, :L], Et.t[:L, :, :L], Ds.t[:L, :, :L], ALU.add, r=[Et, Ds], w=[Et])
                    p2 = ps_next()
                    for h in range(4):
                        op("pe", "matmul", p2.t[:L, h * 64:h * 64 + L], Umat[:L, :L], Et.t[:L, h, :L], start=True, stop=True,
                           r=[cst, Et], w=[p2])
                    negm = cst.t[:L, C_NEGM:C_NEGM + L].unsqueeze(1).broadcast_to([L, 4, L])
                    p2v = p2.t[:L, 0:256].rearrange("p (h s) -> p h s", h=4)[:, :, :L]
                    op("dve", "tensor_tensor", Ds.t[:L, :, :L], p2v, negm, ALU.add, r=[p2, cst], w=[Ds])
                    op("dve", "tensor_reduce", S[:L, 8:12], Ds.t[:L, :, :L], AX.X, ALU.max, r=[Ds], w=[sm_])
                    op("dve", "tensor_tensor", S[:L, 12:16], mbc.t[:L, :], S[:L, 0:4], ALU.add, r=[mbc, sm_], w=[sm_])
                    op("dve", "tensor_tensor", S[:L, 16:20], S[:L, 8:12], S[:L, 12:16], ALU.max, r=[sm_], w=[sm_])
                    op("dve", "tensor_scalar", S[:L, 20:24], S[:L, 16:20], -1.0, None, ALU.mult, r=[sm_], w=[sm_])
                    for h in range(4):
                        op("act", "activation", Pm.t[:L, h, :L], Ds.t[:L, h, :L], AF.Exp, bias=S[:L, 20 + h:21 + h],
                           r=[Ds, sm_], w=[Pm])
                    op("dve", "tensor_tensor", S[:L, 24:28], S[:L, 12:16], S[:L, 16:20], ALU.subtract, r=[sm_], w=[sm_])
                    op("act", "activation", S[:L, 24:28], S[:L, 24:28], AF.Exp, r=[sm_], w=[sm_])
                    op("act", "activation", S[:L, 28:32], S[:L, 20:24], AF.Exp, r=[sm_], w=[sm_])
                    p3 = ps_next()
                    for h in range(4):
                        op("pe", "matmul", p3.t[:L, h * 64:h * 64 + L], qT_.t[:, h, :L], kT_.t[:, h, :L], start=True,
                           stop=True, r=[qT_, kT_], w=[p3])
                    p3v = p3.t[:L, 0:256].rearrange("p (h s) -> p h s", h=4)[:, :, :L]
                    op("dve", "tensor_tensor", Sb_.t[:L, :, :L], p3v, Pm.t[:L, :, :L], ALU.mult, r=[p3, Pm], w=[Sb_])
                    pb_ = psb_next()
                    for h in range(4):
                        op("pe", "transpose", pb_.t[:L, h * 64:h * 64 + L], Sb_.t[:L, h, :L], ident_b[:L, :L],
                           r=[Sb_, cstb], w=[pb_])
                    op("act", "activation", STb.t[:L, :, :L], pb_.t[:L, 0:256].rearrange("p (h s) -> p h s", h=4)[:, :, :L],
                       AF.Copy, r=[pb_], w=[STb])
                    p4 = ps_next()
                    op("pe", "matmul", p4.t[:, 0:4], sel, S[:L, 16:20], start=True, stop=True, r=[cst, sm_], w=[p4])
                    op("dve", "tensor_copy", S[:, 32:36], p4.t[:, 0:4], r=[p4], w=[sm_])
                    op("dve", "tensor_tensor", S[:L, 36:40], S[:L, 4:8], S[:L, 0:4], ALU.subtract, r=[sm_], w=[sm_])
                    op("dve", "tensor_tensor", S[:L, 36:40], S[:L, 36:40], li, ALU.add, r=[sm_, g_], w=[sm_])
                    op("dve", "tensor_tensor", S[:L, 36:40], S[:L, 36:40], S[:L, 32:36], ALU.subtract, r=[sm_], w=[sm_])
                    op("act", "activation", S[:L, 36:40], S[:L, 36:40], AF.Exp, r=[sm_], w=[sm_])
                    op("dve", "tensor_tensor", S[:, 40:44], mbc.t[:, :], S[:, 4:8], ALU.add, r=[mbc, sm_], w=[sm_])
                    op("dve", "tensor_tensor", S[:, 40:44], S[:, 40:44], S[:, 32:36], ALU.subtract, r=[sm_], w=[sm_])
                    op("act", "activation", S[:, 40:44], S[:, 40:44], AF.Exp, r=[sm_], w=[sm_])
                    for h in range(4):
                        pa = ps_next()
                        op("pe", "matmul", pa.t[:L, 0:256], qT_.t[:, h, :L], Cxb.t[:, h, :], start=True, stop=True,
                           r=[qT_, Cxb], w=[pa])
                        op("pe", "matmul", pa.t[:L, 256:257], qT_.t[:, h, :L], Cnb.t[:, h:h + 1], start=True, stop=True,
                           r=[qT_, Cnb], w=[pa])
                        pc = ps_next()
                        op("pe", "matmul", pc.t[:L, 0:256], STb.t[:L, h, :L], v_.t[:L, h, :], start=True, stop=True,
                           r=[STb, v_], w=[pc])
                        op("pe", "matmul", pc.t[:L, 256:257], STb.t[:L, h, :L], onesv.t[:L, :], start=True, stop=True,
                           r=[STb, onesv], w=[pc])
                        op("act", "activation", A2.t[:L, :], pc.t[:L, 0:256], AF.Copy, r=[pc], w=[A2])
                        op("act", "activation", A2n.t[:L, h:h + 1], pc.t[:L, 256:257], AF.Copy, r=[pc], w=[A2n])
                        op("dve", "scalar_tensor_tensor", Hh.t[:L, h, :], pa.t[:L, 0:256], S[:L, 24 + h:25 + h], A2.t[:L, :],
                           ALU.mult, ALU.add, r=[pa, sm_, A2], w=[Hh])
                        op("dve", "scalar_tensor_tensor", S[:L, 44 + h:45 + h], pa.t[:L, 256:257], S[:L, 24 + h:25 + h],
                           A2n.t[:L, h:h + 1], ALU.mult, ALU.add, r=[pa, sm_, A2n], w=[sm_])
                    op("act", "activation", S[:L, 44:48], S[:L, 44:48], AF.Abs, r=[sm_], w=[sm_])
                    op("dve", "tensor_tensor", S[:L, 44:48], S[:L, 44:48], S[:L, 28:32], ALU.max, r=[sm_], w=[sm_])
                    op("dve", "reciprocal", S[:L, 44:48], S[:L, 44:48], r=[sm_], w=[sm_])
                    for h in range(4):
                        op("dve", "tensor_scalar", Hh.t[:L, h, :], Hh.t[:L, h, :], S[:L, 44 + h:45 + h], None, ALU.mult,
                           r=[Hh, sm_], w=[Hh])
                        op("act", "activation", Hsq.t[:L, :], Hh.t[:L, h, :], AF.Square, accum_out=S[:L, 48 + h:49 + h],
                           r=[Hh], w=[Hsq, sm_])
                    op("dve", "tensor_scalar", S[:L, 48:52], S[:L, 48:52], 1.0 / 256, EPS, ALU.mult, ALU.add, r=[sm_], w=[sm_])
                    op("act", "activation", S[:L, 48:52], S[:L, 48:52], AF.Sqrt, r=[sm_], w=[sm_])
                    op("dve", "reciprocal", S[:L, 48:52], S[:L, 48:52], r=[sm_], w=[sm_])
                    for h in range(4):
                        op("dve", "scalar_tensor_tensor", Hh.t[:L, h, :], Hh.t[:L, h, :], S[:L, 48 + h:49 + h],
                           wn.t[:L, h * 256:(h + 1) * 256], ALU.mult, ALU.mult, r=[Hh, sm_, wn], w=[Hh])
                    op("dve", "tensor_tensor", Yb.t[:L, :], Hh.t[:L, :, :].rearrange("p h e -> p (h e)"), o_.t[:L, :],
                       ALU.mult, r=[Hh, o_], w=[Yb])
                    pb2 = psb_next()
                    for j in range(8):
                        op("pe", "transpose", pb2.t[:, j * 64:j * 64 + L], Yb.t[:L, j * 128:(j + 1) * 128], ident_b[:L, :L],
                           r=[Yb, cstb], w=[pb2])
                    op("act", "activation", YT.t[:, :, :L], pb2.t[:, 0:512].rearrange("p (j t) -> p j t", j=8)[:, :, :L],
                       AF.Copy, r=[pb2], w=[YT])
                    dma(mixT[0:1024, t0:t0 + L].rearrange("(j p) t -> p j t", p=128), YT.t[:, :, :L], r=[YT],
                        w=[scr["mixT"]], append=True)
                    op("dve", "tensor_tensor", kw_.t[:L, :, :], k_.t[:L, :, :],
                       S[:L, 36:40].unsqueeze(2).broadcast_to([L, 4, 128]), ALU.mult, r=[k_, sm_], w=[kw_])
                    for h in range(4):
                        pd = ps_next()
                        op("pe", "matmul", pd.t[:, 0:256], kw_.t[:L, h, :], v_.t[:L, h, :], start=True, stop=True,
                           r=[kw_, v_], w=[pd])
                        op("pe", "matmul", pd.t[:, 256:257], kw_.t[:L, h, :], onesv.t[:L, :], start=True, stop=True,
                           r=[kw_, onesv], w=[pd])
                        op("dve", "scalar_tensor_tensor", Cx.t[:, h, :], Cx.t[:, h, :], S[:, 40 + h:41 + h], pd.t[:, 0:256],
                           ALU.mult, ALU.add, r=[Cx, sm_, pd], w=[Cx])
                        op("dve", "scalar_tensor_tensor", Cn.t[:, h:h + 1], Cn.t[:, h:h + 1], S[:, 40 + h:41 + h],
                           pd.t[:, 256:257], ALU.mult, ALU.add, r=[Cn, sm_, pd], w=[Cn])
                    op("act", "activation", Cxb.t[:], Cx.t[:], AF.Copy, r=[Cx], w=[Cxb])
                    op("act", "activation", Cnb.t[:], Cn.t[:], AF.Copy, r=[Cn], w=[Cnb])
                    op("dve", "tensor_copy", mbc.t[:, :], S[:, 32:36], r=[sm_], w=[mbc])

                def m_store(oC, on, om, row):
                    dma(oC[row * 512:(row + 1) * 512, :].rearrange("(h k) v -> k h v", k=128), Cx.t[:], r=[Cx],
                        w=[scr["out"]], append=True)
                    dma(on[row * 128:(row + 1) * 128, :], Cn.t[:], r=[Cn], w=[scr["out"]], append=True)
                    dma(om[row:row + 1, :], mbc.t[0:1, :], r=[mbc], w=[scr["out"]], append=True)

                op("dve", "memset", Cx.t[:], 0.0, w=[Cx])
                op("dve", "memset", Cn.t[:], 0.0, w=[Cn])
                op("dve", "memset", mbc.t[:], 0.0, w=[mbc])
                op("dve", "memset", Cxb.t[:], 0.0, w=[Cxb])
                op("dve", "memset", Cnb.t[:], 0.0, w=[Cnb])
                for (t0, L) in chunks:
                    mlstm_chunk(t0, L)
                m_store(o_pC, o_pn, o_pm, l)
                for s in range(NS):
                    row = l * NS + s
                    dma(Cx.t[:], sC_d[row * 512:(row + 1) * 512, :].rearrange("(h k) v -> k h v", k=128), w=[Cx])
                    dma(Cn.t[:], sn_d[row * 128:(row + 1) * 128, :], w=[Cn])
                    dma(mbc.t[:], sm_d[0:1, row * 4:row * 4 + 4].partition_broadcast(128), w=[mbc])
                    op("act", "activation", Cxb.t[:], Cx.t[:], AF.Copy, r=[Cx], w=[Cxb])
                    op("act", "activation", Cnb.t[:], Cn.t[:], AF.Copy, r=[Cn], w=[Cnb])
                    mlstm_chunk(TP + 8 * s, 8)
                    m_store(o_sC, o_sn, o_sm, row)

            with ExitStack() as ph:
                ph.callback(kb.barrier)
                St = sb("gS", [128, 16, 128], st=ph)
                Stb = sb("gSb", [128, 16, 128], BF16, st=ph)
                gwn = sb("ggwn", [128, 128], st=ph)
                dma(gwn.t[:], pb_d[0:1, pbl + 8 + 1024 + 32:pbl + 8 + 1024 + 32 + 128].partition_broadcast(128), w=[gwn])
                IN = [dict(qT=sb("gqT%d" % i, [128, 16, 64], BF16, st=ph), kT=sb("gkT%d" % i, [128, 16, 64], BF16, st=ph),
                           k=sb("gk%d" % i, [64, 16, 128], BF16, st=ph), v=sb("gv%d" % i, [64, 16, 128], BF16, st=ph),
                           z=sb("gz%d" % i, [64, 2048], BF16, st=ph), g=sb("gg%d" % i, [64, 48], st=ph)) for i in range(2)]
                sm_ = sb("gsm", [128, 128], st=ph)
                Et = sb("gE", [64, 16, 64], st=ph)
                E2 = sb("gE2", [64, 16, 64], st=ph)
                xB = sb("gxB", [64, 8, 64], st=ph)
                xC = sb("gxC", [64, 8, 64], st=ph)
                Bk = [sb("gBk%d" % i, [64, 8, 64], BF16, st=ph) for i in range(2)]
                Ck = [sb("gCk%d" % i, [64, 8, 64], BF16, st=ph) for i in range(2)]
                Qf = sb("gQf", [64, 8, 64], st=ph)
                Qb = sb("gQb", [64, 8, 64], BF16, st=ph)
                AT = sb("gAT", [64, 8, 64], BF16, st=ph)
                R0w = sb("gR0w", [64, 8, 128], BF16, st=ph)
                YwT = sb("gYwT", [128, 8, 64], BF16, st=ph)
                vn = sb("gvn", [64, 8, 128], BF16, st=ph)
                O2 = sb("gO2", [64, 8, 128], st=ph)
                Oo = sb("gOo", [64, 16, 128], st=ph)
                Osq = sb("gOsq", [64, 16, 128], st=ph)
                Yb = sb("gYb", [64, 2048], BF16, st=ph)
                YT = sb("gYT", [128, 16, 64], BF16, st=ph)
                kd = sb("gkd", [64, 8, 128], BF16, st=ph)
                nn = [0]

                def gdn_chunk(t0, L):
                    I_ = IN[nn[0] % 2]
                    nn[0] += 1
                    qT_, kT_, k_, v_, z_, g_ = I_["qT"], I_["kT"], I_["k"], I_["v"], I_["z"], I_["g"]
                    dma(qT_.t[:, :, :L], gqnT[:, t0:t0 + L].rearrange("(h d) t -> d h t", d=128), r=[scr["gqnT"]], w=[qT_])
                    dma(kT_.t[:, :, :L], gknT[:, t0:t0 + L].rearrange("(h d) t -> d h t", d=128), r=[scr["gknT"]], w=[kT_])
                    dma(k_.t[:L, :, :], gk_tok[t0:t0 + L, :].rearrange("t (h d) -> t h d", d=128), r=[scr["gk_tok"]], w=[k_])
                    dma(v_.t[:L, :, :], gv_tok[t0:t0 + L, :].rearrange("t (h d) -> t h d", d=128), r=[scr["gv_tok"]], w=[v_])
                    dma(z_.t[:L, :], gz_tok[t0:t0 + L, :], r=[scr["gz_tok"]], w=[z_])
                    dma(g_.t[:L, :], gg_tok[t0:t0 + L, :], r=[scr["gg_tok"]], w=[g_])
                    beta = g_.t[:L, 0:16]
                    lnb = g_.t[:L, 16:32]
                    gg = g_.t[:L, 32:48]
                    S = sm_.t
                    p1 = ps_next()
                    op("pe", "matmul", p1.t[:L, 0:16], Umat[:L, :L], gg, start=True, stop=True, r=[cst, g_], w=[p1])
                    op("pe", "matmul", p1.t[:, 16:32], ones_f[:L, :], gg, start=True, stop=True, r=[cst, g_], w=[p1])
                    op("dve", "tensor_copy", S[:L, 0:16], p1.t[:L, 0:16], r=[p1], w=[sm_])
                    op("dve", "tensor_copy", S[:, 16:32], p1.t[:, 16:32], r=[p1], w=[sm_])
                    op("act", "activation", S[:L, 32:48], S[:L, 0:16], AF.Exp, r=[sm_], w=[sm_])
                    op("act", "activation", S[:, 48:64], S[:, 16:32], AF.Exp, r=[sm_], w=[sm_])
                    op("dve", "tensor_tensor", S[:L, 64:80], S[:L, 16:32], S[:L, 0:16], ALU.subtract, r=[sm_], w=[sm_])
                    op("act", "activation", S[:L, 64:80], S[:L, 64:80], AF.Exp, r=[sm_], w=[sm_])
                    op("dve", "tensor_tensor", S[:L, 64:80], S[:L, 64:80], beta, ALU.mult, r=[sm_, g_], w=[sm_])
                    SLb = cst.t[:L, C_SL:C_SL + L].unsqueeze(1).broadcast_to([L, 16, L])
                    Ib = cst.t[:L, C_ID:C_ID + L].unsqueeze(1).broadcast_to([L, 16, L])
                    op("dve", "tensor_tensor", Et.t[:L, :, :L], SLb, gg.unsqueeze(2).broadcast_to([L, 16, L]), ALU.mult,
                       r=[cst, g_], w=[Et])
                    op("dve", "tensor_tensor", E2.t[:L, :, :L], Ib, lnb.unsqueeze(2).broadcast_to([L, 16, L]), ALU.mult,
                       r=[cst, g_], w=[E2])
                    op("dve", "tensor_tensor", Et.t[:L, :, :L], Et.t[:L, :, :L], E2.t[:L, :, :L], ALU.add, r=[Et, E2], w=[Et])
                    mls = cst.t[:L, C_MLSN:C_MLSN + L].unsqueeze(1).broadcast_to([L, 8, L])
                    mus = cst.t[:L, C_MUSN:C_MUSN + L].unsqueeze(1).broadcast_to([L, 8, L])
                    mui = cst.t[:L, C_MUI:C_MUI + L].unsqueeze(1).broadcast_to([L, 8, L])
                    idb = cst.t[:L, C_ID:C_ID + L].unsqueeze(1).broadcast_to([L, 8, L])

                    def v8(pt):
                        return pt.t[:L, 0:512].rearrange("p (h s) -> p h s", h=8)[:, :, :L]

                    for hg in range(2):
                        h0_ = hg * 8
                        pB = ps_next()
                        for j in range(8):
                            op("pe", "matmul", pB.t[:L, j * 64:j * 64 + L], Umat[:L, :L], Et.t[:L, h0_ + j, :L], start=True,
                               stop=True, r=[cst, Et], w=[pB])
                        op("act", "activation", xB.t[:L, :, :L], v8(pB), AF.Exp, r=[pB], w=[xB])
                        pC = ps_next()
                        for j in range(8):
                            op("pe", "matmul", pC.t[:L, j * 64:j * 64 + L], Et.t[:L, h0_ + j, :L], Umat[:L, :L], start=True,
                               stop=True, r=[cst, Et], w=[pC])
                        op("act", "activation", xC.t[:L, :, :L], v8(pC), AF.Exp, r=[pC], w=[xC])
                        pM = ps_next()
                        for j in range(8):
                            op("pe", "matmul", pM.t[:L, j * 64:j * 64 + L], kT_.t[:, h0_ + j, :L], kT_.t[:, h0_ + j, :L],
                               start=True, stop=True, r=[kT_], w=[pM])
                        pK = ps_next()
                        for j in range(8):
                            op("pe", "matmul", pK.t[:L, j * 64:j * 64 + L], kT_.t[:, h0_ + j, :L], qT_.t[:, h0_ + j, :L],
                               start=True, stop=True, r=[kT_, qT_], w=[pK])
                        op("dve", "tensor_tensor", xB.t[:L, :, :L], v8(pM), xB.t[:L, :, :L], ALU.mult, r=[pM, xB], w=[xB])
                        op("dve", "tensor_tensor", Bk[0].t[:L, :, :L], xB.t[:L, :, :L], mls, ALU.mult, r=[xB, cst], w=[Bk[0]])
                        op("dve", "tensor_tensor", Qf.t[:L, :, :L], v8(pM), xC.t[:L, :, :L], ALU.mult, r=[pM, xC], w=[Qf])
                        op("dve", "tensor_tensor", Ck[0].t[:L, :, :L], Qf.t[:L, :, :L], mus, ALU.mult, r=[Qf, cst], w=[Ck[0]])
                        op("dve", "tensor_tensor", xC.t[:L, :, :L], v8(pK), xC.t[:L, :, :L], ALU.mult, r=[pK, xC], w=[xC])
                        op("dve", "tensor_tensor", AT.t[:L, :, :L], xC.t[:L, :, :L], mui, ALU.mult, r=[xC, cst], w=[AT])
                        op("dve", "tensor_tensor", Qf.t[:L, :, :L], Qf.t[:L, :, :L], mus, ALU.mult, r=[Qf, cst], w=[Qf])
                        op("dve", "tensor_tensor", Qf.t[:L, :, :L], Qf.t[:L, :, :L], idb, ALU.add, r=[Qf, cst], w=[Qf])
                        op("act", "activation", Qb.t[:L, :, :L], Qf.t[:L, :, :L], AF.Copy, r=[Qf], w=[Qb])
                        mlev = 1
                        cur_ = 0
                        while 2 * mlev < L:
                            Bc, Cc, Bn, Cn_ = Bk[cur_], Ck[cur_], Bk[1 - cur_], Ck[1 - cur_]
                            pP = ps_next()
                            for j in range(8):
                                op("pe", "matmul", pP.t[:L, j * 64:j * 64 + L], Bc.t[:L, j, :L], Cc.t[:L, j, :L], start=True,
                                   stop=True, r=[Bc, Cc], w=[pP])
                            pQ = ps_next()
                            for j in range(8):
                                op("pe", "matmul", pQ.t[:L, j * 64:j * 64 + L], Cc.t[:L, j, :L], Bc.t[:L, j, :L], start=True,
                                   stop=True, r=[Bc, Cc], w=[pQ])
                            op("act", "activation", Cn_.t[:L, :, :L], v8(pP), AF.Copy, r=[pP], w=[Cn_])
                            op("dve", "tensor_copy", Bn.t[:L, :, :L], v8(pQ), r=[pQ], w=[Bn])
                            cur_ = 1 - cur_
                            mlev *= 2
                            pR = ps_next()
                            for j in range(8):
                                op("pe", "matmul", pR.t[:L, j * 64:j * 64 + L], Bk[cur_].t[:L, j, :L], Qb.t[:L, j, :L],
                                   start=True, stop=True, r=[Bk[cur_], Qb], w=[pR])
                            op("dve", "tensor_tensor", Qf.t[:L, :, :L], Qf.t[:L, :, :L], v8(pR), ALU.add, r=[Qf, pR], w=[Qf])
                            op("act", "activation", Qb.t[:L, :, :L], Qf.t[:L, :, :L], AF.Copy, r=[Qf], w=[Qb])
                        op("dve", "tensor_tensor", R0w.t[:L, :, :], k_.t[:L, h0_:h0_ + 8, :],
                           S[:L, 32 + h0_:40 + h0_].unsqueeze(2).broadcast_to([L, 8, 128]), ALU.mult, r=[k_, sm_], w=[R0w])
                        pY = ps_next()
                        for j in range(8):
                            op("pe", "matmul", pY.t[:, j * 64:j * 64 + L], R0w.t[:L, j, :], Qb.t[:L, j, :L], start=True,
                               stop=True, r=[R0w, Qb], w=[pY])
                        op("dve", "tensor_scalar", YwT.t[:, :, :L], pY.t[:, 0:512].rearrange("p (h s) -> p h s", h=8)[:, :, :L],
                           -1.0, None, ALU.mult, r=[pY], w=[YwT])
                        pv = [ps_next(), ps_next()]
                        for j in range(8):
                            dst_ = pv[j // 4].t[:L, (j % 4) * 128:(j % 4 + 1) * 128]
                            op("pe", "matmul", dst_, Qb.t[:L, j, :L], v_.t[:L, h0_ + j, :], start=True, stop=False,
                               r=[Qb, v_], w=[pv[j // 4]])
                            op("pe", "matmul", dst_, YwT.t[:, j, :L], Stb.t[:, h0_ + j, :], start=False, stop=True,
                               r=[YwT, Stb], w=[pv[j // 4]])
                        for q_ in range(2):
                            op("act", "activation", vn.t[:L, q_ * 4:(q_ + 1) * 4, :],
                               pv[q_].t[:L, :].rearrange("p (h e) -> p h e", h=4), AF.Copy, r=[pv[q_]], w=[vn])
                        po1 = [ps_next(), ps_next()]
                        for j in range(8):
                            op("pe", "matmul", po1[j // 4].t[:L, (j % 4) * 128:(j % 4 + 1) * 128], AT.t[:L, j, :L],
                               vn.t[:L, j, :], start=True, stop=True, r=[AT, vn], w=[po1[j // 4]])
                        for q_ in range(2):
                            op("act", "activation", O2.t[:L, q_ * 4:(q_ + 1) * 4, :],
                               po1[q_].t[:L, :].rearrange("p (h e) -> p h e", h=4), AF.Copy, r=[po1[q_]], w=[O2])
                        po2 = [ps_next(), ps_next()]
                        for j in range(8):
                            op("pe", "matmul", po2[j // 4].t[:L, (j % 4) * 128:(j % 4 + 1) * 128], qT_.t[:, h0_ + j, :L],
                               Stb.t[:, h0_ + j, :], start=True, stop=True, r=[qT_, Stb], w=[po2[j // 4]])
                        for q_ in range(2):
                            op("dve", "tensor_tensor", Oo.t[:L, h0_ + q_ * 4:h0_ + q_ * 4 + 4, :],
                               po2[q_].t[:L, :].rearrange("p (h e) -> p h e", h=4),
                               S[:L, 32 + h0_ + q_ * 4:32 + h0_ + q_ * 4 + 4].unsqueeze(2).broadcast_to([L, 4, 128]),
                               ALU.mult, r=[po2[q_], sm_], w=[Oo])
                        op("dve", "tensor_tensor", Oo.t[:L, h0_:h0_ + 8, :], Oo.t[:L, h0_:h0_ + 8, :], O2.t[:L, :, :], ALU.add,
                           r=[Oo, O2], w=[Oo])
                        op("dve", "tensor_tensor", kd.t[:L, :, :], k_.t[:L, h0_:h0_ + 8, :],
                           S[:L, 64 + h0_:72 + h0_].unsqueeze(2).broadcast_to([L, 8, 128]), ALU.mult, r=[k_, sm_], w=[kd])
                        pS_ = [ps_next(), ps_next()]
                        for j in range(8):
                            op("pe", "matmul", pS_[j // 4].t[:, (j % 4) * 128:(j % 4 + 1) * 128], kd.t[:L, j, :], vn.t[:L, j, :],
                               start=True, stop=True, r=[kd, vn], w=[pS_[j // 4]])
                        for q_ in range(2):
                            hs_ = h0_ + q_ * 4
                            op("dve", "tensor_tensor", St.t[:, hs_:hs_ + 4, :], St.t[:, hs_:hs_ + 4, :],
                               S[:, 48 + hs_:52 + hs_].unsqueeze(2).broadcast_to([128, 4, 128]), ALU.mult, r=[St, sm_], w=[St])
                            op("dve", "tensor_tensor", St.t[:, hs_:hs_ + 4, :], St.t[:, hs_:hs_ + 4, :],
                               pS_[q_].t[:, :].rearrange("p (h e) -> p h e", h=4), ALU.add, r=[St, pS_[q_]], w=[St])
                    op("act", "activation", Stb.t[:], St.t[:], AF.Copy, r=[St], w=[Stb])
                    op("act", "activation", Osq.t[:L, :, :], Oo.t[:L, :, :], AF.Square, r=[Oo], w=[Osq])
                    op("dve", "tensor_reduce", S[:L, 80:96], Osq.t[:L, :, :], AX.X, ALU.add, r=[Osq], w=[sm_])
                    op("dve", "tensor_scalar", S[:L, 80:96], S[:L, 80:96], 1.0 / 128, EPS, ALU.mult, ALU.add, r=[sm_], w=[sm_])
                    op("act", "activation", S[:L, 80:96], S[:L, 80:96], AF.Sqrt, r=[sm_], w=[sm_])
                    op("dve", "reciprocal", S[:L, 80:96], S[:L, 80:96], r=[sm_], w=[sm_])
                    op("dve", "tensor_tensor", Oo.t[:L, :, :], Oo.t[:L, :, :],
                       S[:L, 80:96].unsqueeze(2).broadcast_to([L, 16, 128]), ALU.mult, r=[Oo, sm_], w=[Oo])
                    op("dve", "tensor_tensor", Oo.t[:L, :, :], Oo.t[:L, :, :],
                       gwn.t[:L, :].unsqueeze(1).broadcast_to([L, 16, 128]), ALU.mult, r=[Oo, gwn], w=[Oo])
                    op("dve", "tensor_tensor", Yb.t[:L, :], Oo.t[:L, :, :].rearrange("p h e -> p (h e)"), z_.t[:L, :], ALU.mult,
                       r=[Oo, z_], w=[Yb])
                    for q_ in range(2):
                        pb2 = psb_next()
                        for j in range(8):
                            jj = q_ * 8 + j
                            op("pe", "transpose", pb2.t[:, j * 64:j * 64 + L], Yb.t[:L, jj * 128:(jj + 1) * 128],
                               ident_b[:L, :L], r=[Yb, cstb], w=[pb2])
                        op("act", "activation", YT.t[:, q_ * 8:(q_ + 1) * 8, :L],
                           pb2.t[:, 0:512].rearrange("p (j t) -> p j t", j=8)[:, :, :L], AF.Copy, r=[pb2], w=[YT])
                    dma(mixT[2048:4096, t0:t0 + L].rearrange("(j p) t -> p j t", p=128), YT.t[:, :, :L], r=[YT],
                        w=[scr["mixT"]], append=True)

                op("dve", "memset", St.t[:], 0.0, w=[St])
                op("dve", "memset", Stb.t[:], 0.0, w=[Stb])
                for (t0, L) in chunks:
                    gdn_chunk(t0, L)
                dma(o_pS[l * 2048:(l + 1) * 2048, :].rearrange("(h d) e -> d h e", d=128), St.t[:], r=[St], w=[scr["out"]],
                    append=True)
                for s in range(NS):
                    row = l * NS + s
                    dma(St.t[:], sS_d[row * 2048:(row + 1) * 2048, :].rearrange("(h d) e -> d h e", d=128), w=[St])
                    op("act", "activation", Stb.t[:], St.t[:], AF.Copy, r=[St], w=[Stb])
                    gdn_chunk(TP + 8 * s, 8)
                    dma(o_sS[row * 2048:(row + 1) * 2048, :].rearrange("(h d) e -> d h e", d=128), St.t[:], r=[St],
                        w=[scr["out"]], append=True)

            def h_res(tag, c, m, ti, off, n, acc, acc_ap):
                xo_, so_, cur_t0 = tag
                i = hr[0] % 3
                hr[0] += 1
                tg0 = cur["t0"] + off
                kcb = c // 128
                dma(xo_[i].t[:m, :n], xT[c:c + m, tg0:tg0 + n], r=[XT_B[kcb]], w=[xo_[i]])
                op("dve", "tensor_tensor", so_[i].t[:m, :n], acc_ap, xo_[i].t[:m, :n], ALU.add, r=[acc, xo_[i]], w=[so_[i]])
                dma(xT[c:c + m, tg0:tg0 + n], so_[i].t[:m, :n], r=[so_[i]], w=[XT_B[kcb]], append=True)

            hr = [0]
            cur = {}
            with ExitStack() as ph:
                ph.callback(kb.barrier)
                actT = sb("actTc", [128, KC, 1024], BF16, st=ph)
                lb = make_lin_bufs(ph)
                xo_ = [sb("cxo%d" % i, [128, 512], st=ph) for i in range(3)]
                so_ = [sb("cso%d" % i, [128, 512], st=ph) for i in range(3)]
                segs = [(w_out[l * D:(l + 1) * D, :], c0, 256, h_res, (xo_, so_, None)) for c0 in range(0, D, 256)]
                for (t0, n, ts) in tok_tiles(1024):
                    cur["t0"] = t0
                    dma(actT.t[:, :, :n], mixT[:, t0:t0 + n].rearrange("(k p) t -> p k t", p=128), r=[scr["mixT"]], w=[actT])
                    linear(lb, actT, KC, ts, segs)

            with ExitStack() as ph:
                ph.callback(kb.barrier)
                actT = sb("actTd", [128, KC, 1024], BF16, st=ph)
                nb = make_norm_bufs(ph)
                lb = make_lin_bufs(ph)
                sg = {}
                for cb in range(2):
                    for ti in range(2):
                        sg[(cb, ti)] = sb("dsg%d%d" % (cb, ti), [128, 512], st=ph)
                ab = [sb("dab%d" % i, [128, 512], BF16, st=ph) for i in range(3)]
                an = [0]

                def h_ffn(tag, c, m, ti, off, n, acc, acc_ap):
                    kind, c0 = tag
                    cb = (c - c0) // 128
                    s_ = sg[(cb, ti)]
                    if kind == "g":
                        op("act", "activation", s_.t[:m, :n], acc_ap, AF.Silu, r=[acc], w=[s_])
                    else:
                        a_ = ab[an[0] % 3]
                        an[0] += 1
                        op("dve", "tensor_tensor", a_.t[:m, :n], acc_ap, s_.t[:m, :n], ALU.mult, r=[acc, s_], w=[a_])
                        tg0 = cur["t0"] + off
                        dma(aT[c:c + m, tg0:tg0 + n], a_.t[:m, :n], r=[a_], w=[scr["aT"]], append=True)

                for (t0, n, ts) in tok_tiles(1024):
                    cur["t0"] = t0
                    norm_to_act(nb, PP_NFFN + l * 32, t0, n, actT)
                    segs = []
                    for c0 in range(0, DFF, 256):
                        cn = min(256, DFF - c0)
                        segs.append((w_gate[l * D:(l + 1) * D, :], c0, cn, h_ffn, ("g", c0)))
                        segs.append((w_up[l * D:(l + 1) * D, :], c0, cn, h_ffn, ("u", c0)))
                    linear(lb, actT, KC, ts, segs)

            with ExitStack() as ph:
                ph.callback(kb.barrier)
                actT = sb("actTe", [128, KF, 736], BF16, st=ph)
                lb = make_lin_bufs(ph)
                xo_ = [sb("exo%d" % i, [128, 512], st=ph) for i in range(3)]
                so_ = [sb("eso%d" % i, [128, 512], st=ph) for i in range(3)]
                segs = [(w_down[l * DFF:(l + 1) * DFF, :], c0, 256, h_res, (xo_, so_, None)) for c0 in range(0, D, 256)]
                for (t0, n, ts) in tok_tiles(736):
                    cur["t0"] = t0
                    dma(actT.t[:, :, :n], aT[:, t0:t0 + n].rearrange("(k p) t -> p k t", p=128), r=[scr["aT"]], w=[actT])
                    linear(lb, actT, KF, ts, segs)

        with ExitStack() as ph:
            ph.callback(kb.barrier)
            nb = make_norm_bufs(ph)
            yT_ = sb("fyT", [128, KC, 256], st=ph)
            yo = [sb("fyo%d" % i, [128, D], st=ph) for i in range(2)]
            yn = [0]
            for t0 in range(0, T, 256):
                n = min(256, T - t0)

                def dst(kc, xt, wn_, rs, m):
                    op("dve", "scalar_tensor_tensor", yT_.t[:, kc, :m], xt.t[:, kc, :m], wn_, rs.t[:, :m], ALU.mult, ALU.mult,
                       r=[xt, rs, pp], w=[yT_])
                norm_piece(nb, PP_NFIN, t0, n, dst)
                for s0 in range(0, n, 128):
                    ns = min(128, n - s0)
                    o = yo[yn[0] % 2]
                    yn[0] += 1
                    for g in range(8):
                        pt = ps_next()
                        for j in range(4):
                            kc = g * 4 + j
                            op("pe", "transpose", pt.t[:ns, j * 128:(j + 1) * 128], yT_.t[:, kc, s0:s0 + ns], ident_f,
                               r=[yT_, cst], w=[pt])
                        if g % 2 == 0:
                            op("dve", "tensor_copy", o.t[:ns, g * 512:(g + 1) * 512], pt.t[:ns, :], r=[pt], w=[o])
                        else:
                            op("act", "activation", o.t[:ns, g * 512:(g + 1) * 512], pt.t[:ns, :], AF.Copy, r=[pt], w=[o])
                    dma(y_d[t0 + s0:t0 + s0 + ns, :], o.t[:ns, :], r=[o], w=[scr["y"]], append=True)
        kb.finish()
    return nc


def _pack_inputs(cfg, core, x_prompt, x_sample, st, meta_tokens, P):
    SEQ, NS, DEPTH, DFF = cfg["SEQ"], cfg["NS"], cfg["DEPTH"], cfg["DFF"]
    f = np.float32
    s_idx = core // 2 if cfg.get("pair", True) else core
    xs = x_sample[core * NS:(core + 1) * NS].reshape(NS * 8, D)
    xin = np.concatenate([meta_tokens, x_prompt[s_idx], xs], axis=0).astype(f)
    sl = slice(core * NS, (core + 1) * NS)
    m = {"xin": np.ascontiguousarray(xin)}
    m["sC"] = np.ascontiguousarray(st["C"][:, sl]).reshape(-1, 256)
    m["sn"] = np.ascontiguousarray(st["n"][:, sl].transpose(0, 1, 3, 2)).reshape(-1, 4)
    m["sm"] = np.ascontiguousarray(st["m"][:, sl]).reshape(1, -1)
    m["sh"] = np.ascontiguousarray(st["h"][:, sl].reshape(DEPTH, NS, 8, 128).transpose(0, 3, 2, 1)).reshape(DEPTH * 128, -1)
    m["src"] = np.ascontiguousarray(
        st["rc"][:, sl].reshape(DEPTH, NS, 3, 8, 128).transpose(0, 4, 3, 1, 2)).reshape(DEPTH * 128, -1)
    m["sS"] = np.ascontiguousarray(st["S"][:, sl]).reshape(-1, 128)
    m["sgc"] = np.ascontiguousarray(
        st["gc"][:, sl].reshape(DEPTH, NS, 3, 48, 128).transpose(0, 4, 3, 1, 2)).reshape(DEPTH * 128, -1)
    return m


def _pack_shared(cfg, P):
    DEPTH, DFF = cfg["DEPTH"], cfg["DFF"]
    f = np.float32

    def pc(v, nblk):
        return np.asarray(v, f).reshape(nblk, 128).T

    cols = [pc(P["norm_mix"][l], 32) for l in range(DEPTH)] + [pc(P["norm_ffn"][l], 32) for l in range(DEPTH)]
    cols.append(pc(P["norm_final"], 32))
    for l in range(DEPTH):
        cols.append(np.asarray(P["r_conv_w"][l], f).reshape(4, 8, 128).transpose(2, 1, 0).reshape(128, 32))
        cols.append(pc(P["r_conv_b"][l], 8))
        cols.append(pc(P["r_gate_a_b"][l], 8))
        cols.append(pc(P["r_gate_x_b"][l], 8))
        cols.append(pc(P["r_lambda"][l], 8))
        cols.append(np.asarray(P["g_conv_w"][l], f).reshape(4, 48, 128).transpose(2, 1, 0).reshape(128, 192))
    pp = np.ascontiguousarray(np.concatenate(cols, axis=1), f)
    pbs = []
    for l in range(DEPTH):
        pbs += [P["m_bias_i"][l], P["m_bias_f"][l], P["m_norm"][l], P["g_A_log"][l], P["g_dt_bias"][l], P["g_norm"][l]]
    pb = np.ascontiguousarray(np.concatenate([np.asarray(a, f).reshape(-1) for a in pbs])[None, :], f)
    rgw = np.stack([np.asarray(P["r_gate_a_w"], f), np.asarray(P["r_gate_x_w"], f)], axis=1)
    sh = {
        "w_in": np.asarray(P["w_in"], f).reshape(-1, DIN),
        "w_out": np.asarray(P["w_out"], f).reshape(-1, D),
        "w_gate": np.asarray(P["w_gate"], f).reshape(-1, DFF),
        "w_up": np.asarray(P["w_up"], f).reshape(-1, DFF),
        "w_down": np.asarray(P["w_down"], f).reshape(-1, D),
        "pp": pp, "pb": pb, "rgw": np.ascontiguousarray(rgw.reshape(-1, 128)), "consts": make_consts(),
    }
    return sh


def run(cfg, n_cores, x_prompt, x_sample, st, meta_tokens, P):
    SEQ, NS, DEPTH, DFF = cfg["SEQ"], cfg["NS"], cfg["DEPTH"], cfg["DFF"]
    nc = build(cfg)
    shared = _pack_shared(cfg, P)
    in_maps = []
    for c in range(n_cores):
        m = _pack_inputs(cfg, c, x_prompt, x_sample, st, meta_tokens, P)
        m.update(shared)
        in_maps.append(m)
    res = run_bass_kernel_spmd(nc, in_maps, core_ids=list(range(n_cores)))
    R = res.results
    TP = 16 + SEQ
    pair = cfg.get("pair", True)
    pcores = [2 * s for s in range(x_prompt.shape[0])] if pair else list(range(x_prompt.shape[0]))
    y_prompt = np.stack([R[c]["y"][16:TP] for c in pcores])
    y_sample = np.concatenate([R[c]["y"][TP:].reshape(NS, 8, D) for c in range(n_cores)])

    def pst(name, fn):
        return np.stack([fn(R[c][name]) for c in pcores], axis=1)

    def sst(name, fn):
        return np.concatenate([fn(R[c][name]) for c in range(n_cores)], axis=1)

    outs = [y_prompt, y_sample]
    outs.append(pst("o_pC", lambda a: a.reshape(DEPTH, 4, 128, 256)))
    outs.append(pst("o_pn", lambda a: a.reshape(DEPTH, 128, 4).transpose(0, 2, 1)))
    outs.append(pst("o_pm", lambda a: a.reshape(DEPTH, 4)))
    outs.append(pst("o_ph", lambda a: a.reshape(DEPTH, 128, 8).transpose(0, 2, 1).reshape(DEPTH, 1024)))
    outs.append(pst("o_prc", lambda a: a.reshape(DEPTH, 128, 8, 3).transpose(0, 3, 2, 1).reshape(DEPTH, 3, 1024)))
    outs.append(pst("o_pS", lambda a: a.reshape(DEPTH, 16, 128, 128)))
    outs.append(pst("o_pgc", lambda a: a.reshape(DEPTH, 128, 48, 3).transpose(0, 3, 2, 1).reshape(DEPTH, 3, 6144)))
    outs.append(sst("o_sC", lambda a: a.reshape(DEPTH, NS, 4, 128, 256)))
    outs.append(sst("o_sn", lambda a: a.reshape(DEPTH, NS, 128, 4).transpose(0, 1, 3, 2)))
    outs.append(sst("o_sm", lambda a: a.reshape(DEPTH, NS, 4)))
    outs.append(sst("o_sh", lambda a: a.reshape(DEPTH, 128, 8, NS).transpose(0, 3, 2, 1).reshape(DEPTH, NS, 1024)))
    outs.append(sst("o_src", lambda a: a.reshape(DEPTH, 128, 8, NS, 3).transpose(0, 3, 4, 2, 1).reshape(DEPTH, NS, 3, 1024)))
    outs.append(sst("o_sS", lambda a: a.reshape(DEPTH, NS, 16, 128, 128)))
    outs.append(sst("o_sgc", lambda a: a.reshape(DEPTH, 128, 48, NS, 3).transpose(0, 3, 4, 2, 1).reshape(DEPTH, NS, 3, 6144)))
    return tuple(np.ascontiguousarray(o, dtype=np.float32) for o in outs)


def kernel(x_prompt, x_sample, state_mlstm_C, state_mlstm_n, state_mlstm_m, state_rglru_h,
           state_rglru_conv, state_gdn_S, state_gdn_conv, meta_tokens, norm_mix, w_in,
           m_bias_i, m_bias_f, m_norm, r_conv_w, r_conv_b, r_gate_a_w, r_gate_a_b,
           r_gate_x_w, r_gate_x_b, r_lambda, g_conv_w, g_A_log, g_dt_bias, g_norm, w_out,
           norm_ffn, w_gate, w_up, w_down, norm_final):
    A = lambda a: np.asarray(a, np.float32)
    st = dict(C=A(state_mlstm_C), n=A(state_mlstm_n), m=A(state_mlstm_m), h=A(state_rglru_h),
              rc=A(state_rglru_conv), S=A(state_gdn_S), gc=A(state_gdn_conv))
    P = dict(norm_mix=A(norm_mix), w_in=w_in, m_bias_i=A(m_bias_i), m_bias_f=A(m_bias_f), m_norm=A(m_norm),
             r_conv_w=A(r_conv_w), r_conv_b=A(r_conv_b), r_gate_a_w=A(r_gate_a_w), r_gate_a_b=A(r_gate_a_b),
             r_gate_x_w=A(r_gate_x_w), r_gate_x_b=A(r_gate_x_b), r_lambda=A(r_lambda), g_conv_w=A(g_conv_w),
             g_A_log=A(g_A_log), g_dt_bias=A(g_dt_bias), g_norm=A(g_norm), w_out=w_out, norm_ffn=A(norm_ffn),
             w_gate=w_gate, w_up=w_up, w_down=w_down, norm_final=A(norm_final))
    return run(dict(FULL), 8, A(x_prompt), A(x_sample), st, A(meta_tokens), P)
```

```python
import numpy as np
from contextlib import ExitStack
import concourse.bass as bass
import concourse.mybir as mybir
from concourse.bass_utils import run_bass_kernel_spmd

F32 = mybir.dt.float32
BF16 = mybir.dt.bfloat16
AF = mybir.ActivationFunctionType
ALU = mybir.AluOpType
AX = mybir.AxisListType

D = 4096
DIN = 13352
KC = 32
EPS = 1e-6
NEG = -30000.0
FULL = dict(SEQ=2048, NS=16, DEPTH=2, DFF=11008)

C_ID, C_U, C_SL, C_NEGM, C_MUI, C_MUSN, C_MLSN, C_ONE, C_SEL8, C_SEL16, C_SEL64, C_END = [128 * i for i in range(12)]


def make_consts():
    c = np.zeros((128, C_END), np.float32)
    p = np.arange(128)[:, None]
    f = np.arange(128)[None, :]
    c[:, C_ID:C_ID + 128] = (p == f)
    c[:, C_U:C_U + 128] = (p <= f)
    c[:, C_SL:C_SL + 128] = (p > f)
    c[:, C_NEGM:C_NEGM + 128] = np.where(f <= p, 0.0, NEG)
    c[:, C_MUI:C_MUI + 128] = (p <= f)
    c[:, C_MUSN:C_MUSN + 128] = -1.0 * (p < f)
    c[:, C_MLSN:C_MLSN + 128] = -1.0 * (f < p)
    c[:, C_ONE:C_ONE + 128] = 1.0
    for off, L in ((C_SEL8, 8), (C_SEL16, 16), (C_SEL64, 64)):
        c[L - 1, off:off + 128] = 1.0
    return c


class Buf:
    __slots__ = ("w", "r", "t")

    def __init__(self, t=None):
        self.w = {}
        self.r = {}
        self.t = t


class KB:
    def __init__(self, nc, es, nd=24):
        self.nc = nc
        self.E = {"pe": nc.tensor, "act": nc.scalar, "dve": nc.vector, "pool": nc.gpsimd, "sp": nc.sync}
        self.H = {}
        self.cnt = {}
        for k in ("pe", "act", "dve", "pool"):
            self.H[k] = es.enter_context(nc.semaphore("s_" + k))
            self.cnt[k] = 0
        self.nd = nd
        self.dq = {}
        for q in ("sp", "pool", "act"):
            keys = []
            for i in range(nd if q == "sp" else 8):
                key = "d%s%d" % (q, i)
                self.H[key] = es.enter_context(nc.semaphore(key))
                keys.append(key)
            self.dq[q] = [keys, 0]
        self.dval = {}
        self.seen = {k: {} for k in self.E}

    def _need(self, r, w, skip_dma_waw=False):
        need = {}
        for b in r:
            for k, v in b.w.items():
                if need.get(k, 0) < v:
                    need[k] = v
        for b in w:
            for k, v in b.w.items():
                if skip_dma_waw and k[0] == "d" and k != "dve":
                    continue
                if need.get(k, 0) < v:
                    need[k] = v
            for k, v in b.r.items():
                if need.get(k, 0) < v:
                    need[k] = v
        return need

    def _wait(self, eng, need):
        seen = self.seen[eng]
        for k, v in need.items():
            if eng == "pe" and k == "pe":
                continue
            if seen.get(k, 0) < v:
                self.E[eng].wait_ge(self.H[k], v)
                seen[k] = v

    def op(self, eng, meth, *a, r=(), w=(), **kw):
        self._wait(eng, self._need(r, w))
        ins = getattr(self.E[eng], meth)(*a, **kw)
        self.cnt[eng] += 1
        c = self.cnt[eng]
        ins.then_inc(self.H[eng], 1)
        for b in r:
            b.r[eng] = c
        for b in w:
            b.w = {eng: c}
            b.r = {}
        return ins

    def dma(self, out, in_, r=(), w=(), q="sp", append=False, **kw):
        self._wait(q, self._need(r, w, skip_dma_waw=append))
        keys, n = self.dq[q]
        key = keys[n % len(keys)]
        self.dq[q][1] = n + 1
        prev = self.dval.get(key, 0)
        if prev and self.seen[q].get(key, 0) < prev:
            self.E[q].wait_ge(self.H[key], prev)
            self.seen[q][key] = prev
        val = prev + 16
        self.dval[key] = val
        ins = self.E[q].dma_start(out=out, in_=in_, **kw)
        ins.then_inc(self.H[key], 16)
        for b in r:
            b.r[key] = val
        for b in w:
            if append:
                b.w[key] = val
            else:
                b.w = {key: val}
            b.r = {}
        return ins

    def barrier(self):
        need = {k: v for k, v in self.dval.items()}
        for k in ("pe", "act", "dve", "pool"):
            if self.cnt[k]:
                need[k] = self.cnt[k]
        for eng in ("sp", "pe", "act", "dve", "pool"):
            self._wait(eng, {k: v for k, v in need.items() if k != eng})

    def finish(self):
        need = {k: v for k, v in self.dval.items()}
        for k in ("pe", "act", "dve", "pool"):
            if self.cnt[k]:
                need[k] = self.cnt[k]
        self._wait("sp", need)


def build(cfg):
    SEQ, NS, DEPTH, DFF = cfg["SEQ"], cfg["NS"], cfg["DEPTH"], cfg["DFF"]
    KF = DFF // 128
    TP = 16 + SEQ
    T = TP + NS * 8
    NE = 3 + TP + NS * 11
    SB0 = 3 + TP
    nc = bass.Bass("TRN2", target_bir_lowering=False)

    def din(name, shape):
        return nc.dram_tensor(name, list(shape), F32, kind="ExternalInput").ap()

    def dout(name, shape):
        return nc.dram_tensor(name, list(shape), F32, kind="ExternalOutput").ap()

    def dscr(name, shape, dt=F32):
        return nc.dram_tensor(name, list(shape), dt).ap()

    xin = din("xin", [T, D])
    w_in = din("w_in", [DEPTH * D, DIN])
    w_out = din("w_out", [DEPTH * D, D])
    w_gate = din("w_gate", [DEPTH * D, DFF])
    w_up = din("w_up", [DEPTH * D, DFF])
    w_down = din("w_down", [DEPTH * DFF, D])
    NPP = (2 * DEPTH + 1) * 32 + DEPTH * (8 * 4 + 8 * 4 + 48 * 4)
    pp_d = din("pp", [128, NPP])
    NPB = DEPTH * (8 + 1024 + 16 + 16 + 128)
    pb_d = din("pb", [1, NPB])
    rgw_d = din("rgw", [DEPTH * 2 * 8 * 128, 128])
    consts_d = din("consts", [128, C_END])
    sC_d = din("sC", [DEPTH * NS * 4 * 128, 256])
    sn_d = din("sn", [DEPTH * NS * 128, 4])
    sm_d = din("sm", [1, DEPTH * NS * 4])
    sh_d = din("sh", [DEPTH * 128, 8 * NS])
    src_d = din("src", [DEPTH * 128, 8 * NS * 3])
    sS_d = din("sS", [DEPTH * NS * 16 * 128, 128])
    sgc_d = din("sgc", [DEPTH * 128, 48 * NS * 3])

    y_d = dout("y", [T, D])
    o_pC = dout("o_pC", [DEPTH * 4 * 128, 256])
    o_pn = dout("o_pn", [DEPTH * 128, 4])
    o_pm = dout("o_pm", [DEPTH, 4])
    o_ph = dout("o_ph", [DEPTH * 128, 8])
    o_prc = dout("o_prc", [DEPTH * 128, 8 * 3])
    o_pS = dout("o_pS", [DEPTH * 16 * 128, 128])
    o_pgc = dout("o_pgc", [DEPTH * 128, 48 * 3])
    o_sC = dout("o_sC", [DEPTH * NS * 4 * 128, 256])
    o_sn = dout("o_sn", [DEPTH * NS * 128, 4])
    o_sm = dout("o_sm", [DEPTH * NS, 4])
    o_sh = dout("o_sh", [DEPTH * 128, 8 * NS])
    o_src = dout("o_src", [DEPTH * 128, 8 * NS * 3])
    o_sS = dout("o_sS", [DEPTH * NS * 16 * 128, 128])
    o_sgc = dout("o_sgc", [DEPTH * 128, 48 * NS * 3])

    xT = dscr("xT", [D, T])
    mqT = dscr("mqT", [512, T], BF16)
    mkT = dscr("mkT", [512, T], BF16)
    mk_tok = dscr("mk_tok", [T, 512], BF16)
    mv_tok = dscr("mv_tok", [T, 1024], BF16)
    mo_tok = dscr("mo_tok", [T, 1024], BF16)
    mg_tok = dscr("mg_tok", [T, 8])
    rxT = dscr("rxT", [1024, T])
    rgT = dscr("rgT", [1024, T])
    gqkvT = dscr("gqkvT", [6144, T])
    gz_tok = dscr("gz_tok", [T, 2048], BF16)
    gg_tok = dscr("gg_tok", [T, 48])
    gqnT = dscr("gqnT", [2048, T], BF16)
    gknT = dscr("gknT", [2048, T], BF16)
    gk_tok = dscr("gk_tok", [T, 2048], BF16)
    gv_tok = dscr("gv_tok", [T, 2048], BF16)
    mixT = dscr("mixT", [D, T], BF16)
    aT = dscr("aT", [DFF, T], BF16)

    PP_NMIX = 0
    PP_NFFN = DEPTH * 32
    PP_NFIN = 2 * DEPTH * 32
    PP_R = (2 * DEPTH + 1) * 32
    PPL = 8 * 4 + 8 * 4 + 48 * 4
    PBL = 8 + 1024 + 16 + 16 + 128

    with ExitStack() as es:
        es.enter_context(nc.allow_non_contiguous_dma(reason="small strided state / layout DMAs"))
        kb = KB(nc, es)
        op, dma = kb.op, kb.dma

        sbn = [0]

        def sb(name, shape, dt=F32, st=es):
            sbn[0] += 1
            t = st.enter_context(nc.sbuf_tensor("t%d_%s" % (sbn[0], name), list(shape), dt))
            return Buf(t)

        PS = [Buf(es.enter_context(nc.psum_tensor("ps%d" % i, [128, 512], F32))) for i in range(6)]
        PSB = [Buf(es.enter_context(nc.psum_tensor("psb%d" % i, [128, 1024], BF16))) for i in range(2)]
        psn = [0]
        psbn = [0]

        def ps_next():
            psn[0] += 1
            return PS[psn[0] % 6]

        def psb_next():
            psbn[0] += 1
            return PSB[psbn[0] % 2]

        cst = sb("cst", [128, C_END])
        cstb = sb("cstb", [128, 384], BF16)
        pp = sb("pp", [128, NPP])
        dma(cst.t[:], consts_d[:, :], w=[cst])
        dma(pp.t[:], pp_d[:, :], w=[pp])
        op("dve", "tensor_copy", cstb.t[:, 0:128], cst.t[:, C_ID:C_ID + 128], r=[cst], w=[cstb])
        op("dve", "tensor_copy", cstb.t[:, 128:256], cst.t[:, C_U:C_U + 128], r=[cst], w=[cstb])
        op("dve", "tensor_copy", cstb.t[:, 256:384], cst.t[:, C_ONE:C_ONE + 128], r=[cst], w=[cstb])
        ident_f = cst.t[:, C_ID:C_ID + 128]
        ident_b = cstb.t[:, 0:128]
        ones_b = cstb.t[:, 256:384]
        ones_f = cst.t[:, C_ONE:C_ONE + 128]
        Umat = cst.t[:, C_U:C_U + 128]

        XT_B = [Buf() for _ in range(KC)]
        scr = {n: Buf() for n in ("mqT", "mkT", "mk_tok", "mv_tok", "mo_tok", "mg_tok", "rxT", "rgT", "gqkvT",
                                  "gz_tok", "gg_tok", "gqnT", "gknT", "gk_tok", "gv_tok", "mixT", "aT", "y", "out")}

        subs = [(t0, min(128, T - t0)) for t0 in range(0, T, 128)]

        with ExitStack() as ph:
            ph.callback(kb.barrier)
            xs = [sb("x0s%d" % i, [128, D], st=ph) for i in range(2)]
            xo = [sb("x0o%d" % i, [128, KC, 128], st=ph) for i in range(2)]
            for si, (t0, n) in enumerate(subs):
                a, o = xs[si % 2], xo[si % 2]
                dma(a.t[:n, :], xin[t0:t0 + n, :], w=[a])
                for g in range(8):
                    pt = ps_next()
                    for j in range(4):
                        kc = g * 4 + j
                        op("pe", "transpose", pt.t[:, j * 128:j * 128 + n], a.t[:n, kc * 128:(kc + 1) * 128],
                           ident_f[:n, :n], r=[a, cst], w=[pt])
                    eng = "dve" if g % 2 == 0 else "act"
                    src = pt.t[:, :].rearrange("p (j t) -> p j t", j=4)[:, :, :n]
                    if eng == "dve":
                        op("dve", "tensor_copy", o.t[:, g * 4:g * 4 + 4, :n], src, r=[pt], w=[o])
                    else:
                        op("act", "activation", o.t[:, g * 4:g * 4 + 4, :n], src, AF.Copy, r=[pt], w=[o])
                dma(xT[:, t0:t0 + n].rearrange("(k p) t -> p k t", p=128), o.t[:, :, :n], r=[o], w=XT_B, append=True)

        def norm_piece(ph_bufs, wcol0, t0, n, dst_fn):
            xt, sq, rs = ph_bufs["xt"][ph_bufs["i"] % 2], ph_bufs["sq"], ph_bufs["rs"]
            ph_bufs["i"] += 1
            dma(xt.t[:, :, :n], xT[:, t0:t0 + n].rearrange("(k p) t -> p k t", p=128), r=XT_B, w=[xt])
            op("act", "activation", sq.t[:, :, :n], xt.t[:, :, :n], AF.Square, r=[xt], w=[sq])
            pt = ps_next()
            for kc in range(KC):
                op("pe", "matmul", pt.t[:, :n], ones_b, sq.t[:, kc, :n], start=(kc == 0), stop=(kc == KC - 1),
                   r=[sq, cstb], w=[pt])
            op("dve", "tensor_scalar", rs.t[:, :n], pt.t[:, :n], 1.0 / D, EPS, ALU.mult, ALU.add, r=[pt], w=[rs])
            op("act", "activation", rs.t[:, :n], rs.t[:, :n], AF.Sqrt, r=[rs], w=[rs])
            op("dve", "reciprocal", rs.t[:, :n], rs.t[:, :n], r=[rs], w=[rs])
            for kc in range(KC):
                dst_fn(kc, xt, pp.t[:, wcol0 + kc:wcol0 + kc + 1], rs, n)

        def make_norm_bufs(ph):
            return dict(xt=[sb("nxt%d" % i, [128, KC, 256], st=ph) for i in range(2)],
                        sq=sb("nsq", [128, KC, 256], BF16, st=ph), rs=sb("nrs", [128, 256], st=ph), i=0)

        def norm_to_act(nb, wcol0, t0, n, actT):
            for p0 in range(0, n, 256):
                pn = min(256, n - p0)

                def dst(kc, xt, wn, rs, m, p0=p0):
                    op("dve", "scalar_tensor_tensor", actT.t[:, kc, p0:p0 + m], xt.t[:, kc, :m], wn, rs.t[:, :m],
                       ALU.mult, ALU.mult, r=[xt, rs, pp], w=[actT])
                norm_piece(nb, wcol0, t0 + p0, pn, dst)

        def linear(ph, actT, KCn, ts_list, segs, PK=8):
            stg = ph["stg"]
            wbs = ph["wb"]
            nbuf = len(stg)
            npieces = (KCn + PK - 1) // PK
            plist = [(si, pi) for si in range(len(segs)) for pi in range(npieces)]
            base = ph["n"]
            issued = [0]

            def issue_upto(k):
                while issued[0] < min(k, len(plist)):
                    si_, pi_ = plist[issued[0]]
                    Wd, c0, ncol = segs[si_][0], segs[si_][1], segs[si_][2]
                    k0 = pi_ * PK
                    kn = min(PK, KCn - k0)
                    s_ = stg[(base + issued[0]) % nbuf]
                    dma(s_.t[:, :kn, :ncol],
                        Wd[k0 * 128:(k0 + kn) * 128, c0:c0 + ncol].rearrange("(k p) c -> p k c", p=128), w=[s_])
                    issued[0] += 1

            def do_cast(i):
                si_, pi_ = plist[i]
                ncol = segs[si_][2]
                kn = min(PK, KCn - pi_ * PK)
                s_ = stg[(base + i) % nbuf]
                wb = wbs[(base + i) % len(wbs)]
                ph["n"] += 1
                ce = ("dve", "act", "dve", "act", "pool", "dve", "act")[ph["n"] % 7]
                if ce == "act":
                    op("act", "activation", wb.t[:, :kn, :ncol], s_.t[:, :kn, :ncol], AF.Copy, r=[s_], w=[wb])
                else:
                    op(ce, "tensor_copy", wb.t[:, :kn, :ncol], s_.t[:, :kn, :ncol], r=[s_], w=[wb])

            issue_upto(nbuf)
            do_cast(0)
            idx = 0
            for si, seg in enumerate(segs):
                Wd, c0, ncol, handler, tag = seg[:5]
                if len(seg) > 5 and seg[5] is not None:
                    seg[5](tag, c0, ncol, ts_list)
                ncb = (ncol + 127) // 128
                accs = {}
                for cb in range(ncb):
                    for ti in range(len(ts_list)):
                        accs[(cb, ti)] = ps_next()
                for pi in range(npieces):
                    k0 = pi * PK
                    kn = min(PK, KCn - k0)
                    if idx + 1 < len(plist):
                        issue_upto(idx + 1 + nbuf)
                        do_cast(idx + 1)
                    wb = wbs[(base + idx) % len(wbs)]
                    idx += 1
                    for cb in range(ncb):
                        m = min(128, ncol - cb * 128)
                        for ti, (off, n) in enumerate(ts_list):
                            acc = accs[(cb, ti)]
                            for j in range(kn):
                                kc = k0 + j
                                op("pe", "matmul", acc.t[:m, :n], wb.t[:, j, cb * 128:cb * 128 + m],
                                   actT.t[:, kc, off:off + n], start=(kc == 0), stop=(kc == KCn - 1),
                                   r=[wb, actT], w=[acc])
                for cb in range(ncb):
                    m = min(128, ncol - cb * 128)
                    for ti, (off, n) in enumerate(ts_list):
                        acc = accs[(cb, ti)]
                        handler(tag, c0 + cb * 128, m, ti, off, n, acc, acc.t[:m, :n])

        def make_lin_bufs(ph):
            return dict(stg=[sb("lstg%d" % i, [128, 8, 256], st=ph) for i in range(3)],
                        wb=[sb("lwb%d" % i, [128, 8, 256], BF16, st=ph) for i in range(3)], n=0)

        def tok_tiles(maxn):
            res = []
            t0 = 0
            while t0 < T:
                n = min(maxn, T - t0)
                ts = [(o, min(512, n - o)) for o in range(0, n, 512)]
                res.append((t0, n, ts))
                t0 += n
            return res

        for l in range(DEPTH):
            ppl = PP_R + l * PPL
            pbl = l * PBL
            with ExitStack() as ph:
                ph.callback(kb.barrier)
                actT = sb("actT", [128, KC, 736], BF16, st=ph)
                nb = make_norm_bufs(ph)
                lb = make_lin_bufs(ph)
                stf = [sb("stf%d" % i, [128, 512], st=ph) for i in range(3)]
                stb = [sb("stb%d" % i, [128, 512], BF16, st=ph) for i in range(3)]
                tmb = [sb("tmb%d" % i, [128, 4, 128], BF16, st=ph) for i in range(3)]
                tmf = [sb("tmf%d" % i, [128, 48], st=ph) for i in range(3)]
                gbias = sb("gbias", [128, 8], st=ph)
                gdt = sb("gdt", [128, 32], st=ph)
                dma(gbias.t[:], pb_d[0:1, pbl:pbl + 8].partition_broadcast(128), w=[gbias])
                dma(gdt.t[:], pb_d[0:1, pbl + 8 + 1024:pbl + 8 + 1024 + 32].partition_broadcast(128), w=[gdt])
                op("act", "activation", gdt.t[:, 0:16], gdt.t[:, 0:16], AF.Exp, r=[gdt], w=[gdt])
                op("dve", "tensor_scalar", gdt.t[:, 0:16], gdt.t[:, 0:16], -1.0, None, ALU.mult, r=[gdt], w=[gdt])
                rr = [0]

                def h_in(tag, c, m, ti, off, n, acc, acc_ap, tt0=None):
                    kind, dst, r0 = tag
                    i = rr[0] % 3
                    rr[0] += 1
                    tg0 = cur["t0"] + off
                    row = c - r0
                    if kind in ("f32",):
                        s_ = stf[i]
                        if i % 2 == 0:
                            op("dve", "tensor_copy", s_.t[:m, :n], acc_ap, r=[acc], w=[s_])
                        else:
                            op("act", "activation", s_.t[:m, :n], acc_ap, AF.Copy, r=[acc], w=[s_])
                        dma(dst[0][row:row + m, tg0:tg0 + n], s_.t[:m, :n], r=[s_], w=[scr[dst[1]]], append=True)
                        return
                    if kind == "gate":
                        s_ = stf[i]
                        op("dve", "tensor_copy", s_.t[:m, :n], acc_ap, r=[acc], w=[s_])
                        for s0 in range(0, n, 128):
                            ns = min(128, n - s0)
                            pt = ps_next()
                            op("pe", "transpose", pt.t[:ns, :m], s_.t[:m, s0:s0 + ns], ident_f[:m, :m], r=[s_, cst], w=[pt])
                            tf = tmf[rr[0] % 3]
                            rr[0] += 1
                            if dst[1] == "mg_tok":
                                op("dve", "tensor_tensor", tf.t[:ns, 0:8], pt.t[:ns, 0:8], gbias.t[:ns, :], ALU.add,
                                   r=[pt, gbias], w=[tf])
                                op("act", "activation", tf.t[:ns, 8:12], tf.t[:ns, 4:8], AF.Exp, scale=-1.0, r=[tf], w=[tf])
                                op("act", "activation", tf.t[:ns, 8:12], tf.t[:ns, 8:12], AF.Ln, bias=1.0, r=[tf], w=[tf])
                                op("dve", "tensor_scalar", tf.t[:ns, 4:8], tf.t[:ns, 8:12], -1.0, None, ALU.mult, r=[tf], w=[tf])
                                dma(mg_tok[tg0 + s0:tg0 + s0 + ns, :], tf.t[:ns, 0:8], r=[tf], w=[scr["mg_tok"]], append=True)
                            else:
                                op("act", "activation", tf.t[:ns, 32:48], pt.t[:ns, 0:16], AF.Exp, scale=-1.0, r=[pt], w=[tf])
                                op("dve", "tensor_scalar", tf.t[:ns, 32:48], tf.t[:ns, 32:48], 1.0, None, ALU.add, r=[tf], w=[tf])
                                op("dve", "reciprocal", tf.t[:ns, 0:16], tf.t[:ns, 32:48], r=[tf], w=[tf])
                                op("act", "activation", tf.t[:ns, 16:32], tf.t[:ns, 32:48], AF.Ln, r=[tf], w=[tf])
                                op("dve", "tensor_scalar", tf.t[:ns, 16:32], tf.t[:ns, 16:32], -1.0, None, ALU.mult, r=[tf], w=[tf])
                                op("dve", "tensor_tensor", tf.t[:ns, 32:48], pt.t[:ns, 16:32], gdt.t[:ns, 16:32], ALU.add,
                                   r=[pt, gdt], w=[tf])
                                op("act", "activation", tf.t[:ns, 32:48], tf.t[:ns, 32:48], AF.Exp, r=[tf], w=[tf])
                                op("act", "activation", tf.t[:ns, 32:48], tf.t[:ns, 32:48], AF.Ln, bias=1.0, r=[tf], w=[tf])
                                op("dve", "tensor_tensor", tf.t[:ns, 32:48], tf.t[:ns, 32:48], gdt.t[:ns, 0:16], ALU.mult,
                                   r=[tf, gdt], w=[tf])
                                dma(gg_tok[tg0 + s0:tg0 + s0 + ns, :], tf.t[:ns, 0:48], r=[tf], w=[scr["gg_tok"]], append=True)
                        return
                    s_ = stb[i]
                    if kind == "mq":
                        op("act", "activation", s_.t[:m, :n], acc_ap, AF.Copy, scale=128.0 ** -0.5, r=[acc], w=[s_])
                    elif kind == "sig":
                        op("act", "activation", s_.t[:m, :n], acc_ap, AF.Sigmoid, r=[acc], w=[s_])
                    elif kind == "silu":
                        op("act", "activation", s_.t[:m, :n], acc_ap, AF.Silu, r=[acc], w=[s_])
                    else:
                        op("dve", "tensor_copy", s_.t[:m, :n], acc_ap, r=[acc], w=[s_])
                    fm, tm = dst
                    if fm is not None:
                        dma(fm[0][row:row + m, tg0:tg0 + n], s_.t[:m, :n], r=[s_], w=[scr[fm[1]]], append=True)
                    if tm is not None:
                        pb_ = psb_next()
                        tb = tmb[rr[0] % 3]
                        rr[0] += 1
                        nsub = (n + 127) // 128
                        for s in range(nsub):
                            ns = min(128, n - s * 128)
                            op("pe", "transpose", pb_.t[:ns, s * 128:s * 128 + m], s_.t[:m, s * 128:s * 128 + ns],
                               ident_b[:m, :m], r=[s_, cstb], w=[pb_])
                        nfull = n // 128
                        if nfull:
                            op("dve", "tensor_copy", tb.t[:, :nfull, :m],
                               pb_.t[:, 0:nfull * 128].rearrange("p (s c) -> p s c", s=nfull)[:, :, :m], r=[pb_], w=[tb])
                            dma(tm[0][tg0:tg0 + nfull * 128, row:row + m].rearrange("(s p) c -> p s c", p=128),
                                tb.t[:, :nfull, :m], r=[tb], w=[scr[tm[1]]], append=True)
                        if n % 128:
                            ns = n % 128
                            op("dve", "tensor_copy", tb.t[:ns, nfull, :m], pb_.t[:ns, nfull * 128:nfull * 128 + m],
                               r=[pb_], w=[tb])
                            dma(tm[0][tg0 + nfull * 128:tg0 + n, row:row + m], tb.t[:ns, nfull, :m], r=[tb],
                                w=[scr[tm[1]]], append=True)

                groups = [
                    (0, 512, "mq", ((mqT, "mqT"), None)),
                    (512, 512, "cp", ((mkT, "mkT"), (mk_tok, "mk_tok"))),
                    (1024, 1024, "cp", (None, (mv_tok, "mv_tok"))),
                    (2048, 1024, "sig", (None, (mo_tok, "mo_tok"))),
                    (3072, 8, "gate", (None, "mg_tok")),
                    (3080, 1024, "f32", (rxT, "rxT")),
                    (4104, 1024, "f32", (rgT, "rgT")),
                    (5128, 6144, "f32", (gqkvT, "gqkvT")),
                    (11272, 2048, "silu", (None, (gz_tok, "gz_tok"))),
                    (13320, 32, "gate", (None, "gg_tok")),
                ]
                segs = []
                for (g0, gn, kind, dst) in groups:
                    for c0 in range(g0, g0 + gn, 256):
                        segs.append((w_in[l * D:(l + 1) * D, :], c0, min(256, g0 + gn - c0), h_in, (kind, dst, g0)))
                cur = {}
                for (t0, n, ts) in tok_tiles(736):
                    cur["t0"] = t0
                    norm_to_act(nb, PP_NMIX + l * 32, t0, n, actT)
                    linear(lb, actT, KC, ts, segs)

            with ExitStack() as ph:
                ph.callback(kb.barrier)
                gw = sb("rgw", [128, 16, 128], st=ph)
                gwb = sb("rgwb", [128, 16, 128], BF16, st=ph)
                dma(gw.t[:], rgw_d[l * 2048:(l + 1) * 2048, :].rearrange("(g d) e -> d g e", d=128), w=[gw])
                op("dve", "tensor_copy", gwb.t[:], gw.t[:], r=[gw], w=[gwb])
                nsp = sb("nsp", [128, 8], st=ph)
                lam = pp.t[:, ppl + 56:ppl + 64]
                op("act", "activation", nsp.t[:], lam, AF.Exp, scale=-1.0, r=[pp], w=[nsp])
                op("act", "activation", nsp.t[:], nsp.t[:], AF.Ln, bias=1.0, r=[nsp], w=[nsp])
                op("dve", "tensor_scalar", nsp.t[:], nsp.t[:], -8.0, None, ALU.mult, r=[nsp], w=[nsp])
                XE = sb("rXE", [128, NE], st=ph)
                XC = sb("rXC", [128, NE], st=ph)
                XCb = sb("rXCb", [128, NE], BF16, st=ph)
                RG = sb("rRG", [128, NE], st=ph)
                IG = sb("rIG", [128, NE], st=ph)
                AA = sb("rAA", [128, NE], st=ph)
                HH = sb("rHH", [128, NE], st=ph)
                GG = sb("rGG", [128, T], st=ph)
                G2 = sb("rG2", [128, T], st=ph)
                YY = sb("rYY", [128, T], BF16, st=ph)
                h0 = sb("rh0", [128, 8 * NS], st=ph)
                hl = sb("rhl", [128, 8, 1 + NS], st=ph)
                cvo = sb("rcvo", [128, 8, 3 + 3 * NS], st=ph)
                dma(h0.t[:], sh_d[l * 128:(l + 1) * 128, :], w=[h0])
                op("dve", "memset", XE.t[:, 0:3], 0.0, w=[XE])
                NV = NE - 3
                for b in range(8):
                    dma(XE.t[:, 3:3 + TP], rxT[b * 128:(b + 1) * 128, 0:TP], r=[scr["rxT"]], w=[XE])
                    xes = XE.t[:, SB0:NE].rearrange("p (s j) -> p s j", j=11)
                    dma(xes[:, :, 3:11], rxT[b * 128:(b + 1) * 128, TP:T].rearrange("p (s j) -> p s j", j=8),
                        r=[scr["rxT"]], w=[XE], append=True)
                    dma(xes[:, :, 0:3],
                        src_d[l * 128:(l + 1) * 128, b * NS * 3:(b + 1) * NS * 3].rearrange("p (s j) -> p s j", j=3),
                        w=[XE], append=True)
                    dma(GG.t[:], rgT[b * 128:(b + 1) * 128, :], r=[scr["rgT"]], w=[GG])
                    cw = ppl + b * 4
                    op("dve", "tensor_scalar", XC.t[:, 0:NV], XE.t[:, 3:NE], pp.t[:, cw + 3:cw + 4],
                       pp.t[:, ppl + 32 + b:ppl + 33 + b], ALU.mult, ALU.add, r=[XE, pp], w=[XC])
                    for k in range(3):
                        op("dve", "scalar_tensor_tensor", XC.t[:, 0:NV], XE.t[:, k:k + NV], pp.t[:, cw + k:cw + k + 1],
                           XC.t[:, 0:NV], ALU.mult, ALU.add, r=[XE, pp, XC], w=[XC])
                    op("pool", "tensor_copy", XCb.t[:, 0:NV], XC.t[:, 0:NV], r=[XC], w=[XCb])
                    for c0 in range(0, NV, 512):
                        cn = min(512, NV - c0)
                        p1 = ps_next()
                        op("pe", "matmul", p1.t[:, :cn], gwb.t[:, b, :], XCb.t[:, c0:c0 + cn], start=True, stop=True,
                           r=[gwb, XCb], w=[p1])
                        op("act", "activation", RG.t[:, c0:c0 + cn], p1.t[:, :cn], AF.Sigmoid,
                           bias=pp.t[:, ppl + 40 + b:ppl + 41 + b], r=[p1, pp], w=[RG])
                        p2 = ps_next()
                        op("pe", "matmul", p2.t[:, :cn], gwb.t[:, 8 + b, :], XCb.t[:, c0:c0 + cn], start=True, stop=True,
                           r=[gwb, XCb], w=[p2])
                        op("act", "activation", IG.t[:, c0:c0 + cn], p2.t[:, :cn], AF.Sigmoid,
                           bias=pp.t[:, ppl + 48 + b:ppl + 49 + b], r=[p2, pp], w=[IG])
                    op("act", "activation", AA.t[:, 0:NV], RG.t[:, 0:NV], AF.Exp, scale=nsp.t[:, b:b + 1], r=[RG, nsp], w=[AA])
                    op("dve", "tensor_tensor", RG.t[:, 0:NV], AA.t[:, 0:NV], AA.t[:, 0:NV], ALU.mult, r=[AA], w=[RG])
                    op("dve", "tensor_scalar", RG.t[:, 0:NV], RG.t[:, 0:NV], -1.0, 1.0, ALU.mult, ALU.add, r=[RG], w=[RG])
                    op("dve", "tensor_scalar", RG.t[:, 0:NV], RG.t[:, 0:NV], 0.0, None, ALU.max, r=[RG], w=[RG])
                    op("act", "activation", RG.t[:, 0:NV], RG.t[:, 0:NV], AF.Sqrt, r=[RG], w=[RG])
                    op("dve", "tensor_tensor", IG.t[:, 0:NV], IG.t[:, 0:NV], XC.t[:, 0:NV], ALU.mult, r=[IG, XC], w=[IG])
                    op("dve", "tensor_tensor", IG.t[:, 0:NV], IG.t[:, 0:NV], RG.t[:, 0:NV], ALU.mult, r=[IG, RG], w=[IG])
                    op("dve", "tensor_tensor_scan", HH.t[:, 0:TP], AA.t[:, 0:TP], IG.t[:, 0:TP], 0.0, ALU.mult, ALU.add,
                       r=[AA, IG], w=[HH])
                    for s in range(NS):
                        e0 = SB0 + 11 * s
                        op("dve", "tensor_tensor_scan", HH.t[:, e0:e0 + 8], AA.t[:, e0:e0 + 8], IG.t[:, e0:e0 + 8],
                           h0.t[:, b * NS + s:b * NS + s + 1], ALU.mult, ALU.add, r=[AA, IG, h0], w=[HH])
                    op("dve", "tensor_tensor", G2.t[:], GG.t[:], GG.t[:], ALU.mult, r=[GG], w=[G2])
                    op("dve", "tensor_scalar", G2.t[:], G2.t[:], 0.044715, 1.0, ALU.mult, ALU.add, r=[G2], w=[G2])
                    op("dve", "tensor_tensor", G2.t[:], G2.t[:], GG.t[:], ALU.mult, r=[G2, GG], w=[G2])
                    op("act", "activation", G2.t[:], G2.t[:], AF.Sigmoid, scale=1.5957691216057308, r=[G2], w=[G2])
                    op("dve", "tensor_tensor", G2.t[:], G2.t[:], GG.t[:], ALU.mult, r=[G2, GG], w=[G2])
                    op("dve", "tensor_tensor", YY.t[:, 0:TP], HH.t[:, 0:TP], G2.t[:, 0:TP], ALU.mult, r=[HH, G2], w=[YY])
                    hs = HH.t[:, SB0:SB0 + 11 * NS].rearrange("p (s j) -> p s j", j=11)
                    op("dve", "tensor_tensor", YY.t[:, TP:T].rearrange("p (s j) -> p s j", j=8), hs[:, :, 0:8],
                       G2.t[:, TP:T].rearrange("p (s j) -> p s j", j=8), ALU.mult, r=[HH, G2], w=[YY])
                    dma(mixT[1024 + b * 128:1024 + (b + 1) * 128, :], YY.t[:], r=[YY], w=[scr["mixT"]], append=True)
                    op("act", "activation", hl.t[:, b, 0:1], HH.t[:, TP - 1:TP], AF.Copy, r=[HH], w=[hl])
                    op("act", "activation", hl.t[:, b, 1:1 + NS], hs[:, :, 7], AF.Copy, r=[HH], w=[hl])
                    op("act", "activation", cvo.t[:, b, 0:3], XE.t[:, TP:TP + 3], AF.Copy, r=[XE], w=[cvo])
                    op("act", "activation", cvo.t[:, b, 3:3 + 3 * NS].rearrange("p (s j) -> p s j", j=3), xes[:, :, 8:11],
                       AF.Copy, r=[XE], w=[cvo])
                dma(o_ph[l * 128:(l + 1) * 128, :], hl.t[:, :, 0], r=[hl], w=[scr["out"]], append=True)
                dma(o_sh[l * 128:(l + 1) * 128, :].rearrange("p (b s) -> p b s", b=8), hl.t[:, :, 1:1 + NS], r=[hl],
                    w=[scr["out"]], append=True)
                dma(o_prc[l * 128:(l + 1) * 128, :].rearrange("p (b j) -> p b j", b=8), cvo.t[:, :, 0:3], r=[cvo],
                    w=[scr["out"]], append=True)
                dma(o_src[l * 128:(l + 1) * 128, :].rearrange("p (b j) -> p b j", b=8), cvo.t[:, :, 3:3 + 3 * NS], r=[cvo],
                    w=[scr["out"]], append=True)

            with ExitStack() as ph:
                ph.callback(kb.barrier)
                XE2 = [sb("gXE%d" % i, [128, NE], st=ph) for i in range(2)]
                XC2 = [sb("gXC%d" % i, [128, NE], st=ph) for i in range(2)]
                XS2 = [sb("gXS%d" % i, [128, T], st=ph) for i in range(2)]
                SQ2 = [sb("gSQ%d" % i, [128, T], st=ph) for i in range(2)]
                XN2 = [sb("gXN%d" % i, [128, T], BF16, st=ph) for i in range(2)]
                cvo = sb("gcvo", [128, 48, 3 + 3 * NS], st=ph)
                tmb = [sb("gtmb%d" % i, [128, 4, 128], BF16, st=ph) for i in range(4)]
                for XE in XE2:
                    op("dve", "memset", XE.t[:, 0:3], 0.0, w=[XE])
                NV = NE - 3
                def g_load(b):
                    XE = XE2[b % 2]
                    dma(XE.t[:, 3:3 + TP], gqkvT[b * 128:(b + 1) * 128, 0:TP], r=[scr["gqkvT"]], w=[XE])
                    xes = XE.t[:, SB0:NE].rearrange("p (s j) -> p s j", j=11)
                    dma(xes[:, :, 3:11], gqkvT[b * 128:(b + 1) * 128, TP:T].rearrange("p (s j) -> p s j", j=8),
                        r=[scr["gqkvT"]], w=[XE], append=True)
                    dma(xes[:, :, 0:3],
                        sgc_d[l * 128:(l + 1) * 128, b * NS * 3:(b + 1) * NS * 3].rearrange("p (s j) -> p s j", j=3),
                        w=[XE], append=True)

                g_load(0)
                for b in range(48):
                    XE, XC, XS, SQ, XN = XE2[b % 2], XC2[b % 2], XS2[b % 2], SQ2[b % 2], XN2[b % 2]
                    if b + 1 < 48:
                        g_load(b + 1)
                    xes = XE.t[:, SB0:NE].rearrange("p (s j) -> p s j", j=11)
                    cw = ppl + 64 + b * 4
                    op("dve", "tensor_scalar", XC.t[:, 0:NV], XE.t[:, 3:NE], pp.t[:, cw + 3:cw + 4], None, ALU.mult,
                       r=[XE, pp], w=[XC])
                    for k in range(3):
                        op("dve", "scalar_tensor_tensor", XC.t[:, 0:NV], XE.t[:, k:k + NV], pp.t[:, cw + k:cw + k + 1],
                           XC.t[:, 0:NV], ALU.mult, ALU.add, r=[XE, pp, XC], w=[XC])
                    op("act", "activation", XS.t[:, 0:TP], XC.t[:, 0:TP], AF.Silu, r=[XC], w=[XS])
                    xcs = XC.t[:, SB0:SB0 + 11 * NS].rearrange("p (s j) -> p s j", j=11)
                    op("act", "activation", XS.t[:, TP:T].rearrange("p (s j) -> p s j", j=8), xcs[:, :, 0:8], AF.Silu,
                       r=[XC], w=[XS])
                    op("act", "activation", cvo.t[:, b, 0:3], XE.t[:, TP:TP + 3], AF.Copy, r=[XE], w=[cvo])
                    op("act", "activation", cvo.t[:, b, 3:3 + 3 * NS].rearrange("p (s j) -> p s j", j=3), xes[:, :, 8:11],
                       AF.Copy, r=[XE], w=[cvo])
                    if b < 32:
                        op("dve", "tensor_tensor", SQ.t[:], XS.t[:], XS.t[:], ALU.mult, r=[XS], w=[SQ])
                        for c0 in range(0, T, 512):
                            cn = min(512, T - c0)
                            p1 = ps_next()
                            op("pe", "matmul", p1.t[:, :cn], ones_f, SQ.t[:, c0:c0 + cn], start=True, stop=True,
                               r=[SQ, cst], w=[p1])
                            op("dve", "tensor_scalar", SQ.t[:, c0:c0 + cn], p1.t[:, :cn], EPS, None, ALU.add, r=[p1], w=[SQ])
                        op("act", "activation", SQ.t[:], SQ.t[:], AF.Sqrt, r=[SQ], w=[SQ])
                        op("dve", "reciprocal", SQ.t[:], SQ.t[:], r=[SQ], w=[SQ])
                        if b < 16:
                            op("dve", "scalar_tensor_tensor", XN.t[:], XS.t[:], 128.0 ** -0.5, SQ.t[:], ALU.mult, ALU.mult,
                               r=[XS, SQ], w=[XN])
                        else:
                            op("dve", "tensor_tensor", XN.t[:], XS.t[:], SQ.t[:], ALU.mult, r=[XS, SQ], w=[XN])
                    else:
                        op("pool", "tensor_copy", XN.t[:], XS.t[:], r=[XS], w=[XN])
                    if b < 16:
                        dma(gqnT[b * 128:(b + 1) * 128, :], XN.t[:], r=[XN], w=[scr["gqnT"]], append=True)
                    elif b < 32:
                        dma(gknT[(b - 16) * 128:(b - 15) * 128, :], XN.t[:], r=[XN], w=[scr["gknT"]], append=True)
                    if b >= 16:
                        dst, dn = (gk_tok, "gk_tok") if b < 32 else (gv_tok, "gv_tok")
                        hcol = (b - 16) * 128 if b < 32 else (b - 32) * 128
                        for g0 in range(0, T, 512):
                            gn = min(512, T - g0)
                            pb_ = psb_next()
                            tb = tmb[(g0 // 512) % 4]
                            nsub = (gn + 127) // 128
                            for s in range(nsub):
                                ns = min(128, gn - s * 128)
                                op("pe", "transpose", pb_.t[:ns, s * 128:(s + 1) * 128],
                                   XN.t[:, g0 + s * 128:g0 + s * 128 + ns], ident_b, r=[XN, cstb], w=[pb_])
                            nfull = gn // 128
                            if nfull:
                                op("act", "activation", tb.t[:, :nfull, :],
                                   pb_.t[:, 0:nfull * 128].rearrange("p (s c) -> p s c", s=nfull), AF.Copy, r=[pb_], w=[tb])
                                dma(dst[g0:g0 + nfull * 128, hcol:hcol + 128].rearrange("(s p) c -> p s c", p=128),
                                    tb.t[:, :nfull, :], r=[tb], w=[scr[dn]], append=True)
                            if gn % 128:
                                ns = gn % 128
                                op("act", "activation", tb.t[:ns, nfull, :], pb_.t[:ns, nfull * 128:(nfull + 1) * 128],
                                   AF.Copy, r=[pb_], w=[tb])
                                dma(dst[g0 + nfull * 128:g0 + gn, hcol:hcol + 128], tb.t[:ns, nfull, :], r=[tb],
                                    w=[scr[dn]], append=True)
                dma(o_pgc[l * 128:(l + 1) * 128, :].rearrange("p (b j) -> p b j", b=48), cvo.t[:, :, 0:3], r=[cvo],
                    w=[scr["out"]], append=True)
                dma(o_sgc[l * 128:(l + 1) * 128, :].rearrange("p (b j) -> p b j", b=48), cvo.t[:, :, 3:3 + 3 * NS],
                    r=[cvo], w=[scr["out"]], append=True)

            chunks = [(0, 16)] + [(16 + 64 * j, 64) for j in range(SEQ // 64)]

            with ExitStack() as ph:
                ph.callback(kb.barrier)
                Cx = sb("mCx", [128, 4, 256], st=ph)
                Cn = sb("mCn", [128, 4], st=ph)
                Cxb = sb("mCxb", [128, 4, 256], BF16, st=ph)
                Cnb = sb("mCnb", [128, 4], BF16, st=ph)
                mbc = sb("mmbc", [128, 4], st=ph)
                wn = sb("mwn", [128, 1024], st=ph)
                dma(wn.t[:], pb_d[0:1, pbl + 8:pbl + 8 + 1024].partition_broadcast(128), w=[wn])
                IN = [dict(qT=sb("mqT%d" % i, [128, 4, 64], BF16, st=ph), kT=sb("mkT%d" % i, [128, 4, 64], BF16, st=ph),
                           k=sb("mk%d" % i, [64, 4, 128], BF16, st=ph), v=sb("mv%d" % i, [64, 4, 256], BF16, st=ph),
                           o=sb("mo%d" % i, [64, 1024], BF16, st=ph), g=sb("mg%d" % i, [64, 8], st=ph)) for i in range(2)]
                sm_ = sb("msm", [128, 64], st=ph)
                Et = sb("mE", [64, 4, 64], st=ph)
                Ds = sb("mDs", [64, 4, 64], st=ph)
                Pm = sb("mPm", [64, 4, 64], st=ph)
                Sb_ = sb("mSb", [64, 4, 64], BF16, st=ph)
                STb = sb("mSTb", [64, 4, 64], BF16, st=ph)
                A2 = sb("mA2", [64, 256], st=ph)
                A2n = sb("mA2n", [64, 4], st=ph)
                Hh = sb("mHh", [64, 4, 256], st=ph)
                Hsq = sb("mHsq", [64, 256], st=ph)
                Yb = sb("mYb", [64, 1024], BF16, st=ph)
                YT = sb("mYT", [128, 8, 64], BF16, st=ph)
                kw_ = sb("mkw", [64, 4, 128], BF16, st=ph)
                onesv = sb("mones", [64, 1], BF16, st=ph)
                op("dve", "memset", onesv.t[:], 1.0, w=[onesv])
                nn = [0]

                def mlstm_chunk(t0, L):
                    I_ = IN[nn[0] % 2]
                    nn[0] += 1
                    qT_, kT_, k_, v_, o_, g_ = I_["qT"], I_["kT"], I_["k"], I_["v"], I_["o"], I_["g"]
                    dma(qT_.t[:, :, :L], mqT[:, t0:t0 + L].rearrange("(h d) t -> d h t", d=128), r=[scr["mqT"]], w=[qT_])
                    dma(kT_.t[:, :, :L], mkT[:, t0:t0 + L].rearrange("(h d) t -> d h t", d=128), r=[scr["mkT"]], w=[kT_])
                    dma(k_.t[:L, :, :], mk_tok[t0:t0 + L, :].rearrange("t (h d) -> t h d", d=128), r=[scr["mk_tok"]], w=[k_])
                    dma(v_.t[:L, :, :], mv_tok[t0:t0 + L, :].rearrange("t (h d) -> t h d", d=256), r=[scr["mv_tok"]], w=[v_])
                    dma(o_.t[:L, :], mo_tok[t0:t0 + L, :], r=[scr["mo_tok"]], w=[o_])
                    dma(g_.t[:L, :], mg_tok[t0:t0 + L, :], r=[scr["mg_tok"]], w=[g_])
                    li = g_.t[:L, 0:4]
                    lf = g_.t[:L, 4:8]
                    sel = cst.t[:L, {8: C_SEL8, 16: C_SEL16, 64: C_SEL64}[L]:][:, 0:128]
                    S = sm_.t
                    p1 = ps_next()
                    op("pe", "matmul", p1.t[:L, 0:4], Umat[:L, :L], lf, start=True, stop=True, r=[cst, g_], w=[p1])
                    op("pe", "matmul", p1.t[:, 8:12], ones_f[:L, :], lf, start=True, stop=True, r=[cst, g_], w=[p1])
                    op("dve", "tensor_copy", S[:L, 0:4], p1.t[:L, 0:4], r=[p1], w=[sm_])
                    op("dve", "tensor_copy", S[:, 4:8], p1.t[:, 8:12], r=[p1], w=[sm_])
                    SLb = cst.t[:L, C_SL:C_SL + L].unsqueeze(1).broadcast_to([L, 4, L])
                    Ib = cst.t[:L, C_ID:C_ID + L].unsqueeze(1).broadcast_to([L, 4, L])
                    op("dve", "tensor_tensor", Et.t[:L, :, :L], SLb, lf.unsqueeze(2).broadcast_to([L, 4, L]), ALU.mult,
                       r=[cst, g_], w=[Et])
                    op("dve", "tensor_tensor", Ds.t[:L, :, :L], Ib, li.unsqueeze(2).broadcast_to([L, 4, L]), ALU.mult,
                       r=[cst, g_], w=[Ds])
                    op("dve", "tensor_tensor", Et.t[:L, :, :L], Et.t[:L, :, :L], Ds.t[:L, :, :L], ALU.add, r=[Et, Ds], w=[Et])
                    p2 = ps_next()
                    for h in range(4):
                        op("pe", "matmul", p2.t[:L, h * 64:h * 64 + L], Umat[:L, :L], Et.t[:L, h, :L], start=True, stop=True,
                           r=[cst, Et], w=[p2])
                    negm = cst.t[:L, C_NEGM:C_NEGM + L].unsqueeze(1).broadcast_to([L, 4, L])
                    p2v = p2.t[:L, 0:256].rearrange("p (h s) -> p h s", h=4)[:, :, :L]
                    op("dve", "tensor_tensor", Ds.t[:L, :, :L], p2v, negm, ALU.add, r=[p2, cst], w=[Ds])
                    op("dve", "tensor_reduce", S[:L, 8:12], Ds.t[:L, :, :L], AX.X, ALU.max, r=[Ds], w=[sm_])
                    op("dve", "tensor_tensor", S[:L, 12:16], mbc.t[:L, :], S[:L, 0:4], ALU.add, r=[mbc, sm_], w=[sm_])
                    op("dve", "tensor_tensor", S[:L, 16:20], S[:L, 8:12], S[:L, 12:16], ALU.max, r=[sm_], w=[sm_])
                    op("dve", "tensor_scalar", S[:L, 20:24], S[:L, 16:20], -1.0, None, ALU.mult, r=[sm_], w=[sm_])
                    for h in range(4):
                        op("act", "activation", Pm.t[:L, h, :L], Ds.t[:L, h, :L], AF.Exp, bias=S[:L, 20 + h:21 + h],
                           r=[Ds, sm_], w=[Pm])
                    op("dve", "tensor_tensor", S[:L, 24:28], S[:L, 12:16], S[:L, 16:20], ALU.subtract, r=[sm_], w=[sm_])
                    op("act", "activation", S[:L, 24:28], S[:L, 24:28], AF.Exp, r=[sm_], w=[sm_])
                    op("act", "activation", S[:L, 28:32], S[:L, 20:24], AF.Exp, r=[sm_], w=[sm_])
                    p3 = ps_next()
                    for h in range(4):
                        op("pe", "matmul", p3.t[:L, h * 64:h * 64 + L], qT_.t[:, h, :L], kT_.t[:, h, :L], start=True,
                           stop=True, r=[qT_, kT_], w=[p3])
                    p3v = p3.t[:L, 0:256].rearrange("p (h s) -> p h s", h=4)[:, :, :L]
                    op("dve", "tensor_tensor", Sb_.t[:L, :, :L], p3v, Pm.t[:L, :, :L], ALU.mult, r=[p3, Pm], w=[Sb_])
                    pb_ = psb_next()
                    for h in range(4):
                        op("pe", "transpose", pb_.t[:L, h * 64:h * 64 + L], Sb_.t[:L, h, :L], ident_b[:L, :L],
                           r=[Sb_, cstb], w=[pb_])
                    op("act", "activation", STb.t[:L, :, :L], pb_.t[:L, 0:256].rearrange("p (h s) -> p h s", h=4)[:, :, :L],
                       AF.Copy, r=[pb_], w=[STb])
                    p4 = ps_next()
                    op("pe", "matmul", p4.t[:, 0:4], sel, S[:L, 16:20], start=True, stop=True, r=[cst, sm_], w=[p4])
                    op("dve", "tensor_copy", S[:, 32:36], p4.t[:, 0:4], r=[p4], w=[sm_])
                    op("dve", "tensor_tensor", S[:L, 36:40], S[:L, 4:8], S[:L, 0:4], ALU.subtract, r=[sm_], w=[sm_])
                    op("dve", "tensor_tensor", S[:L, 36:40], S[:L, 36:40], li, ALU.add, r=[sm_, g_], w=[sm_])
                    op("dve", "tensor_tensor", S[:L, 36:40], S[:L, 36:40], S[:L, 32:36], ALU.subtract, r=[sm_], w=[sm_])
                    op("act", "activation", S[:L, 36:40], S[:L, 36:40], AF.Exp, r=[sm_], w=[sm_])
                    op("dve", "tensor_tensor", S[:, 40:44], mbc.t[:, :], S[:, 4:8], ALU.add, r=[mbc, sm_], w=[sm_])
                    op("dve", "tensor_tensor", S[:, 40:44], S[:, 40:44], S[:, 32:36], ALU.subtract, r=[sm_], w=[sm_])
                    op("act", "activation", S[:, 40:44], S[:, 40:44], AF.Exp, r=[sm_], w=[sm_])
                    for h in range(4):
                        pa = ps_next()
                        op("pe", "matmul", pa.t[:L, 0:256], qT_.t[:, h, :L], Cxb.t[:, h, :], start=True, stop=True,
                           r=[qT_, Cxb], w=[pa])
                        op("pe", "matmul", pa.t[:L, 256:257], qT_.t[:, h, :L], Cnb.t[:, h:h + 1], start=True, stop=True,
                           r=[qT_, Cnb], w=[pa])
                        pc = ps_next()
                        op("pe", "matmul", pc.t[:L, 0:256], STb.t[:L, h, :L], v_.t[:L, h, :], start=True, stop=True,
                           r=[STb, v_], w=[pc])
                        op("pe", "matmul", pc.t[:L, 256:257], STb.t[:L, h, :L], onesv.t[:L, :], start=True, stop=True,
                           r=[STb, onesv], w=[pc])
                        op("act", "activation", A2.t[:L, :], pc.t[:L, 0:256], AF.Copy, r=[pc], w=[A2])
                        op("act", "activation", A2n.t[:L, h:h + 1], pc.t[:L, 256:257], AF.Copy, r=[pc], w=[A2n])
                        op("dve", "scalar_tensor_tensor", Hh.t[:L, h, :], pa.t[:L, 0:256], S[:L, 24 + h:25 + h], A2.t[:L, :],
                           ALU.mult, ALU.add, r=[pa, sm_, A2], w=[Hh])
                        op("dve", "scalar_tensor_tensor", S[:L, 44 + h:45 + h], pa.t[:L, 256:257], S[:L, 24 + h:25 + h],
                           A2n.t[:L, h:h + 1], ALU.mult, ALU.add, r=[pa, sm_, A2n], w=[sm_])
                    op("act", "activation", S[:L, 44:48], S[:L, 44:48], AF.Abs, r=[sm_], w=[sm_])
                    op("dve", "tensor_tensor", S[:L, 44:48], S[:L, 44:48], S[:L, 28:32], ALU.max, r=[sm_], w=[sm_])
                    op("dve", "reciprocal", S[:L, 44:48], S[:L, 44:48], r=[sm_], w=[sm_])
                    for h in range(4):
                        op("dve", "tensor_scalar", Hh.t[:L, h, :], Hh.t[:L, h, :], S[:L, 44 + h:45 + h], None, ALU.mult,
                           r=[Hh, sm_], w=[Hh])
                        op("act", "activation", Hsq.t[:L, :], Hh.t[:L, h, :], AF.Square, accum_out=S[:L, 48 + h:49 + h],
                           r=[Hh], w=[Hsq, sm_])
                    op("dve", "tensor_scalar", S[:L, 48:52], S[:L, 48:52], 1.0 / 256, EPS, ALU.mult, ALU.add, r=[sm_], w=[sm_])
                    op("act", "activation", S[:L, 48:52], S[:L, 48:52], AF.Sqrt, r=[sm_], w=[sm_])
                    op("dve", "reciprocal", S[:L, 48:52], S[:L, 48:52], r=[sm_], w=[sm_])
                    for h in range(4):
                        op("dve", "scalar_tensor_tensor", Hh.t[:L, h, :], Hh.t[:L, h, :], S[:L, 48 + h:49 + h],
                           wn.t[:L, h * 256:(h + 1) * 256], ALU.mult, ALU.mult, r=[Hh, sm_, wn], w=[Hh])
                    op("dve", "tensor_tensor", Yb.t[:L, :], Hh.t[:L, :, :].rearrange("p h e -> p (h e)"), o_.t[:L, :],
                       ALU.mult, r=[Hh, o_], w=[Yb])
                    pb2 = psb_next()
                    for j in range(8):
                        op("pe", "transpose", pb2.t[:, j * 64:j * 64 + L], Yb.t[:L, j * 128:(j + 1) * 128], ident_b[:L, :L],
                           r=[Yb, cstb], w=[pb2])
                    op("act", "activation", YT.t[:, :, :L], pb2.t[:, 0:512].rearrange("p (j t) -> p j t", j=8)[:, :, :L],
                       AF.Copy, r=[pb2], w=[YT])
                    dma(mixT[0:1024, t0:t0 + L].rearrange("(j p) t -> p j t", p=128), YT.t[:, :, :L], r=[YT],
                        w=[scr["mixT"]], append=True)
                    op("dve", "tensor_tensor", kw_.t[:L, :, :], k_.t[:L, :, :],
                       S[:L, 36:40].unsqueeze(2).broadcast_to([L, 4, 128]), ALU.mult, r=[k_, sm_], w=[kw_])
                    for h in range(4):
                        pd = ps_next()
                        op("pe", "matmul", pd.t[:, 0:256], kw_.t[:L, h, :], v_.t[:L, h, :], start=True, stop=True,
                           r=[kw_, v_], w=[pd])
                        op("pe", "matmul", pd.t[:, 256:257], kw_.t[:L, h, :], onesv.t[:L, :], start=True, stop=True,
                           r=[kw_, onesv], w=[pd])
                        op("dve", "scalar_tensor_tensor", Cx.t[:, h, :], Cx.t[:, h, :], S[:, 40 + h:41 + h], pd.t[:, 0:256],
                           ALU.mult, ALU.add, r=[Cx, sm_, pd], w=[Cx])
                        op("dve", "scalar_tensor_tensor", Cn.t[:, h:h + 1], Cn.t[:, h:h + 1], S[:, 40 + h:41 + h],
                           pd.t[:, 256:257], ALU.mult, ALU.add, r=[Cn, sm_, pd], w=[Cn])
                    op("act", "activation", Cxb.t[:], Cx.t[:], AF.Copy, r=[Cx], w=[Cxb])
                    op("act", "activation", Cnb.t[:], Cn.t[:], AF.Copy, r=[Cn], w=[Cnb])
                    op("dve", "tensor_copy", mbc.t[:, :], S[:, 32:36], r=[sm_], w=[mbc])

                def m_store(oC, on, om, row):
                    dma(oC[row * 512:(row + 1) * 512, :].rearrange("(h k) v -> k h v", k=128), Cx.t[:], r=[Cx],
                        w=[scr["out"]], append=True)
                    dma(on[row * 128:(row + 1) * 128, :], Cn.t[:], r=[Cn], w=[scr["out"]], append=True)
                    dma(om[row:row + 1, :], mbc.t[0:1, :], r=[mbc], w=[scr["out"]], append=True)

                op("dve", "memset", Cx.t[:], 0.0, w=[Cx])
                op("dve", "memset", Cn.t[:], 0.0, w=[Cn])
                op("dve", "memset", mbc.t[:], 0.0, w=[mbc])
                op("dve", "memset", Cxb.t[:], 0.0, w=[Cxb])
                op("dve", "memset", Cnb.t[:], 0.0, w=[Cnb])
                for (t0, L) in chunks:
                    mlstm_chunk(t0, L)
                m_store(o_pC, o_pn, o_pm, l)
                for s in range(NS):
                    row = l * NS + s
                    dma(Cx.t[:], sC_d[row * 512:(row + 1) * 512, :].rearrange("(h k) v -> k h v", k=128), w=[Cx])
                    dma(Cn.t[:], sn_d[row * 128:(row + 1) * 128, :], w=[Cn])
                    dma(mbc.t[:], sm_d[0:1, row * 4:row * 4 + 4].partition_broadcast(128), w=[mbc])
                    op("act", "activation", Cxb.t[:], Cx.t[:], AF.Copy, r=[Cx], w=[Cxb])
                    op("act", "activation", Cnb.t[:], Cn.t[:], AF.Copy, r=[Cn], w=[Cnb])
                    mlstm_chunk(TP + 8 * s, 8)
                    m_store(o_sC, o_sn, o_sm, row)

            with ExitStack() as ph:
                ph.callback(kb.barrier)
                St = sb("gS", [128, 16, 128], st=ph)
                Stb = sb("gSb", [128, 16, 128], BF16, st=ph)
                gwn = sb("ggwn", [128, 128], st=ph)
                dma(gwn.t[:], pb_d[0:1, pbl + 8 + 1024 + 32:pbl + 8 + 1024 + 32 + 128].partition_broadcast(128), w=[gwn])
                IN = [dict(qT=sb("gqT%d" % i, [128, 16, 64], BF16, st=ph), kT=sb("gkT%d" % i, [128, 16, 64], BF16, st=ph),
                           k=sb("gk%d" % i, [64, 16, 128], BF16, st=ph), v=sb("gv%d" % i, [64, 16, 128], BF16, st=ph),
                           z=sb("gz%d" % i, [64, 2048], BF16, st=ph), g=sb("gg%d" % i, [64, 48], st=ph)) for i in range(2)]
                sm_ = sb("gsm", [128, 128], st=ph)
                Et = sb("gE", [64, 16, 64], st=ph)
                E2 = sb("gE2", [64, 16, 64], st=ph)
                xB2 = [sb("gxB%d" % i, [64, 8, 64], st=ph) for i in range(2)]
                xC2 = [sb("gxC%d" % i, [64, 8, 64], st=ph) for i in range(2)]
                Bk2 = [[sb("gBk%d%d" % (g, i), [64, 8, 64], BF16, st=ph) for i in range(2)] for g in range(2)]
                Ck2 = [[sb("gCk%d%d" % (g, i), [64, 8, 64], BF16, st=ph) for i in range(2)] for g in range(2)]
                Qf2 = [sb("gQf%d" % i, [64, 8, 64], st=ph) for i in range(2)]
                Qb2 = [sb("gQb%d" % i, [64, 8, 64], BF16, st=ph) for i in range(2)]
                AT2 = [sb("gAT%d" % i, [64, 8, 64], BF16, st=ph) for i in range(2)]
                R0w2 = [sb("gR0w%d" % i, [64, 8, 128], BF16, st=ph) for i in range(2)]
                YwT2 = [sb("gYwT%d" % i, [128, 8, 64], BF16, st=ph) for i in range(2)]
                vn2 = [sb("gvn%d" % i, [64, 8, 128], BF16, st=ph) for i in range(2)]
                O22 = [sb("gO2%d" % i, [64, 8, 128], st=ph) for i in range(2)]
                Oo = sb("gOo", [64, 16, 128], st=ph)
                Osq = sb("gOsq", [64, 16, 128], st=ph)
                Yb = sb("gYb", [64, 2048], BF16, st=ph)
                YT = sb("gYT", [128, 16, 64], BF16, st=ph)
                kd2 = [sb("gkd%d" % i, [64, 8, 128], BF16, st=ph) for i in range(2)]
                nn = [0]

                def gdn_chunk(t0, L):
                    I_ = IN[nn[0] % 2]
                    nn[0] += 1
                    qT_, kT_, k_, v_, z_, g_ = I_["qT"], I_["kT"], I_["k"], I_["v"], I_["z"], I_["g"]
                    dma(qT_.t[:, :, :L], gqnT[:, t0:t0 + L].rearrange("(h d) t -> d h t", d=128), r=[scr["gqnT"]], w=[qT_])
                    dma(kT_.t[:, :, :L], gknT[:, t0:t0 + L].rearrange("(h d) t -> d h t", d=128), r=[scr["gknT"]], w=[kT_])
                    dma(k_.t[:L, :, :], gk_tok[t0:t0 + L, :].rearrange("t (h d) -> t h d", d=128), r=[scr["gk_tok"]], w=[k_])
                    dma(v_.t[:L, :, :], gv_tok[t0:t0 + L, :].rearrange("t (h d) -> t h d", d=128), r=[scr["gv_tok"]], w=[v_])
                    dma(z_.t[:L, :], gz_tok[t0:t0 + L, :], r=[scr["gz_tok"]], w=[z_])
                    dma(g_.t[:L, :], gg_tok[t0:t0 + L, :], r=[scr["gg_tok"]], w=[g_])
                    beta = g_.t[:L, 0:16]
                    lnb = g_.t[:L, 16:32]
                    gg = g_.t[:L, 32:48]
                    S = sm_.t
                    p1 = ps_next()
                    op("pe", "matmul", p1.t[:L, 0:16], Umat[:L, :L], gg, start=True, stop=True, r=[cst, g_], w=[p1])
                    op("pe", "matmul", p1.t[:, 16:32], ones_f[:L, :], gg, start=True, stop=True, r=[cst, g_], w=[p1])
                    op("dve", "tensor_copy", S[:L, 0:16], p1.t[:L, 0:16], r=[p1], w=[sm_])
                    op("dve", "tensor_copy", S[:, 16:32], p1.t[:, 16:32], r=[p1], w=[sm_])
                    op("act", "activation", S[:L, 32:48], S[:L, 0:16], AF.Exp, r=[sm_], w=[sm_])
                    op("act", "activation", S[:, 48:64], S[:, 16:32], AF.Exp, r=[sm_], w=[sm_])
                    op("dve", "tensor_tensor", S[:L, 64:80], S[:L, 16:32], S[:L, 0:16], ALU.subtract, r=[sm_], w=[sm_])
                    op("act", "activation", S[:L, 64:80], S[:L, 64:80], AF.Exp, r=[sm_], w=[sm_])
                    op("dve", "tensor_tensor", S[:L, 64:80], S[:L, 64:80], beta, ALU.mult, r=[sm_, g_], w=[sm_])
                    SLb = cst.t[:L, C_SL:C_SL + L].unsqueeze(1).broadcast_to([L, 16, L])
                    Ib = cst.t[:L, C_ID:C_ID + L].unsqueeze(1).broadcast_to([L, 16, L])
                    op("dve", "tensor_tensor", Et.t[:L, :, :L], SLb, gg.unsqueeze(2).broadcast_to([L, 16, L]), ALU.mult,
                       r=[cst, g_], w=[Et])
                    op("dve", "tensor_tensor", E2.t[:L, :, :L], Ib, lnb.unsqueeze(2).broadcast_to([L, 16, L]), ALU.mult,
                       r=[cst, g_], w=[E2])
                    op("dve", "tensor_tensor", Et.t[:L, :, :L], Et.t[:L, :, :L], E2.t[:L, :, :L], ALU.add, r=[Et, E2], w=[Et])
                    mls = cst.t[:L, C_MLSN:C_MLSN + L].unsqueeze(1).broadcast_to([L, 8, L])
                    mus = cst.t[:L, C_MUSN:C_MUSN + L].unsqueeze(1).broadcast_to([L, 8, L])
                    mui = cst.t[:L, C_MUI:C_MUI + L].unsqueeze(1).broadcast_to([L, 8, L])
                    idb = cst.t[:L, C_ID:C_ID + L].unsqueeze(1).broadcast_to([L, 8, L])

                    def v8(pt):
                        return pt.t[:L, 0:512].rearrange("p (h s) -> p h s", h=8)[:, :, :L]

                    def hg_body(hg):
                        xB, xC, Bk, Ck, Qf, Qb, AT = xB2[hg], xC2[hg], Bk2[hg], Ck2[hg], Qf2[hg], Qb2[hg], AT2[hg]
                        R0w, YwT, vn, O2, kd = R0w2[hg], YwT2[hg], vn2[hg], O22[hg], kd2[hg]
                        h0_ = hg * 8
                        pB = ps_next()
                        for j in range(8):
                            op("pe", "matmul", pB.t[:L, j * 64:j * 64 + L], Umat[:L, :L], Et.t[:L, h0_ + j, :L], start=True,
                               stop=True, r=[cst, Et], w=[pB])
                        op("act", "activation", xB.t[:L, :, :L], v8(pB), AF.Exp, r=[pB], w=[xB])
                        pC = ps_next()
                        for j in range(8):
                            op("pe", "matmul", pC.t[:L, j * 64:j * 64 + L], Et.t[:L, h0_ + j, :L], Umat[:L, :L], start=True,
                               stop=True, r=[cst, Et], w=[pC])
                        op("act", "activation", xC.t[:L, :, :L], v8(pC), AF.Exp, r=[pC], w=[xC])
                        yield
                        pM = ps_next()
                        for j in range(8):
                            op("pe", "matmul", pM.t[:L, j * 64:j * 64 + L], kT_.t[:, h0_ + j, :L], kT_.t[:, h0_ + j, :L],
                               start=True, stop=True, r=[kT_], w=[pM])
                        pK = ps_next()
                        for j in range(8):
                            op("pe", "matmul", pK.t[:L, j * 64:j * 64 + L], kT_.t[:, h0_ + j, :L], qT_.t[:, h0_ + j, :L],
                               start=True, stop=True, r=[kT_, qT_], w=[pK])
                        op("dve", "tensor_tensor", xB.t[:L, :, :L], v8(pM), xB.t[:L, :, :L], ALU.mult, r=[pM, xB], w=[xB])
                        op("dve", "tensor_tensor", Bk[0].t[:L, :, :L], xB.t[:L, :, :L], mls, ALU.mult, r=[xB, cst], w=[Bk[0]])
                        op("dve", "tensor_tensor", Qf.t[:L, :, :L], v8(pM), xC.t[:L, :, :L], ALU.mult, r=[pM, xC], w=[Qf])
                        op("dve", "tensor_tensor", Ck[0].t[:L, :, :L], Qf.t[:L, :, :L], mus, ALU.mult, r=[Qf, cst], w=[Ck[0]])
                        op("dve", "tensor_tensor", xC.t[:L, :, :L], v8(pK), xC.t[:L, :, :L], ALU.mult, r=[pK, xC], w=[xC])
                        op("dve", "tensor_tensor", AT.t[:L, :, :L], xC.t[:L, :, :L], mui, ALU.mult, r=[xC, cst], w=[AT])
                        op("dve", "tensor_tensor", Qf.t[:L, :, :L], Qf.t[:L, :, :L], mus, ALU.mult, r=[Qf, cst], w=[Qf])
                        op("dve", "tensor_tensor", Qf.t[:L, :, :L], Qf.t[:L, :, :L], idb, ALU.add, r=[Qf, cst], w=[Qf])
                        op("act", "activation", Qb.t[:L, :, :L], Qf.t[:L, :, :L], AF.Copy, r=[Qf], w=[Qb])
                        yield
                        mlev = 1
                        cur_ = 0
                        while 2 * mlev < L:
                            Bc, Cc, Bn, Cn_ = Bk[cur_], Ck[cur_], Bk[1 - cur_], Ck[1 - cur_]
                            pP = ps_next()
                            for j in range(8):
                                op("pe", "matmul", pP.t[:L, j * 64:j * 64 + L], Bc.t[:L, j, :L], Cc.t[:L, j, :L], start=True,
                                   stop=True, r=[Bc, Cc], w=[pP])
                            pQ = ps_next()
                            for j in range(8):
                                op("pe", "matmul", pQ.t[:L, j * 64:j * 64 + L], Cc.t[:L, j, :L], Bc.t[:L, j, :L], start=True,
                                   stop=True, r=[Bc, Cc], w=[pQ])
                            op("act", "activation", Cn_.t[:L, :, :L], v8(pP), AF.Copy, r=[pP], w=[Cn_])
                            op("dve", "tensor_copy", Bn.t[:L, :, :L], v8(pQ), r=[pQ], w=[Bn])
                            yield
                            cur_ = 1 - cur_
                            mlev *= 2
                            pR = ps_next()
                            for j in range(8):
                                op("pe", "matmul", pR.t[:L, j * 64:j * 64 + L], Bk[cur_].t[:L, j, :L], Qb.t[:L, j, :L],
                                   start=True, stop=True, r=[Bk[cur_], Qb], w=[pR])
                            op("dve", "tensor_tensor", Qf.t[:L, :, :L], Qf.t[:L, :, :L], v8(pR), ALU.add, r=[Qf, pR], w=[Qf])
                            op("act", "activation", Qb.t[:L, :, :L], Qf.t[:L, :, :L], AF.Copy, r=[Qf], w=[Qb])
                            yield
                        op("dve", "tensor_tensor", R0w.t[:L, :, :], k_.t[:L, h0_:h0_ + 8, :],
                           S[:L, 32 + h0_:40 + h0_].unsqueeze(2).broadcast_to([L, 8, 128]), ALU.mult, r=[k_, sm_], w=[R0w])
                        pY = ps_next()
                        for j in range(8):
                            op("pe", "matmul", pY.t[:, j * 64:j * 64 + L], R0w.t[:L, j, :], Qb.t[:L, j, :L], start=True,
                               stop=True, r=[R0w, Qb], w=[pY])
                        op("dve", "tensor_scalar", YwT.t[:, :, :L], pY.t[:, 0:512].rearrange("p (h s) -> p h s", h=8)[:, :, :L],
                           -1.0, None, ALU.mult, r=[pY], w=[YwT])
                        yield
                        pv = [ps_next(), ps_next()]
                        for j in range(8):
                            dst_ = pv[j // 4].t[:L, (j % 4) * 128:(j % 4 + 1) * 128]
                            op("pe", "matmul", dst_, Qb.t[:L, j, :L], v_.t[:L, h0_ + j, :], start=True, stop=False,
                               r=[Qb, v_], w=[pv[j // 4]])
                            op("pe", "matmul", dst_, YwT.t[:, j, :L], Stb.t[:, h0_ + j, :], start=False, stop=True,
                               r=[YwT, Stb], w=[pv[j // 4]])
                        for q_ in range(2):
                            op("act", "activation", vn.t[:L, q_ * 4:(q_ + 1) * 4, :],
                               pv[q_].t[:L, :].rearrange("p (h e) -> p h e", h=4), AF.Copy, r=[pv[q_]], w=[vn])
                        yield
                        po1 = [ps_next(), ps_next()]
                        for j in range(8):
                            op("pe", "matmul", po1[j // 4].t[:L, (j % 4) * 128:(j % 4 + 1) * 128], AT.t[:L, j, :L],
                               vn.t[:L, j, :], start=True, stop=True, r=[AT, vn], w=[po1[j // 4]])
                        for q_ in range(2):
                            op("act", "activation", O2.t[:L, q_ * 4:(q_ + 1) * 4, :],
                               po1[q_].t[:L, :].rearrange("p (h e) -> p h e", h=4), AF.Copy, r=[po1[q_]], w=[O2])
                        po2 = [ps_next(), ps_next()]
                        for j in range(8):
                            op("pe", "matmul", po2[j // 4].t[:L, (j % 4) * 128:(j % 4 + 1) * 128], qT_.t[:, h0_ + j, :L],
                               Stb.t[:, h0_ + j, :], start=True, stop=True, r=[qT_, Stb], w=[po2[j // 4]])
                        for q_ in range(2):
                            op("dve", "tensor_tensor", Oo.t[:L, h0_ + q_ * 4:h0_ + q_ * 4 + 4, :],
                               po2[q_].t[:L, :].rearrange("p (h e) -> p h e", h=4),
                               S[:L, 32 + h0_ + q_ * 4:32 + h0_ + q_ * 4 + 4].unsqueeze(2).broadcast_to([L, 4, 128]),
                               ALU.mult, r=[po2[q_], sm_], w=[Oo])
                        op("dve", "tensor_tensor", Oo.t[:L, h0_:h0_ + 8, :], Oo.t[:L, h0_:h0_ + 8, :], O2.t[:L, :, :], ALU.add,
                           r=[Oo, O2], w=[Oo])
                        yield
                        op("dve", "tensor_tensor", kd.t[:L, :, :], k_.t[:L, h0_:h0_ + 8, :],
                           S[:L, 64 + h0_:72 + h0_].unsqueeze(2).broadcast_to([L, 8, 128]), ALU.mult, r=[k_, sm_], w=[kd])
                        pS_ = [ps_next(), ps_next()]
                        for j in range(8):
                            op("pe", "matmul", pS_[j // 4].t[:, (j % 4) * 128:(j % 4 + 1) * 128], kd.t[:L, j, :], vn.t[:L, j, :],
                               start=True, stop=True, r=[kd, vn], w=[pS_[j // 4]])
                        for q_ in range(2):
                            hs_ = h0_ + q_ * 4
                            op("dve", "tensor_tensor", St.t[:, hs_:hs_ + 4, :], St.t[:, hs_:hs_ + 4, :],
                               S[:, 48 + hs_:52 + hs_].unsqueeze(2).broadcast_to([128, 4, 128]), ALU.mult, r=[St, sm_], w=[St])
                            op("dve", "tensor_tensor", St.t[:, hs_:hs_ + 4, :], St.t[:, hs_:hs_ + 4, :],
                               pS_[q_].t[:, :].rearrange("p (h e) -> p h e", h=4), ALU.add, r=[St, pS_[q_]], w=[St])

                    gens = [hg_body(0), hg_body(1)]
                    while gens:
                        for g_ in list(gens):
                            try:
                                next(g_)
                            except StopIteration:
                                gens.remove(g_)
                    op("act", "activation", Stb.t[:], St.t[:], AF.Copy, r=[St], w=[Stb])
                    op("act", "activation", Osq.t[:L, :, :], Oo.t[:L, :, :], AF.Square, r=[Oo], w=[Osq])
                    op("dve", "tensor_reduce", S[:L, 80:96], Osq.t[:L, :, :], AX.X, ALU.add, r=[Osq], w=[sm_])
                    op("dve", "tensor_scalar", S[:L, 80:96], S[:L, 80:96], 1.0 / 128, EPS, ALU.mult, ALU.add, r=[sm_], w=[sm_])
                    op("act", "activation", S[:L, 80:96], S[:L, 80:96], AF.Sqrt, r=[sm_], w=[sm_])
                    op("dve", "reciprocal", S[:L, 80:96], S[:L, 80:96], r=[sm_], w=[sm_])
                    op("dve", "tensor_tensor", Oo.t[:L, :, :], Oo.t[:L, :, :],
                       S[:L, 80:96].unsqueeze(2).broadcast_to([L, 16, 128]), ALU.mult, r=[Oo, sm_], w=[Oo])
                    op("dve", "tensor_tensor", Oo.t[:L, :, :], Oo.t[:L, :, :],
                       gwn.t[:L, :].unsqueeze(1).broadcast_to([L, 16, 128]), ALU.mult, r=[Oo, gwn], w=[Oo])
                    op("dve", "tensor_tensor", Yb.t[:L, :], Oo.t[:L, :, :].rearrange("p h e -> p (h e)"), z_.t[:L, :], ALU.mult,
                       r=[Oo, z_], w=[Yb])
                    for q_ in range(2):
                        pb2 = psb_next()
                        for j in range(8):
                            jj = q_ * 8 + j
                            op("pe", "transpose", pb2.t[:, j * 64:j * 64 + L], Yb.t[:L, jj * 128:(jj + 1) * 128],
                               ident_b[:L, :L], r=[Yb, cstb], w=[pb2])
                        op("act", "activation", YT.t[:, q_ * 8:(q_ + 1) * 8, :L],
                           pb2.t[:, 0:512].rearrange("p (j t) -> p j t", j=8)[:, :, :L], AF.Copy, r=[pb2], w=[YT])
                    dma(mixT[2048:4096, t0:t0 + L].rearrange("(j p) t -> p j t", p=128), YT.t[:, :, :L], r=[YT],
                        w=[scr["mixT"]], append=True)

                op("dve", "memset", St.t[:], 0.0, w=[St])
                op("dve", "memset", Stb.t[:], 0.0, w=[Stb])
                for (t0, L) in chunks:
                    gdn_chunk(t0, L)
                dma(o_pS[l * 2048:(l + 1) * 2048, :].rearrange("(h d) e -> d h e", d=128), St.t[:], r=[St], w=[scr["out"]],
                    append=True)
                for s in range(NS):
                    row = l * NS + s
                    dma(St.t[:], sS_d[row * 2048:(row + 1) * 2048, :].rearrange("(h d) e -> d h e", d=128), w=[St])
                    op("act", "activation", Stb.t[:], St.t[:], AF.Copy, r=[St], w=[Stb])
                    gdn_chunk(TP + 8 * s, 8)
                    dma(o_sS[row * 2048:(row + 1) * 2048, :].rearrange("(h d) e -> d h e", d=128), St.t[:], r=[St],
                        w=[scr["out"]], append=True)

            xsel = {}

            def h_res_pre(tag, c0, ncol, ts_list):
                xo_, so_, _ = tag
                k = 0
                for cb in range((ncol + 127) // 128):
                    m = min(128, ncol - cb * 128)
                    c = c0 + cb * 128
                    for ti, (off, n) in enumerate(ts_list):
                        i = k % len(xo_)
                        k += 1
                        tg0 = cur["t0"] + off
                        dma(xo_[i].t[:m, :n], xT[c:c + m, tg0:tg0 + n], r=[XT_B[c // 128]], w=[xo_[i]])
                        xsel[(c, ti)] = i

            def h_res(tag, c, m, ti, off, n, acc, acc_ap):
                xo_, so_, cur_t0 = tag
                i = xsel[(c, ti)]
                tg0 = cur["t0"] + off
                kcb = c // 128
                op("dve", "tensor_tensor", so_[i].t[:m, :n], acc_ap, xo_[i].t[:m, :n], ALU.add, r=[acc, xo_[i]], w=[so_[i]])
                dma(xT[c:c + m, tg0:tg0 + n], so_[i].t[:m, :n], r=[so_[i]], w=[XT_B[kcb]], append=True)

            hr = [0]
            cur = {}
            with ExitStack() as ph:
                ph.callback(kb.barrier)
                actT = sb("actTc", [128, KC, 736], BF16, st=ph)
                lb = make_lin_bufs(ph)
                xo_ = [sb("cxo%d" % i, [128, 512], st=ph) for i in range(4)]
                so_ = [sb("cso%d" % i, [128, 512], st=ph) for i in range(4)]
                segs = [(w_out[l * D:(l + 1) * D, :], c0, 256, h_res, (xo_, so_, None), h_res_pre) for c0 in range(0, D, 256)]
                for (t0, n, ts) in tok_tiles(736):
                    cur["t0"] = t0
                    dma(actT.t[:, :, :n], mixT[:, t0:t0 + n].rearrange("(k p) t -> p k t", p=128), r=[scr["mixT"]], w=[actT])
                    linear(lb, actT, KC, ts, segs)

            with ExitStack() as ph:
                ph.callback(kb.barrier)
                actT = sb("actTd", [128, KC, 736], BF16, st=ph)
                nb = make_norm_bufs(ph)
                lb = make_lin_bufs(ph)
                sg = {}
                for cb in range(2):
                    for ti in range(2):
                        sg[(cb, ti)] = sb("dsg%d%d" % (cb, ti), [128, 512], st=ph)
                ab = [sb("dab%d" % i, [128, 512], BF16, st=ph) for i in range(3)]
                an = [0]

                def h_ffn(tag, c, m, ti, off, n, acc, acc_ap):
                    kind, c0 = tag
                    cb = (c - c0) // 128
                    s_ = sg[(cb, ti)]
                    if kind == "g":
                        op("act", "activation", s_.t[:m, :n], acc_ap, AF.Silu, r=[acc], w=[s_])
                    else:
                        a_ = ab[an[0] % 3]
                        an[0] += 1
                        op("dve", "tensor_tensor", a_.t[:m, :n], acc_ap, s_.t[:m, :n], ALU.mult, r=[acc, s_], w=[a_])
                        tg0 = cur["t0"] + off
                        dma(aT[c:c + m, tg0:tg0 + n], a_.t[:m, :n], r=[a_], w=[scr["aT"]], append=True)

                for (t0, n, ts) in tok_tiles(736):
                    cur["t0"] = t0
                    norm_to_act(nb, PP_NFFN + l * 32, t0, n, actT)
                    segs = []
                    for c0 in range(0, DFF, 256):
                        cn = min(256, DFF - c0)
                        segs.append((w_gate[l * D:(l + 1) * D, :], c0, cn, h_ffn, ("g", c0)))
                        segs.append((w_up[l * D:(l + 1) * D, :], c0, cn, h_ffn, ("u", c0)))
                    linear(lb, actT, KC, ts, segs)

            with ExitStack() as ph:
                ph.callback(kb.barrier)
                actT = sb("actTe", [128, KF, 736], BF16, st=ph)
                lb = make_lin_bufs(ph)
                xo_ = [sb("exo%d" % i, [128, 512], st=ph) for i in range(4)]
                so_ = [sb("eso%d" % i, [128, 512], st=ph) for i in range(4)]
                segs = [(w_down[l * DFF:(l + 1) * DFF, :], c0, 256, h_res, (xo_, so_, None), h_res_pre) for c0 in range(0, D, 256)]
                for (t0, n, ts) in tok_tiles(736):
                    cur["t0"] = t0
                    dma(actT.t[:, :, :n], aT[:, t0:t0 + n].rearrange("(k p) t -> p k t", p=128), r=[scr["aT"]], w=[actT])
                    linear(lb, actT, KF, ts, segs)

        with ExitStack() as ph:
            ph.callback(kb.barrier)
            nb = make_norm_bufs(ph)
            yT_ = sb("fyT", [128, KC, 256], st=ph)
            yo = [sb("fyo%d" % i, [128, D], st=ph) for i in range(2)]
            yn = [0]
            for t0 in range(0, T, 256):
                n = min(256, T - t0)

                def dst(kc, xt, wn_, rs, m):
                    op("dve", "scalar_tensor_tensor", yT_.t[:, kc, :m], xt.t[:, kc, :m], wn_, rs.t[:, :m], ALU.mult, ALU.mult,
                       r=[xt, rs, pp], w=[yT_])
                norm_piece(nb, PP_NFIN, t0, n, dst)
                for s0 in range(0, n, 128):
                    ns = min(128, n - s0)
                    o = yo[yn[0] % 2]
                    yn[0] += 1
                    for g in range(8):
                        pt = ps_next()
                        for j in range(4):
                            kc = g * 4 + j
                            op("pe", "transpose", pt.t[:ns, j * 128:(j + 1) * 128], yT_.t[:, kc, s0:s0 + ns], ident_f,
                               r=[yT_, cst], w=[pt])
                        if g % 2 == 0:
                            op("dve", "tensor_copy", o.t[:ns, g * 512:(g + 1) * 512], pt.t[:ns, :], r=[pt], w=[o])
                        else:
                            op("act", "activation", o.t[:ns, g * 512:(g + 1) * 512], pt.t[:ns, :], AF.Copy, r=[pt], w=[o])
                    dma(y_d[t0 + s0:t0 + s0 + ns, :], o.t[:ns, :], r=[o], w=[scr["y"]], append=True)
        kb.finish()
    return nc


def _pack_inputs(cfg, core, x_prompt, x_sample, st, meta_tokens, P):
    SEQ, NS, DEPTH, DFF = cfg["SEQ"], cfg["NS"], cfg["DEPTH"], cfg["DFF"]
    f = np.float32
    s_idx = core // 2 if cfg.get("pair", True) else core
    xs = x_sample[core * NS:(core + 1) * NS].reshape(NS * 8, D)
    xin = np.concatenate([meta_tokens, x_prompt[s_idx], xs], axis=0).astype(f)
    sl = slice(core * NS, (core + 1) * NS)
    m = {"xin": np.ascontiguousarray(xin)}
    m["sC"] = np.ascontiguousarray(st["C"][:, sl]).reshape(-1, 256)
    m["sn"] = np.ascontiguousarray(st["n"][:, sl].transpose(0, 1, 3, 2)).reshape(-1, 4)
    m["sm"] = np.ascontiguousarray(st["m"][:, sl]).reshape(1, -1)
    m["sh"] = np.ascontiguousarray(st["h"][:, sl].reshape(DEPTH, NS, 8, 128).transpose(0, 3, 2, 1)).reshape(DEPTH * 128, -1)
    m["src"] = np.ascontiguousarray(
        st["rc"][:, sl].reshape(DEPTH, NS, 3, 8, 128).transpose(0, 4, 3, 1, 2)).reshape(DEPTH * 128, -1)
    m["sS"] = np.ascontiguousarray(st["S"][:, sl]).reshape(-1, 128)
    m["sgc"] = np.ascontiguousarray(
        st["gc"][:, sl].reshape(DEPTH, NS, 3, 48, 128).transpose(0, 4, 3, 1, 2)).reshape(DEPTH * 128, -1)
    return m


def _pack_shared(cfg, P):
    DEPTH, DFF = cfg["DEPTH"], cfg["DFF"]
    f = np.float32

    def pc(v, nblk):
        return np.asarray(v, f).reshape(nblk, 128).T

    cols = [pc(P["norm_mix"][l], 32) for l in range(DEPTH)] + [pc(P["norm_ffn"][l], 32) for l in range(DEPTH)]
    cols.append(pc(P["norm_final"], 32))
    for l in range(DEPTH):
        cols.append(np.asarray(P["r_conv_w"][l], f).reshape(4, 8, 128).transpose(2, 1, 0).reshape(128, 32))
        cols.append(pc(P["r_conv_b"][l], 8))
        cols.append(pc(P["r_gate_a_b"][l], 8))
        cols.append(pc(P["r_gate_x_b"][l], 8))
        cols.append(pc(P["r_lambda"][l], 8))
        cols.append(np.asarray(P["g_conv_w"][l], f).reshape(4, 48, 128).transpose(2, 1, 0).reshape(128, 192))
    pp = np.ascontiguousarray(np.concatenate(cols, axis=1), f)
    pbs = []
    for l in range(DEPTH):
        pbs += [P["m_bias_i"][l], P["m_bias_f"][l], P["m_norm"][l], P["g_A_log"][l], P["g_dt_bias"][l], P["g_norm"][l]]
    pb = np.ascontiguousarray(np.concatenate([np.asarray(a, f).reshape(-1) for a in pbs])[None, :], f)
    rgw = np.stack([np.asarray(P["r_gate_a_w"], f), np.asarray(P["r_gate_x_w"], f)], axis=1)
    sh = {
        "w_in": np.asarray(P["w_in"], f).reshape(-1, DIN),
        "w_out": np.asarray(P["w_out"], f).reshape(-1, D),
        "w_gate": np.asarray(P["w_gate"], f).reshape(-1, DFF),
        "w_up": np.asarray(P["w_up"], f).reshape(-1, DFF),
        "w_down": np.asarray(P["w_down"], f).reshape(-1, D),
        "pp": pp, "pb": pb, "rgw": np.ascontiguousarray(rgw.reshape(-1, 128)), "consts": make_consts(),
    }
    return sh


def run(cfg, n_cores, x_prompt, x_sample, st, meta_tokens, P):
    SEQ, NS, DEPTH, DFF = cfg["SEQ"], cfg["NS"], cfg["DEPTH"], cfg["DFF"]
    nc = build(cfg)
    shared = _pack_shared(cfg, P)
    in_maps = []
    for c in range(n_cores):
        m = _pack_inputs(cfg, c, x_prompt, x_sample, st, meta_tokens, P)
        m.update(shared)
        in_maps.append(m)
    res = run_bass_kernel_spmd(nc, in_maps, core_ids=list(range(n_cores)))
    R = res.results
    TP = 16 + SEQ
    pair = cfg.get("pair", True)
    pcores = [2 * s for s in range(x_prompt.shape[0])] if pair else list(range(x_prompt.shape[0]))
    y_prompt = np.stack([R[c]["y"][16:TP] for c in pcores])
    y_sample = np.concatenate([R[c]["y"][TP:].reshape(NS, 8, D) for c in range(n_cores)])

    def pst(name, fn):
        return np.stack([fn(R[c][name]) for c in pcores], axis=1)

    def sst(name, fn):
        return np.concatenate([fn(R[c][name]) for c in range(n_cores)], axis=1)

    outs = [y_prompt, y_sample]
    outs.append(pst("o_pC", lambda a: a.reshape(DEPTH, 4, 128, 256)))
    outs.append(pst("o_pn", lambda a: a.reshape(DEPTH, 128, 4).transpose(0, 2, 1)))
    outs.append(pst("o_pm", lambda a: a.reshape(DEPTH, 4)))
    outs.append(pst("o_ph", lambda a: a.reshape(DEPTH, 128, 8).transpose(0, 2, 1).reshape(DEPTH, 1024)))
    outs.append(pst("o_prc", lambda a: a.reshape(DEPTH, 128, 8, 3).transpose(0, 3, 2, 1).reshape(DEPTH, 3, 1024)))
    outs.append(pst("o_pS", lambda a: a.reshape(DEPTH, 16, 128, 128)))
    outs.append(pst("o_pgc", lambda a: a.reshape(DEPTH, 128, 48, 3).transpose(0, 3, 2, 1).reshape(DEPTH, 3, 6144)))
    outs.append(sst("o_sC", lambda a: a.reshape(DEPTH, NS, 4, 128, 256)))
    outs.append(sst("o_sn", lambda a: a.reshape(DEPTH, NS, 128, 4).transpose(0, 1, 3, 2)))
    outs.append(sst("o_sm", lambda a: a.reshape(DEPTH, NS, 4)))
    outs.append(sst("o_sh", lambda a: a.reshape(DEPTH, 128, 8, NS).transpose(0, 3, 2, 1).reshape(DEPTH, NS, 1024)))
    outs.append(sst("o_src", lambda a: a.reshape(DEPTH, 128, 8, NS, 3).transpose(0, 3, 4, 2, 1).reshape(DEPTH, NS, 3, 1024)))
    outs.append(sst("o_sS", lambda a: a.reshape(DEPTH, NS, 16, 128, 128)))
    outs.append(sst("o_sgc", lambda a: a.reshape(DEPTH, 128, 48, NS, 3).transpose(0, 3, 4, 2, 1).reshape(DEPTH, NS, 3, 6144)))
    return tuple(np.ascontiguousarray(o, dtype=np.float32) for o in outs)


def kernel(x_prompt, x_sample, state_mlstm_C, state_mlstm_n, state_mlstm_m, state_rglru_h,
           state_rglru_conv, state_gdn_S, state_gdn_conv, meta_tokens, norm_mix, w_in,
           m_bias_i, m_bias_f, m_norm, r_conv_w, r_conv_b, r_gate_a_w, r_gate_a_b,
           r_gate_x_w, r_gate_x_b, r_lambda, g_conv_w, g_A_log, g_dt_bias, g_norm, w_out,
           norm_ffn, w_gate, w_up, w_down, norm_final):
    A = lambda a: np.asarray(a, np.float32)
    st = dict(C=A(state_mlstm_C), n=A(state_mlstm_n), m=A(state_mlstm_m), h=A(state_rglru_h),
              rc=A(state_rglru_conv), S=A(state_gdn_S), gc=A(state_gdn_conv))
    P = dict(norm_mix=A(norm_mix), w_in=w_in, m_bias_i=A(m_bias_i), m_bias_f=A(m_bias_f), m_norm=A(m_norm),
             r_conv_w=A(r_conv_w), r_conv_b=A(r_conv_b), r_gate_a_w=A(r_gate_a_w), r_gate_a_b=A(r_gate_a_b),
             r_gate_x_w=A(r_gate_x_w), r_gate_x_b=A(r_gate_x_b), r_lambda=A(r_lambda), g_conv_w=A(g_conv_w),
             g_A_log=A(g_A_log), g_dt_bias=A(g_dt_bias), g_norm=A(g_norm), w_out=w_out, norm_ffn=A(norm_ffn),
             w_gate=w_gate, w_up=w_up, w_down=w_down, norm_final=A(norm_final))
    return run(dict(FULL), 8, A(x_prompt), A(x_sample), st, A(meta_tokens), P)
```
